# Optimizing a Trainium2 kernel written in Bass

```python
import math
import jax, jax.numpy as jnp
from jax import lax
import numpy as np

D_MODEL = 1024
BATCH = 8
SEQ = 2048
DEPTH = 4

N_Q_HEADS = 8
N_KV_HEADS = 2
HEAD_DIM = 64
Q_GROUP = N_Q_HEADS // N_KV_HEADS
WINDOW = 128
BLOCK = 128
ROPE_THETA = 500000.0
ROT_DIM = HEAD_DIM // 4
ATTN_WIDTH = N_Q_HEADS * HEAD_DIM
KV_WIDTH = N_KV_HEADS * HEAD_DIM
NEG_INF = -1e30
CONV_WIDTH = D_MODEL // 2
CONV_K = 3
SSM_WIDTH = D_MODEL // 2
SSM_GROUP = 16
SSM_GROUPS = SSM_WIDTH // SSM_GROUP
SSM_STATE = 64
DT_MIN = 1e-3
DT_MAX = 1e-1
N_BRANCH = 3
GATE_WIDTH = N_BRANCH * D_MODEL
FFN_HIDDEN = -(-8 * D_MODEL // (3 * 256)) * 256
NORM_EPS = 1e-6

IN_SIZES = (ATTN_WIDTH, KV_WIDTH, KV_WIDTH, CONV_WIDTH, CONV_WIDTH, CONV_WIDTH, SSM_WIDTH, GATE_WIDTH)
IN_COLS = sum(IN_SIZES)
IN_SPLITS = tuple(int(v) for v in np.cumsum(IN_SIZES)[:-1])

kernel_name = "hybrid_gated_swa_conv_s5_block"


def rmsnorm(x, g):
    xf = x.astype(jnp.float32)
    y = xf * lax.rsqrt(jnp.mean(xf * xf, axis=-1, keepdims=True) + NORM_EPS)
    return (y * g.astype(jnp.float32)).astype(x.dtype)


def rope_tables(seq_len):
    pos = jnp.arange(seq_len, dtype=jnp.float32)
    inv_freq = ROPE_THETA ** (-jnp.arange(0, ROT_DIM, 2, dtype=jnp.float32) / ROT_DIM)
    ang = pos[:, None] * inv_freq[None, :]
    return jnp.cos(ang), jnp.sin(ang)


def partial_rope(t, cos, sin):
    half = ROT_DIM // 2
    tf = t.astype(jnp.float32)
    t1, t2, rest = tf[..., :half], tf[..., half:ROT_DIM], tf[..., ROT_DIM:]
    c = cos[None, :, None, :]
    s = sin[None, :, None, :]
    out = jnp.concatenate([t1 * c - t2 * s, t2 * c + t1 * s, rest], axis=-1)
    return out.astype(t.dtype)


def sliding_window_attention(q, k, v, sinks):
    b, l = q.shape[0], q.shape[1]
    nb = l // BLOCK
    qb = q.reshape(b, nb, BLOCK, N_KV_HEADS, Q_GROUP, HEAD_DIM).astype(jnp.float32)

    def band(t):
        tp = jnp.pad(t, ((0, 0), (BLOCK, 0), (0, 0), (0, 0)))
        tp = tp.reshape(b, nb + 1, BLOCK, N_KV_HEADS, HEAD_DIM)
        return jnp.concatenate([tp[:, :-1], tp[:, 1:]], axis=2).astype(jnp.float32)

    kb, vb = band(k), band(v)
    s = jnp.einsum("bnqkgd,bnskd->bnkgqs", qb, kb) * (HEAD_DIM ** -0.5)
    qi = jnp.arange(BLOCK)[:, None]
    kj = jnp.arange(2 * BLOCK)[None, :]
    delta = qi + BLOCK - kj
    band_ok = (delta >= 0) & (delta < WINDOW)
    kpos = jnp.arange(nb)[:, None] * BLOCK - BLOCK + kj
    mask = band_ok[None, :, :] & (kpos >= 0)[:, None, :]
    s = jnp.where(mask[None, :, None, None, :, :], s, NEG_INF)
    sink = sinks.astype(jnp.float32).reshape(N_KV_HEADS, Q_GROUP)[None, None, :, :, None, None]
    m = jnp.maximum(jnp.max(s, axis=-1, keepdims=True), sink)
    p = jnp.exp(s - m)
    denom = jnp.sum(p, axis=-1, keepdims=True) + jnp.exp(sink - m)
    o = jnp.einsum("bnkgqs,bnskd->bnqkgd", p / denom, vb)
    return o.reshape(b, l, ATTN_WIDTH).astype(q.dtype)


def short_conv(z, w):
    l = z.shape[1]
    zp = jnp.pad(z, ((0, 0), (CONV_K - 1, 0), (0, 0)))
    y = w[0] * zp[:, 0:l]
    for j in range(1, CONV_K):
        y = y + w[j] * zp[:, j:j + l]
    return y


def s5_ssm(u, a_re, a_im, b_re, b_im, c_re, c_im, d, log_dt):
    bsz, l = u.shape[0], u.shape[1]
    uf = u.astype(jnp.float32).reshape(bsz, l, SSM_GROUPS, SSM_GROUP)
    lam = lax.complex(a_re.astype(jnp.float32), a_im.astype(jnp.float32))
    dt = jnp.exp(log_dt.astype(jnp.float32))[:, None]
    lam_bar = jnp.exp(lam * dt)
    b_c = lax.complex(b_re.astype(jnp.float32), b_im.astype(jnp.float32))
    b_bar = ((lam_bar - 1.0) / lam)[..., None] * b_c
    bu = jnp.einsum("blgh,gph->blgp", uf.astype(jnp.complex64), b_bar)
    a_elems = jnp.broadcast_to(lam_bar, bu.shape)

    def combine(e1, e2):
        a1, x1 = e1
        a2, x2 = e2
        return a1 * a2, a2 * x1 + x2

    _, states = lax.associative_scan(combine, (a_elems, bu), axis=1)
    c_c = lax.complex(c_re.astype(jnp.float32), c_im.astype(jnp.float32))
    y = jnp.einsum("blgp,ghp->blgh", states, c_c).real
    y = y + d.astype(jnp.float32).reshape(SSM_GROUPS, SSM_GROUP) * uf
    return y.reshape(bsz, l, SSM_WIDTH).astype(u.dtype)


def setup_inputs(seed: int = 0) -> dict:
    key = jax.random.key(seed)
    ks = jax.random.split(key, 24)
    L = DEPTH

    def nrm(k, shape, fan_in):
        return jax.random.normal(k, shape, jnp.float32) * (fan_in ** -0.5)

    x = jax.random.normal(ks[0], (BATCH, SEQ, D_MODEL), jnp.float32)
    norm_mix = 1.0 + 0.02 * jax.random.normal(ks[1], (L, D_MODEL), jnp.float32)
    w_in = nrm(ks[2], (L, D_MODEL, IN_COLS), D_MODEL)
    b_gate = 0.02 * jax.random.normal(ks[3], (L, GATE_WIDTH), jnp.float32)
    attn_sinks = 0.5 * jax.random.normal(ks[4], (L, N_Q_HEADS), jnp.float32)
    w_attn_o = nrm(ks[5], (L, ATTN_WIDTH, D_MODEL), ATTN_WIDTH)
    conv_w = nrm(ks[6], (L, CONV_K, CONV_WIDTH), CONV_K)
    w_conv_o = nrm(ks[7], (L, CONV_WIDTH, D_MODEL), CONV_WIDTH)
    ssm_a_re = -0.5 + 0.01 * jax.random.normal(ks[8], (L, SSM_GROUPS, SSM_STATE), jnp.float32)
    ssm_a_im = (math.pi * jnp.arange(SSM_STATE, dtype=jnp.float32))[None, None, :] \
        + 0.01 * jax.random.normal(ks[9], (L, SSM_GROUPS, SSM_STATE), jnp.float32)
    ssm_b_re = nrm(ks[10], (L, SSM_GROUPS, SSM_STATE, SSM_GROUP), 2 * SSM_GROUP)
    ssm_b_im = nrm(ks[11], (L, SSM_GROUPS, SSM_STATE, SSM_GROUP), 2 * SSM_GROUP)
    ssm_c_re = nrm(ks[12], (L, SSM_GROUPS, SSM_GROUP, SSM_STATE), 2 * SSM_STATE)
    ssm_c_im = nrm(ks[13], (L, SSM_GROUPS, SSM_GROUP, SSM_STATE), 2 * SSM_STATE)
    ssm_d = jax.random.normal(ks[14], (L, SSM_WIDTH), jnp.float32)
    ssm_log_dt = jax.random.uniform(ks[15], (L, SSM_GROUPS), jnp.float32,
                                    minval=math.log(DT_MIN), maxval=math.log(DT_MAX))
    w_ssm_glu = nrm(ks[16], (L, SSM_WIDTH, SSM_WIDTH), SSM_WIDTH)
    w_ssm_o = nrm(ks[17], (L, SSM_WIDTH, D_MODEL), SSM_WIDTH)
    w_mix_o = nrm(ks[18], (L, D_MODEL, D_MODEL), D_MODEL)
    norm_ffn = 1.0 + 0.02 * jax.random.normal(ks[19], (L, D_MODEL), jnp.float32)
    w_ffn_in = nrm(ks[20], (L, D_MODEL, 2 * FFN_HIDDEN), D_MODEL)
    w_ffn_out = nrm(ks[21], (L, FFN_HIDDEN, D_MODEL), FFN_HIDDEN)
    norm_final = 1.0 + 0.02 * jax.random.normal(ks[22], (D_MODEL,), jnp.float32)
    return {"x": x, "norm_mix": norm_mix, "w_in": w_in, "b_gate": b_gate,
            "attn_sinks": attn_sinks, "w_attn_o": w_attn_o, "conv_w": conv_w, "w_conv_o": w_conv_o,
            "ssm_a_re": ssm_a_re, "ssm_a_im": ssm_a_im, "ssm_b_re": ssm_b_re, "ssm_b_im": ssm_b_im,
            "ssm_c_re": ssm_c_re, "ssm_c_im": ssm_c_im, "ssm_d": ssm_d, "ssm_log_dt": ssm_log_dt,
            "w_ssm_glu": w_ssm_glu, "w_ssm_o": w_ssm_o, "w_mix_o": w_mix_o, "norm_ffn": norm_ffn,
            "w_ffn_in": w_ffn_in, "w_ffn_out": w_ffn_out, "norm_final": norm_final}


def reference(x, norm_mix, w_in, b_gate, attn_sinks, w_attn_o, conv_w, w_conv_o,
              ssm_a_re, ssm_a_im, ssm_b_re, ssm_b_im, ssm_c_re, ssm_c_im, ssm_d, ssm_log_dt,
              w_ssm_glu, w_ssm_o, w_mix_o, norm_ffn, w_ffn_in, w_ffn_out, norm_final):
    b, l = x.shape[0], x.shape[1]
    cos, sin = rope_tables(l)
    for i in range(DEPTH):
        h = rmsnorm(x, norm_mix[i])
        proj = h @ w_in[i]
        q, k, v, cb, cc, cx, u, g = jnp.split(proj, IN_SPLITS, axis=-1)
        q = partial_rope(q.reshape(b, l, N_Q_HEADS, HEAD_DIM), cos, sin)
        k = partial_rope(k.reshape(b, l, N_KV_HEADS, HEAD_DIM), cos, sin)
        v = v.reshape(b, l, N_KV_HEADS, HEAD_DIM)
        y_attn = sliding_window_attention(q, k, v, attn_sinks[i]) @ w_attn_o[i]
        y_conv = (cb * short_conv(cc * cx, conv_w[i])) @ w_conv_o[i]
        ys = jax.nn.gelu(s5_ssm(u, ssm_a_re[i], ssm_a_im[i], ssm_b_re[i], ssm_b_im[i],
                                ssm_c_re[i], ssm_c_im[i], ssm_d[i], ssm_log_dt[i]))
        y_ssm = (ys * jax.nn.sigmoid(ys @ w_ssm_glu[i])) @ w_ssm_o[i]
        gates = jax.nn.sigmoid(g + b_gate[i]).reshape(b, l, N_BRANCH, D_MODEL)
        merged = gates[:, :, 0] * y_attn + gates[:, :, 1] * y_conv + gates[:, :, 2] * y_ssm
        x = x + merged @ w_mix_o[i]
        h = rmsnorm(x, norm_ffn[i])
        gt, up = jnp.split(h @ w_ffn_in[i], 2, axis=-1)
        x = x + (jax.nn.silu(gt) * up) @ w_ffn_out[i]
    return rmsnorm(x, norm_final)
```

```python
import contextlib
import math
import numpy as np
import concourse.bass as bass
import concourse.mybir as mybir
from concourse.bass_utils import run_bass_kernel_spmd

F32 = mybir.dt.float32
BF16 = mybir.dt.bfloat16
AF = mybir.ActivationFunctionType
ALU = mybir.AluOpType

D = 1024
SEQ = 2048
NLAYER = 4
NKT = 8
NTB = 4
TBW = 512
FFH = 2816
INC = 5888
EPS = 1e-6
SCH = 64
NCH = SEQ // SCH


class Op:
    __slots__ = ("eng", "fn", "deps", "dma", "chan", "ticket", "need_inc", "idx", "ndma")

    def __init__(self, eng, fn, dma, chan, ndma):
        self.eng = eng
        self.fn = fn
        self.dma = dma
        self.chan = chan
        self.ndma = ndma
        self.deps = set()
        self.ticket = None
        self.need_inc = False


class Sched:
    ENGS = ("pe", "act", "dve", "pool", "sp")

    def __init__(self):
        self.ops = []
        self.last_w = {}
        self.readers = {}
        self.last_by_eng = {}
        self.dmas_since = []

    enabled = True

    def op(self, eng, fn, reads=(), writes=(), dma=False, chan=None, ndma=1, extra=()):
        if not self.enabled:
            return None
        o = Op(eng, fn, dma, chan, ndma)
        o.idx = len(self.ops)
        deps = set(extra)
        for k in reads:
            w = self.last_w.get(k)
            if w is not None:
                deps.add(w)
        for k in writes:
            w = self.last_w.get(k)
            if w is not None:
                deps.add(w)
            deps.update(self.readers.get(k, ()))
        for k in reads:
            self.readers.setdefault(k, []).append(o)
        for k in writes:
            self.last_w[k] = o
            self.readers[k] = []
        deps.discard(o)
        o.deps = deps
        self.ops.append(o)
        if dma:
            self.dmas_since.append(o)
        else:
            self.last_by_eng[eng] = o
        return o

    def barrier(self):
        if not self.enabled:
            return
        allops = [o for o in self.last_by_eng.values()] + list(self.dmas_since)
        self.last_w = {}
        self.readers = {}
        self.dmas_since = []
        for e in self.ENGS:
            self.op(e, lambda eng: eng.nop(), extra=[o for o in allops])

    def emit(self, nc):
        for o in self.ops:
            for d in o.deps:
                if d.dma:
                    continue
                if d.eng != o.eng or d.eng != "pe":
                    d.need_inc = True
        counts = {e: 0 for e in self.ENGS}
        chan_counts = {}
        for o in self.ops:
            if o.dma:
                c = chan_counts.get(o.chan, 0) + o.ndma
                chan_counts[o.chan] = c
                o.ticket = ("c:" + o.chan, 16 * c)
            elif o.need_inc:
                counts[o.eng] += 1
                o.ticket = ("e:" + o.eng, counts[o.eng])
        sem_names = ["e:" + e for e in self.ENGS] + ["c:" + c for c in chan_counts] + ["k:" + e for e in self.ENGS]
        with contextlib.ExitStack() as st:
            sems = {}
            for n in sem_names:
                sems[n] = st.enter_context(nc.semaphore(n.replace(":", "_")))
            block = st.enter_context(nc.Block())
            per_eng = {e: [o for o in self.ops if o.eng == e] for e in self.ENGS}

            def run(engname, eng):
                waited = {}
                CH.sem = sems["k:" + engname]
                CH.cnt = 0
                for o in per_eng[engname]:
                    need = {}
                    for d in o.deps:
                        if (not d.dma) and d.eng == engname and engname == "pe":
                            continue
                        s, v = d.ticket
                        if waited.get(s, 0) >= v:
                            continue
                        if need.get(s, 0) < v:
                            need[s] = v
                    for s, v in need.items():
                        eng.wait_ge(sems[s], v)
                        waited[s] = v
                    if engname in ("act", "dve", "pool") and not o.dma:
                        ins = o.fn(EngProxy(eng))
                    else:
                        ins = o.fn(eng)
                    if o.dma:
                        if not isinstance(ins, (list, tuple)):
                            ins = [ins]
                        assert len(ins) == o.ndma
                        for i_ in ins:
                            i_.then_inc(sems[o.ticket[0]], 16)
                    elif o.need_inc:
                        ins.then_inc(sems[o.ticket[0]], 1)

            @block.tensor
            def _(e):
                run("pe", e)

            @block.scalar
            def _(e):
                run("act", e)

            @block.vector
            def _(e):
                run("dve", e)

            @block.gpsimd
            def _(e):
                run("pool", e)

            @block.sync
            def _(e):
                run("sp", e)


class _Chain:
    sem = None
    cnt = 0


CH = _Chain()


def C(e, ins):
    CH.cnt += 1
    ins.then_inc(CH.sem, 1)
    e.wait_ge(CH.sem, CH.cnt)
    return ins


class EngProxy:
    def __init__(self, e):
        self._e = e
        self._last = None

    def __getattr__(self, name):
        real = getattr(self._e, name)

        def w(*a, **k):
            if self._last is not None:
                C(self._e, self._last)
            ins = real(*a, **k)
            self._last = ins
            return ins

        return w


def red_angle(e, x, tmpf, tmpi):
    PI = math.pi
    e.tensor_scalar(out=tmpf, in0=x, scalar1=1.0 / (2 * PI), scalar2=0.5, op0=ALU.mult, op1=ALU.add)
    e.tensor_copy(out=tmpi, in_=tmpf)
    e.tensor_copy(out=tmpf, in_=tmpi)
    e.scalar_tensor_tensor(out=x, in0=tmpf, scalar=-2 * PI, in1=x, op0=ALU.mult, op1=ALU.add)
    e.tensor_scalar(out=tmpf, in0=x, scalar1=-PI, scalar2=2 * PI, op0=ALU.is_lt, op1=ALU.mult)
    e.tensor_tensor(out=x, in0=x, in1=tmpf, op=ALU.add)
    e.tensor_scalar(out=tmpf, in0=x, scalar1=PI, scalar2=-2 * PI, op0=ALU.is_gt, op1=ALU.mult)
    return e.tensor_tensor(out=x, in0=x, in1=tmpf, op=ALU.add)


class Arena:
    def __init__(self, nc, lo=16512, hi=225792):
        self.nc = nc
        self.lo = lo
        self.hi = hi
        self.top = lo
        self.n = 0
        self.stack = []
        self.offs = {}
        self.peak = lo

    def alloc(self, name, shape, dtype):
        nbytes = int(np.prod(shape[1:])) * mybir.dt.size(dtype)
        off = (self.top + 63) // 64 * 64
        assert off + nbytes <= self.hi, (name, off, nbytes, self.hi)
        t = self.nc.alloc_sbuf_tensor_at(f"{name}_{self.n}", list(shape), dtype, offset=off)
        self.offs[name] = off
        self.top = off + nbytes
        self.peak = max(self.peak, self.top)
        self.n += 1
        return t

    def push(self):
        self.stack.append(self.top)

    def pop(self):
        self.top = self.stack.pop()


P_GMIX = 0
P_GFFN = 8
P_BG = 16
P_CONVW = 40
P_SSMD = 52
P_SINK = 56
P_AA = 60
P_AI = 76
P_LDT = 92
P_GFIN = 108
NPRM = 116

W_SLOTS = 3
CSTW = 2 * 512 + 128 + 2 * SEQ + SCH + 128


def build_program(NL=NLAYER, DBG=99):
    nc = bass.Bass("TRN2", target_bir_lowering=False)
    dt_in = lambda name, shape: nc.dram_tensor(name, list(shape), F32, kind="ExternalInput").ap()
    xT_d = dt_in("xT", [D, SEQ])
    w_in_d = dt_in("w_in", [NL, D, INC])
    w_ao_d = dt_in("w_attn_o", [NL, 512, D])
    w_co_d = dt_in("w_conv_o", [NL, 512, D])
    w_glu_d = dt_in("w_ssm_glu", [NL, 512, 512])
    w_so_d = dt_in("w_ssm_o", [NL, 512, D])
    w_mix_d = dt_in("w_mix_o", [NL, D, D])
    w_fi_d = dt_in("w_ffn_in", [NL, D, 2 * FFH])
    w_fo_d = dt_in("w_ffn_out", [NL, FFH, D])
    prm_d = dt_in("prm", [NL, 128, NPRM])
    ssmB_d = dt_in("ssmB", [NL, 128, 2, 2048])
    ssmC_d = dt_in("ssmC", [NL, 128, 2, 1024])
    cst_d = dt_in("cst", [128, CSTW])
    outT_d = nc.dram_tensor("outT", [D, SEQ], F32, kind="ExternalOutput").ap()

    S = Sched()
    A = Arena(nc)
    PS = nc.alloc_psum_tensor("ps", [128, 8, 512], F32)

    xT = A.alloc("xT", [128, NKT, SEQ], F32)
    hT = A.alloc("hT", [128, NKT, SEQ], BF16)
    WS = [A.alloc(f"ws{i}", [128, 4096], BF16) for i in range(W_SLOTS)]
    mrg_off = (A.top + 63) // 64 * 64
    MRG = A.alloc("mrg", [128, NKT, SEQ], BF16)
    BR = A.alloc("br", [128, 4, SEQ], BF16)
    FFA = nc.alloc_sbuf_tensor_at("ffa", [128, 11, SEQ], BF16, offset=mrg_off)
    PRM = A.alloc("prm", [128, NL, NPRM], F32)
    ONES = A.alloc("ones", [128, 128], BF16)
    PERM = A.alloc("perm", [128, 128], BF16)
    ESK = A.alloc("esk", [128, 4], F32)
    JI = A.alloc("ji", [128, SCH], F32)
    IDN = A.alloc("idn", [128, 128], F32)
    ONESF = A.alloc("onesf", [128, 128], F32)

    psc = [0]

    def nb(n=1):
        i = psc[0] % 8
        psc[0] += 1
        return i

    wsc = [0]

    def wload(views, rshape):
        s = wsc[0] % W_SLOTS
        wsc[0] += 1

        def fn(e, s=s, views=views):
            out = []
            for (c0, a, b, src) in views:
                dst = WS[s][:, c0:c0 + a * b].rearrange("p (a b) -> p a b", a=a)
                for ai in range(a):
                    out.append(e.dma_start(out=dst[:, ai, :], in_=src[:, ai, :]))
            return out

        S.op("pool", fn, writes=[("w", s)], dma=True, chan=f"w{s}", ndma=sum(v[1] for v in views))
        return s

    def wview(s, a, b, c0=0):
        return WS[s][:, c0:c0 + a * b].rearrange("p (a b) -> p a b", a=a)

    def tbs(tb):
        return slice(tb * TBW, (tb + 1) * TBW)

    for kt in range(NKT):
        S.op("sp", lambda e, kt=kt: e.dma_start(out=xT[:, kt, :], in_=xT_d[kt * 128:(kt + 1) * 128, :]),
             writes=[("x", kt, tb) for tb in range(NTB)], dma=True, chan=f"x{kt}")
    S.op("sp", lambda e: e.dma_start(out=PRM[:], in_=prm_d.rearrange("l p c -> p l c")), writes=["prm"], dma=True,
         chan="prm")
    S.op("sp", lambda e: e.dma_start(out=JI[:], in_=cst_d[:, 1152 + 2 * SEQ:1152 + 2 * SEQ + SCH]), writes=["ji"], dma=True,
         chan="ji")
    S.op("sp", lambda e: e.dma_start(out=IDN[:], in_=cst_d[:, CSTW - 128:CSTW]), writes=["idn"], dma=True, chan="idn")
    S.op("dve", lambda e: e.memset(ONESF[:], 1.0), writes=["onesf"])
    S.op("dve", lambda e: e.memset(ONES[:], 1.0), writes=["ones"])
    S.op("pool", lambda e: e.dma_start(out=PERM[:], in_=cst_d[:, 1024:1152]), writes=["perm"], dma=True, chan="perm")

    def rmsnorm_to_h(l, gcol, name):
        A.push()
        SQ = [A.alloc("sq", [128, TBW], BF16) for _ in range(3)]
        MS = [A.alloc("ms", [128, TBW], F32) for _ in range(2)]
        for tb in range(NTB):
            pi = nb()
            for kt in range(NKT):
                q = (tb * NKT + kt) % 3
                S.op("act", lambda e, q=q, kt=kt, tb=tb: e.activation(out=SQ[q][:], in_=xT[:, kt, tbs(tb)], func=AF.Square),
                     reads=[("x", kt, tb)], writes=[("sq", q)])
                S.op("pe", lambda e, q=q, kt=kt, pi=pi: e.matmul(PS[:, pi, :], lhsT=ONES[:], rhs=SQ[q][:], start=(kt == 0),
                                                                stop=(kt == NKT - 1)),
                     reads=[("sq", q), "ones"], writes=[("ps", pi)])
            m = tb % 2
            S.op("dve", lambda e, m=m, pi=pi: e.tensor_scalar(out=MS[m][:], in0=PS[:, pi, :], scalar1=1.0 / D, scalar2=EPS,
                                                             op0=ALU.mult, op1=ALU.add),
                 reads=[("ps", pi)], writes=[("ms", m)])
            S.op("act", lambda e, m=m: e.activation(out=MS[m][:], in_=MS[m][:], func=AF.Sqrt), reads=[("ms", m)],
                 writes=[("ms", m)])
            S.op("dve", lambda e, m=m: e.reciprocal(out=MS[m][:], in_=MS[m][:]), reads=[("ms", m)], writes=[("ms", m)])
            for kt in range(NKT):
                S.op("dve", lambda e, m=m, kt=kt, tb=tb: e.scalar_tensor_tensor(
                    out=hT[:, kt, tbs(tb)], in0=xT[:, kt, tbs(tb)], scalar=PRM[:, l, gcol + kt:gcol + kt + 1], in1=MS[m][:],
                    op0=ALU.mult, op1=ALU.mult),
                     reads=[("x", kt, tb), ("ms", m), "prm"], writes=[("h", kt, tb)])
        S.barrier()
        A.pop()

    def proj_group(pi, s, c0, tb, ncols_slot=512):
        wv = wview(s, NKT, ncols_slot)

        def fn(e):
            ins = None
            for kt in range(NKT):
                ins = e.matmul(PS[:, pi, :], lhsT=wv[:, kt, c0:c0 + 128], rhs=hT[:, kt, tbs(tb)], start=(kt == 0),
                               stop=(kt == NKT - 1))
            return ins

        S.op("pe", fn, reads=[("w", s)] + [("h", kt, tb) for kt in range(NKT)], writes=[("ps", pi)])

    def win_view(l, c0, ncols):
        return w_in_d[l].rearrange("(kt p) n -> p kt n", p=128)[:, :, c0:c0 + ncols]

    def merge_branch(l, b, wo_d):
        A.push()
        SG = [A.alloc("sg", [128, TBW], F32) for _ in range(2)]
        TMP = [A.alloc("tmp", [128, TBW], F32) for _ in range(2)]
        so = wload([(0, 4, 1024, wo_d[l].rearrange("(kt p) n -> p kt n", p=128))], None)
        wo = wview(so, 4, 1024)
        for half in range(2):
            sg_ = wload([(0, NKT, 512, win_view(l, 2816 + b * 1024 + half * 512, 512))], None)
            for fl in range(4):
                f = half * 4 + fl
                for tb in range(NTB):
                    py = nb()

                    def fy(e, py=py, f=f, tb=tb):
                        ins = None
                        for kt in range(4):
                            ins = e.matmul(PS[:, py, :], lhsT=wo[:, kt, f * 128:(f + 1) * 128], rhs=BR[:, kt, tbs(tb)],
                                           start=(kt == 0), stop=(kt == 3))
                        return ins

                    S.op("pe", fy, reads=[("w", so)] + [("br", kt, tb) for kt in range(4)], writes=[("ps", py)])
                    pg = nb()
                    proj_group(pg, sg_, fl * 128, tb)
                    q = (f * NTB + tb) % 2
                    S.op("act", lambda e, q=q, pg=pg, f=f: e.activation(out=SG[q][:], in_=PS[:, pg, :], func=AF.Sigmoid,
                                                                         bias=PRM[:, l, P_BG + b * 8 + f:P_BG + b * 8 + f + 1]),
                         reads=[("ps", pg), "prm"], writes=[("sg", q)])
                    if b == 0:
                        S.op("dve", lambda e, q=q, py=py, f=f, tb=tb: e.tensor_tensor(out=MRG[:, f, tbs(tb)], in0=PS[:, py, :],
                                                                                      in1=SG[q][:], op=ALU.mult),
                             reads=[("ps", py), ("sg", q)], writes=[("mrg", f, tb)])
                    else:
                        S.op("dve", lambda e, q=q, py=py: e.tensor_tensor(out=TMP[q][:], in0=PS[:, py, :], in1=SG[q][:],
                                                                          op=ALU.mult),
                             reads=[("ps", py), ("sg", q)], writes=[("tmp", q)])
                        S.op("dve", lambda e, q=q, f=f, tb=tb: e.tensor_tensor(out=MRG[:, f, tbs(tb)], in0=MRG[:, f, tbs(tb)],
                                                                               in1=TMP[q][:], op=ALU.add),
                             reads=[("tmp", q), ("mrg", f, tb)], writes=[("mrg", f, tb)])
        S.barrier()
        A.pop()

    def layer(l):
        S.enabled = DBG >= 1
        rmsnorm_to_h(l, P_GMIX, "n1")

        A.push()

        S.enabled = DBG >= 2
        A.push()
        Q = nc.alloc_sbuf_tensor_at(f"q_l{l}", [128, 4, SEQ], BF16, offset=mrg_off)
        Kt = A.alloc("k", [128, SEQ], BF16)
        V = A.alloc("v", [128, 16, 128], BF16)
        A.push()
        COS = [A.alloc("cos", [128, TBW], BF16) for _ in range(2)]
        SIN = [A.alloc("sin", [128, TBW], BF16) for _ in range(2)]
        QR = [A.alloc("qr", [128, TBW], BF16) for _ in range(2)]
        T1 = A.alloc("t1", [128, TBW], F32)
        T2 = A.alloc("t2", [128, TBW], F32)
        sq_ = wload([(0, NKT, 512, win_view(l, 0, 512))], None)
        skv = wload([(0, NKT, 256, win_view(l, 512, 256))], None)

        def rope_block(pi, tb, dst_ap, dst_key, cnt):
            q = cnt % 2
            cq = tb % 2
            if DBG < 2.1:
                return
            S.op("dve", lambda e: e.tensor_copy(out=QR[q][:], in_=PS[:, pi, :]), reads=[("ps", pi)],
                 writes=[("qr", q)])
            p2 = nb()
            S.op("pe", lambda e: e.matmul(PS[:, p2, :], lhsT=PERM[:], rhs=QR[q][:], start=True, stop=True),
                 reads=["perm", ("qr", q)], writes=[("ps", p2)])
            S.op("dve", lambda e: e.tensor_tensor(out=T1[:], in0=PS[:, pi, :], in1=COS[cq][:], op=ALU.mult),
                 reads=[("ps", pi), ("cos", cq)], writes=["t1"])
            S.op("dve", lambda e: e.tensor_tensor(out=T2[:], in0=PS[:, p2, :], in1=SIN[cq][:], op=ALU.mult),
                 reads=[("ps", p2), ("sin", cq)], writes=["t2"])
            S.op("dve", lambda e: e.tensor_tensor(out=dst_ap, in0=T1[:], in1=T2[:], op=ALU.add), reads=["t1", "t2"],
                 writes=[dst_key])

        cnt = 0
        for tb in range(NTB):
            cq = tb % 2
            S.op("pool", lambda e, tb=tb, cq=cq: e.dma_start(out=COS[cq][:], in_=cst_d[:, 1152 + tb * TBW:1152 + (tb + 1) * TBW]),
                 writes=[("cos", cq)], dma=True, chan=f"cos{cq}")
            S.op("pool", lambda e, tb=tb, cq=cq: e.dma_start(out=SIN[cq][:], in_=cst_d[:, 1152 + SEQ + tb * TBW:1152 + SEQ + (tb + 1) * TBW]),
                 writes=[("sin", cq)], dma=True, chan=f"sin{cq}")
            for g in range(4):
                pi = nb()
                proj_group(pi, sq_, g * 128, tb)
                rope_block(pi, tb, Q[:, g, tbs(tb)], ("q", g, tb), cnt)
                cnt += 1
            pi = nb()
            proj_group(pi, skv, 0, tb, ncols_slot=256)
            rope_block(pi, tb, Kt[:, tbs(tb)], ("k", tb), cnt)
            cnt += 1
        S.enabled = DBG >= 2.2
        kvv = wview(skv, NKT, 256)
        for t4 in range(4):
            pi = nb()

            def fv(e, pi=pi, t4=t4):
                ins = None
                for j in range(4):
                    tt = t4 * 4 + j
                    for kt in range(NKT):
                        ins = e.matmul(PS[:, pi, j * 128:(j + 1) * 128], lhsT=hT[:, kt, tt * 128:(tt + 1) * 128],
                                       rhs=kvv[:, kt, 128:256], start=(kt == 0), stop=(kt == NKT - 1))
                return ins

            S.op("pe", fv, reads=[("w", skv)] + [("h", kt, t4) for kt in range(NKT)], writes=[("ps", pi)])
            S.op("dve", lambda e, pi=pi, t4=t4: e.tensor_copy(out=V[:, t4 * 4:(t4 + 1) * 4, :],
                                                               in_=PS[:, pi, :].rearrange("p (a b) -> p a b", a=4)),
                 reads=[("ps", pi)], writes=[("v", t4)])
        S.barrier()
        A.pop()
        S.enabled = DBG >= 2.5
        MK = A.alloc("mk", [128, 2, 512], BF16)
        IDB = A.alloc("idb", [128, 128], BF16)
        PB = [A.alloc("pb", [128, TBW], BF16) for _ in range(8)]
        DEN = [A.alloc("den", [128, TBW], F32) for _ in range(1)]
        S.op("pool", lambda e: e.dma_start(out=MK[:], in_=cst_d[:, 0:1024].rearrange("p (a b) -> p a b", a=2)),
             writes=["mk"], dma=True, chan="mk")
        S.op("dve", lambda e: e.tensor_copy(out=IDB[:], in_=IDN[:]), reads=["idn"], writes=["idb"])
        S.op("act", lambda e: e.activation(out=ESK[:], in_=PRM[:, l, P_SINK:P_SINK + 4], func=AF.Exp), reads=["prm"],
             writes=["esk"])
        pcnt = [0]

        def att_s(qb):
            qs = slice(qb * 128, (qb + 1) * 128)
            pbs = {}
            kts = [qb] if qb == 0 else [qb - 1, qb]
            k_ = 0
            for kvh in range(2):
                hs = slice(kvh * 64, (kvh + 1) * 64)
                for kt_ in kts:
                    pa = (k_ if qb else 2 * k_) % 4
                    k_ += 1
                    ks = slice(kt_ * 128, (kt_ + 1) * 128)
                    mi = 0 if kt_ == qb else 1

                    def fs(e, pa=pa, hs=hs, ks=ks, qs=qs, kvh=kvh, mi=mi):
                        e.matmul(PS[:, pa, :], lhsT=IDB[:], rhs=MK[:, mi, :], start=True, stop=False)
                        return e.matmul(PS[:, pa, :].rearrange("p (a b) -> p a b", a=4), lhsT=Kt[hs, ks], rhs=Q[hs, :, qs],
                                        start=False, stop=True, tile_position=(kvh * 64, 0))

                    S.op("pe", fs, reads=[("k", kt_ // 4), "idb", "mk"] + [("q", g, qb // 4) for g in range(4)],
                         writes=[("ps", pa)])
                    pq = pcnt[0] % 8
                    pcnt[0] += 1
                    S.op("act", lambda e, pa=pa, pq=pq: e.activation(out=PB[pq][:], in_=PS[:, pa, :], func=AF.Exp, scale=0.125),
                         reads=[("ps", pa)], writes=[("pb", pq)])
                    pbs[(kvh, kt_)] = pq
            return pbs

        def att_pv(qb, pbs):
            qs = slice(qb * 128, (qb + 1) * 128)
            po = 4 + 2 * (qb % 2)
            pd = po + 1
            kts = [qb] if qb == 0 else [qb - 1, qb]
            n_ = len(kts)
            for kvh in range(2):
                hs = slice(kvh * 64, (kvh + 1) * 64)
                for i_, kt_ in enumerate(kts):
                    pq = pbs[(kvh, kt_)]
                    S.op("pe", lambda e, pq=pq, hs=hs, kt_=kt_, i_=i_, n_=n_, kvh=kvh, po=po: e.matmul(
                        PS[hs, po, :], lhsT=V[:, kt_, hs], rhs=PB[pq][:], start=(i_ == 0), stop=(i_ == n_ - 1),
                        tile_position=(0, kvh * 64)),
                         reads=[("pb", pq), ("v", kt_ // 4)], writes=[("ps", po)])
                    S.op("pe", lambda e, pq=pq, hs=hs, i_=i_, n_=n_, kvh=kvh, pd=pd: e.matmul(
                        PS[hs, pd, :], lhsT=ONES[:, 0:64], rhs=PB[pq][:], start=(i_ == 0), stop=(i_ == n_ - 1),
                        tile_position=(0, kvh * 64)),
                         reads=[("pb", pq), "ones"], writes=[("ps", pd)])
            dq = 0
            S.op("dve", lambda e, dq=dq, pd=pd: e.tensor_tensor(
                out=DEN[dq][:].rearrange("p (a b) -> p a b", a=4), in0=PS[:, pd, :].rearrange("p (a b) -> p a b", a=4),
                in1=ESK[:].unsqueeze(2).to_broadcast([128, 4, 128]), op=ALU.add),
                 reads=[("ps", pd), "esk"], writes=[("den", dq)])
            S.op("dve", lambda e, dq=dq: e.reciprocal(out=DEN[dq][:], in_=DEN[dq][:]), reads=[("den", dq)],
                 writes=[("den", dq)])
            S.op("dve", lambda e, dq=dq, po=po, qs=qs: e.tensor_tensor(
                out=BR[:, :, qs], in0=PS[:, po, :].rearrange("p (a b) -> p a b", a=4),
                in1=DEN[dq][:].rearrange("p (a b) -> p a b", a=4), op=ALU.mult),
                 reads=[("ps", po), ("den", dq)], writes=[("br", g, qb // 4) for g in range(4)])

        prev_pbs = att_s(0)
        for qb in range(16):
            nxt = att_s(qb + 1) if qb + 1 < 16 else None
            att_pv(qb, prev_pbs)
            prev_pbs = nxt
        S.barrier()
        A.pop()
        S.enabled = DBG >= 2.8
        merge_branch(l, 0, w_ao_d)

        S.enabled = DBG >= 3
        A.push()
        Z = A.alloc("z", [128, SEQ + 2], F32)
        Y1 = [A.alloc("y1", [128, TBW], F32) for _ in range(2)]
        S.op("dve", lambda e: e.memset(Z[:, 0:2], 0.0), writes=["z0"])
        scc = wload([(0, NKT, 512, win_view(l, 1280, 512))], None)
        for f in range(4):
            for tb in range(NTB):
                pi = nb()
                proj_group(pi, scc, f * 128, tb)
                S.op("dve", lambda e, pi=pi, f=f, tb=tb: e.tensor_copy(out=BR[:, f, tbs(tb)], in_=PS[:, pi, :]),
                     reads=[("ps", pi)], writes=[("br", f, tb)])
        scx = wload([(0, NKT, 512, win_view(l, 1792, 512))], None)
        for f in range(4):
            for tb in range(NTB):
                pi = nb()
                proj_group(pi, scx, f * 128, tb)
                a0 = tb * TBW
                S.op("dve", lambda e, pi=pi, f=f, tb=tb, a0=a0: e.tensor_tensor(out=Z[:, 2 + a0:2 + a0 + TBW], in0=PS[:, pi, :],
                                                                               in1=BR[:, f, tbs(tb)], op=ALU.mult),
                     reads=[("ps", pi), ("br", f, tb)], writes=[("z", tb)])
                yq = tb % 2
                cw = lambda j, f=f: PRM[:, l, P_CONVW + j * 4 + f:P_CONVW + j * 4 + f + 1]

                def fconv(e, a0=a0, yq=yq, f=f, tb=tb, cw=cw):
                    e.tensor_scalar(out=Y1[yq][:], in0=Z[:, 2 + a0:2 + a0 + TBW], scalar1=cw(2), scalar2=None, op0=ALU.mult)
                    e.scalar_tensor_tensor(out=Y1[yq][:], in0=Z[:, 1 + a0:1 + a0 + TBW], scalar=cw(1), in1=Y1[yq][:],
                                           op0=ALU.mult, op1=ALU.add)
                    return e.scalar_tensor_tensor(out=BR[:, f, tbs(tb)], in0=Z[:, a0:a0 + TBW], scalar=cw(0), in1=Y1[yq][:],
                                                  op0=ALU.mult, op1=ALU.add)

                S.op("dve", fconv, reads=[("z", tb), ("z", tb - 1), "z0", "prm"], writes=[("y1", yq), ("br", f, tb)])
        scb = wload([(0, NKT, 512, win_view(l, 768, 512))], None)
        for f in range(4):
            for tb in range(NTB):
                pi = nb()
                proj_group(pi, scb, f * 128, tb)
                S.op("dve", lambda e, pi=pi, f=f, tb=tb: e.tensor_tensor(out=BR[:, f, tbs(tb)], in0=PS[:, pi, :],
                                                                        in1=BR[:, f, tbs(tb)], op=ALU.mult),
                     reads=[("ps", pi), ("br", f, tb)], writes=[("br", f, tb)])
        S.barrier()
        A.pop()
        merge_branch(l, 1, w_co_d)

        S.enabled = DBG >= 4
        A.push()
        WBU = [A.alloc("wbu", [128, 2048], BF16) for _ in range(2)]
        CW = A.alloc("cw", [128, 16, 2, 64], BF16)
        L1 = A.alloc("l1", [128, 2, 16], F32)
        L2 = A.alloc("l2", [128, 2, 16], F32)
        XC = A.alloc("xc", [128, 2, 16], F32)
        su = wload([(0, NKT, 512, win_view(l, 2304, 512))], None)
        for ct in range(4):
            for tb in range(NTB):
                pi = nb()
                proj_group(pi, su, ct * 128, tb)
                S.op("dve", lambda e, pi=pi, ct=ct, tb=tb: e.tensor_copy(out=BR[:, ct, tbs(tb)], in_=PS[:, pi, :]),
                     reads=[("ps", pi)], writes=[("br", ct, tb)])

        S.barrier()
        def coeffs(eng_name, are, aim, ldt, shape, tmps, key):
            dt_, lr, li, t3, t4, qr, qi, t7 = tmps[:8]
            rd = [key + "_in"]
            wr = [key]
            PI = math.pi
            S.op("act", lambda e: e.activation(out=dt_, in_=ldt, func=AF.Exp), reads=rd, writes=wr)

            ti = tmps[8]

            def red(e, x):
                e.tensor_scalar(out=dt_, in0=x, scalar1=1.0 / (2 * PI), scalar2=0.5, op0=ALU.mult, op1=ALU.add)
                e.tensor_copy(out=ti, in_=dt_)
                e.tensor_copy(out=dt_, in_=ti)
                e.scalar_tensor_tensor(out=x, in0=dt_, scalar=-2 * PI, in1=x, op0=ALU.mult, op1=ALU.add)
                e.tensor_scalar(out=dt_, in0=x, scalar1=-PI, scalar2=2 * PI, op0=ALU.is_lt, op1=ALU.mult)
                e.tensor_tensor(out=x, in0=x, in1=dt_, op=ALU.add)
                e.tensor_scalar(out=dt_, in0=x, scalar1=PI, scalar2=-2 * PI, op0=ALU.is_gt, op1=ALU.mult)
                return e.tensor_tensor(out=x, in0=x, in1=dt_, op=ALU.add)

            def f1(e):
                e.tensor_tensor(out=t3, in0=are, in1=dt_, op=ALU.mult)
                e.tensor_tensor(out=t4, in0=aim, in1=dt_, op=ALU.mult)
                e.tensor_scalar(out=t7, in0=t4, scalar1=0.5 * PI, scalar2=None, op0=ALU.add)
                red(e, t7)
                return red(e, t4)

            S.op(eng_name, f1, reads=wr, writes=wr)

            def f3(e):
                e.activation(out=t3, in_=t3, func=AF.Exp)
                e.activation(out=t7, in_=t7, func=AF.Sin)
                return e.activation(out=t4, in_=t4, func=AF.Sin)

            S.op("act", f3, reads=wr, writes=wr)

            def f4(e):
                e.tensor_tensor(out=lr, in0=t3, in1=t7, op=ALU.mult)
                e.tensor_tensor(out=li, in0=t3, in1=t4, op=ALU.mult)
                e.tensor_scalar(out=t3, in0=lr, scalar1=-1.0, scalar2=None, op0=ALU.add)
                e.tensor_tensor(out=t4, in0=are, in1=are, op=ALU.mult)
                e.tensor_tensor(out=t7, in0=aim, in1=aim, op=ALU.mult)
                e.tensor_tensor(out=t4, in0=t4, in1=t7, op=ALU.add)
                e.reciprocal(out=t4, in_=t4)
                e.tensor_tensor(out=qr, in0=t3, in1=are, op=ALU.mult)
                e.tensor_tensor(out=t7, in0=li, in1=aim, op=ALU.mult)
                e.tensor_tensor(out=qr, in0=qr, in1=t7, op=ALU.add)
                e.tensor_tensor(out=qr, in0=qr, in1=t4, op=ALU.mult)
                e.tensor_tensor(out=qi, in0=li, in1=are, op=ALU.mult)
                e.tensor_tensor(out=t7, in0=t3, in1=aim, op=ALU.mult)
                e.tensor_tensor(out=qi, in0=qi, in1=t7, op=ALU.subtract)
                return e.tensor_tensor(out=qi, in0=qi, in1=t4, op=ALU.mult)

            S.op(eng_name, f4, reads=wr, writes=wr)
            return lr, li, qr, qi

        A.push()
        TA = [A.alloc("ta", [128, 16], F32) for _ in range(8)] + [A.alloc("tai", [128, 16], mybir.dt.int32)]
        lrA, liA, qrA, qiA = coeffs("dve", PRM[:, l, P_AA:P_AA + 16], PRM[:, l, P_AI:P_AI + 16], PRM[:, l, P_LDT:P_LDT + 16],
                                [128, 16], [t[:] for t in TA], "cfA")

        def fL(e):
            e.tensor_copy(out=L1[:, 0, :], in_=lrA)
            e.tensor_copy(out=L1[:, 1, :], in_=lrA)
            e.tensor_scalar(out=L2[:, 0, :], in0=liA, scalar1=-1.0, scalar2=None, op0=ALU.mult)
            e.tensor_copy(out=L2[:, 1, :], in_=liA)
            return e.memset(XC[:], 0.0)

        S.op("dve", fL, reads=["cfA"], writes=["L", "xc"])
        wbase = A.offs["ws0"]
        U4 = 16 * SCH * 4
        mkb = lambda nm, off, dt=F32, shape=None: nc.alloc_sbuf_tensor_at(f"{nm}_l{l}", shape or [128, 16, SCH], dt,
                                                                          offset=wbase + off)
        CJ = mkb("cj", 0, BF16)
        SJ = mkb("sj", U4 // 2, BF16)
        DEC = mkb("dec", U4)
        T1s = mkb("t1s", 2 * U4)
        T2s = mkb("t2s", 3 * U4)
        T3s = mkb("t3s", 4 * U4)
        XB = mkb("xb", 5 * U4, BF16, [128, 2, 16, SCH])
        TIs = mkb("tis", 5 * U4, mybir.dt.int32)
        PHI = A.alloc("phi", [128, 16], F32)
        RR = A.alloc("rr", [128, 16], F32)
        S.op("act", lambda e: e.activation(out=PHI[:], in_=PRM[:, l, P_LDT:P_LDT + 16], func=AF.Exp), reads=["prm"],
             writes=["tabp"])

        def ft1(e):
            e.tensor_tensor(out=RR[:], in0=PRM[:, l, P_AA:P_AA + 16], in1=PHI[:], op=ALU.mult)
            return e.tensor_tensor(out=PHI[:], in0=PRM[:, l, P_AI:P_AI + 16], in1=PHI[:], op=ALU.mult)

        S.op("dve", ft1, reads=["tabp", "prm"], writes=["tabp"])
        S.op("act", lambda e: e.activation(out=RR[:], in_=RR[:], func=AF.Exp), reads=["tabp"], writes=["tabp"])

        def ft2(e):
            e.tensor_tensor(out=T2s[:], in0=PHI[:].unsqueeze(2).to_broadcast([128, 16, SCH]),
                            in1=JI[:].unsqueeze(1).to_broadcast([128, 16, SCH]), op=ALU.mult)
            e.tensor_scalar(out=T3s[:], in0=T2s[:], scalar1=0.5 * math.pi, scalar2=None, op0=ALU.add)
            red_angle(e, T2s[:], T1s[:], TIs[:])
            red_angle(e, T3s[:], T1s[:], TIs[:])
            e.tensor_copy(out=DEC[:], in_=RR[:].unsqueeze(2).to_broadcast([128, 16, SCH]))
            return e.memset(DEC[:, :, 0:1], 0.0)

        S.op("dve", ft2, reads=["tabp", "ji"], writes=["tab"])

        def ft3(e):
            e.activation(out=SJ[:], in_=T2s[:], func=AF.Sin)
            return e.activation(out=CJ[:], in_=T3s[:], func=AF.Sin)

        S.op("act", ft3, reads=["tab"], writes=["tab"])
        def fCd(e):
            return [e.dma_start(out=CW[:, :, 0, :], in_=ssmC_d[l][:, 0, :].rearrange("p (a b) -> p a b", a=16)),
                    e.dma_start(out=CW[:, :, 1, :], in_=ssmC_d[l][:, 1, :].rearrange("p (a b) -> p a b", a=16))]

        S.op("pool", fCd, writes=["cw"], dma=True, chan="cw", ndma=2)
        S.op("pool", lambda e: e.tensor_scalar(out=CW[:, :, 1, :], in0=CW[:, :, 1, :], scalar1=-1.0, scalar2=None,
                                               op0=ALU.mult), reads=["cw"], writes=["cw"])
        DG = A.alloc("dg", [128, 4, 128], F32)
        BB = A.alloc("bb", [128, 2, 512], F32)
        TQ = [A.alloc("tq", [128, 512], F32) for _ in range(2)]
        for qt in range(4):
            S.op("sp", lambda e, qt=qt: e.dma_start(out=BB[:], in_=ssmB_d[l][:, :, qt * 512:(qt + 1) * 512]),
                 writes=["bb"], dma=True, chan="bb")
            for qi_, qsrc in enumerate((qrA, qiA)):
                def fdg(e, qsrc=qsrc, qt=qt):
                    ins = None
                    for j in range(4):
                        ins = e.tensor_scalar(out=DG[:, j, :], in0=IDN[:], scalar1=qsrc[:, qt * 4 + j:qt * 4 + j + 1],
                                              scalar2=None, op0=ALU.mult)
                    return ins

                S.op("dve", fdg, reads=["cfA", "idn"], writes=["dg"])

                def fqb(e, qi_=qi_):
                    ins = None
                    for j in range(4):
                        ins = e.matmul(PS[:, qi_, j * 128:(j + 1) * 128], lhsT=ONESF[:], rhs=DG[:, j, :], start=True, stop=True)
                    return ins

                S.op("pe", fqb, reads=["dg", "onesf"], writes=[("ps", qi_)])

            def fB(e, qt=qt):
                hs_ = slice(qt * 512, (qt + 1) * 512)
                QBr = PS[:, 0, :]
                QBi = PS[:, 1, :]
                e.tensor_tensor(out=TQ[0][:], in0=QBr, in1=BB[:, 0, :], op=ALU.mult)
                e.tensor_tensor(out=TQ[1][:], in0=QBi, in1=BB[:, 1, :], op=ALU.mult)
                e.tensor_tensor(out=WBU[0][:, hs_], in0=TQ[0][:], in1=TQ[1][:], op=ALU.subtract)
                e.tensor_tensor(out=TQ[0][:], in0=QBr, in1=BB[:, 1, :], op=ALU.mult)
                e.tensor_tensor(out=TQ[1][:], in0=QBi, in1=BB[:, 0, :], op=ALU.mult)
                return e.tensor_tensor(out=WBU[1][:, hs_], in0=TQ[0][:], in1=TQ[1][:], op=ALU.add)

            S.op("dve", fB, reads=[("ps", 0), ("ps", 1), "bb"], writes=[("wbu", k) for k in range(8)] + ["tq"])
        S.barrier()
        A.pop()

        A.push()
        XS2 = [A.alloc("xs", [128, 2, 16, SCH], F32) for _ in range(2)]
        TM1 = A.alloc("tm1", [128, 2, 16], F32)
        TM2 = A.alloc("tm2", [128, 2, 16], F32)
        YS = A.alloc("ys", [128, 4, SCH], F32)
        G1 = A.alloc("g1", [128, 4, SCH], F32)
        flat2 = lambda ap: ap.rearrange("p a b -> p (a b)")
        PSBU = [("ps", 4), ("ps", 5), ("ps", 6), ("ps", 7)]

        def stage_a1(c):
            cs_ = slice(c * SCH, (c + 1) * SCH)
            tbk = (c * SCH) // TBW
            xk = c % 2
            XS = XS2[xk]

            def fbu(e, cs_=cs_):
                ins = None
                for pr in range(16):
                    ct = pr // 4
                    for ri in range(2):
                        o0 = (ri * 16 + pr) * SCH
                        bank = 4 + o0 // 512
                        oo = o0 % 512
                        ins = e.matmul(PS[:, bank, oo:oo + SCH], lhsT=WBU[ri][:, pr * 128:(pr + 1) * 128], rhs=BR[:, ct, cs_],
                                       start=True, stop=True)
                return ins

            S.op("pe", fbu, reads=[("wbu", ct) for ct in range(8)] + [("br", ct, tbk) for ct in range(4)], writes=PSBU)
            S.op("act", lambda e: e.activation(out=XS[:].rearrange("p a b c -> p (a b c)"),
                                               in_=PS[:, 4:8, :].rearrange("p a b -> p (a b)"), func=AF.Identity),
                 reads=PSBU, writes=[("xs_re", xk), ("xs_im", xk)])

        def stage_a2(c):
            xk = c % 2
            XS = XS2[xk]
            BRe = XS[:, 0, :, :]
            BIm = XS[:, 1, :, :]
            kre, kim = ("xs_re", xk), ("xs_im", xk)

            def fscan(e):
                e.tensor_tensor(out=T1s[:], in0=BRe, in1=CJ[:], op=ALU.mult)
                e.tensor_tensor(out=T2s[:], in0=BIm, in1=SJ[:], op=ALU.mult)
                e.tensor_tensor(out=T1s[:], in0=T1s[:], in1=T2s[:], op=ALU.add)
                e.tensor_tensor(out=T2s[:], in0=BRe, in1=SJ[:], op=ALU.mult)
                e.tensor_tensor(out=BIm, in0=BIm, in1=CJ[:], op=ALU.mult)
                e.tensor_tensor(out=BIm, in0=BIm, in1=T2s[:], op=ALU.subtract)
                e.tensor_tensor(out=TM1[:], in0=XC[:], in1=L1[:], op=ALU.mult)
                e.tensor_tensor(out=TM2[:], in0=XC[:, ::-1, :], in1=L2[:], op=ALU.mult)
                e.tensor_tensor(out=TM1[:], in0=TM1[:], in1=TM2[:], op=ALU.add)
                e.tensor_tensor(out=T1s[:, :, 0], in0=T1s[:, :, 0], in1=TM1[:, 0, :], op=ALU.add)
                e.tensor_tensor(out=BIm[:, :, 0], in0=BIm[:, :, 0], in1=TM1[:, 1, :], op=ALU.add)
                e.tensor_tensor_scan(out=flat2(BRe), data0=flat2(DEC[:]), data1=flat2(T1s[:]), initial=0.0, op0=ALU.mult,
                                     op1=ALU.add)
                e.tensor_tensor_scan(out=flat2(T2s[:]), data0=flat2(DEC[:]), data1=flat2(BIm), initial=0.0, op0=ALU.mult,
                                     op1=ALU.add)
                e.tensor_tensor(out=T1s[:], in0=T2s[:], in1=SJ[:], op=ALU.mult)
                e.tensor_tensor(out=BIm, in0=BRe, in1=CJ[:], op=ALU.mult)
                e.tensor_tensor(out=XB[:, 0, :, :], in0=BIm, in1=T1s[:], op=ALU.subtract)
                e.tensor_tensor(out=XC[:, 0, :], in0=BIm[:, :, SCH - 1], in1=T1s[:, :, SCH - 1], op=ALU.subtract)
                e.tensor_tensor(out=T1s[:], in0=BRe, in1=SJ[:], op=ALU.mult)
                e.tensor_tensor(out=BIm, in0=T2s[:], in1=CJ[:], op=ALU.mult)
                e.tensor_tensor(out=XC[:, 1, :], in0=BIm[:, :, SCH - 1], in1=T1s[:, :, SCH - 1], op=ALU.add)
                return e.tensor_tensor(out=XB[:, 1, :, :], in0=BIm, in1=T1s[:], op=ALU.add)

            S.op("dve", fscan, reads=[kre, kim, "L", "xc", "tab"], writes=[kre, kim, "t1", "t2", "xc", "xb"])
            py = nb() % 4

            def fcm(e, py=py):
                ins = None
                for ct in range(4):
                    for half in range(2):
                        k = 0
                        for pl in range(2):
                            pr = ct * 4 + half * 2 + pl
                            for ri in range(2):
                                ins = e.matmul(PS[half * 64:(half + 1) * 64, py, ct * SCH:(ct + 1) * SCH],
                                               lhsT=CW[:, pr, ri, :], rhs=XB[:, ri, pr, :], start=(k == 0), stop=(k == 3),
                                               tile_position=(0, half * 64))
                                k += 1
                return ins

            S.op("pe", fcm, reads=["xb", "cw"], writes=[("ps", py)])
            return py

        def stage_b(c, py):
            cs_ = slice(c * SCH, (c + 1) * SCH)
            tbk = (c * SCH) // TBW

            def fys(e, py=py, cs_=cs_):
                for ct in range(4):
                    e.scalar_tensor_tensor(out=YS[:, ct, :], in0=BR[:, ct, cs_], scalar=PRM[:, l, P_SSMD + ct:P_SSMD + ct + 1],
                                           in1=PS[:, py, ct * SCH:(ct + 1) * SCH], op0=ALU.mult, op1=ALU.add)
                e.tensor_tensor(out=G1[:], in0=YS[:], in1=YS[:], op=ALU.mult)
                e.tensor_scalar(out=G1[:], in0=G1[:], scalar1=0.044715, scalar2=1.0, op0=ALU.mult, op1=ALU.add)
                return e.tensor_tensor(out=G1[:], in0=G1[:], in1=YS[:], op=ALU.mult)

            S.op("dve", fys, reads=[("ps", py), "prm", ("br", 0, tbk)], writes=["ys", "g1"])
            S.op("act", lambda e: e.activation(out=G1[:], in_=G1[:], func=AF.Sigmoid, scale=2.0 * math.sqrt(2.0 / math.pi)),
                 reads=["g1"], writes=["g1"])
            S.op("dve", lambda e, cs_=cs_: e.tensor_tensor(out=BR[:, :, cs_], in0=YS[:], in1=G1[:], op=ALU.mult),
                 reads=["ys", "g1"], writes=[("yso", c)])

        pys = {}
        stage_a1(0)
        for c in range(NCH + 1):
            if c + 1 < NCH:
                stage_a1(c + 1)
            if c < NCH:
                pys[c] = stage_a2(c)
            if c >= 1:
                stage_b(c - 1, pys[c - 1])
        S.barrier()
        A.pop()
        SG = [A.alloc("sg", [128, TBW], F32) for _ in range(2)]
        sgl = wload([(0, 4, 512, w_glu_d[l].rearrange("(kt p) n -> p kt n", p=128))], None)
        wg = wview(sgl, 4, 512)
        for tb in range(NTB):
            pis = []
            for f in range(4):
                pi = nb()
                pis.append(pi)

                def fg(e, pi=pi, f=f, tb=tb):
                    ins = None
                    for kt in range(4):
                        ins = e.matmul(PS[:, pi, :], lhsT=wg[:, kt, f * 128:(f + 1) * 128], rhs=BR[:, kt, tbs(tb)],
                                       start=(kt == 0), stop=(kt == 3))
                    return ins

                S.op("pe", fg, reads=[("w", sgl)] + [("br", kt, tb) for kt in range(4)], writes=[("ps", pi)])
            for f in range(4):
                q = f % 2
                S.op("act", lambda e, q=q, pi=pis[f]: e.activation(out=SG[q][:], in_=PS[:, pi, :], func=AF.Sigmoid),
                     reads=[("ps", pis[f])], writes=[("sg", q)])
                S.op("dve", lambda e, q=q, f=f, tb=tb: e.tensor_tensor(out=BR[:, f, tbs(tb)], in0=BR[:, f, tbs(tb)], in1=SG[q][:],
                                                                      op=ALU.mult),
                     reads=[("sg", q), ("br", f, tb)], writes=[("br", f, tb)])
        S.barrier()
        A.pop()
        merge_branch(l, 2, w_so_d)

        S.enabled = DBG >= 5
        for half in range(2):
            sm = wload([(0, NKT, 512, w_mix_d[l].rearrange("(kt p) n -> p kt n", p=128)[:, :, half * 512:(half + 1) * 512])],
                       None)
            wm = wview(sm, NKT, 512)
            for fl in range(4):
                f2 = half * 4 + fl
                for tb in range(NTB):
                    pi = nb()

                    def fm(e, pi=pi, fl=fl, tb=tb, wm=wm):
                        ins = None
                        for kt in range(NKT):
                            ins = e.matmul(PS[:, pi, :], lhsT=wm[:, kt, fl * 128:(fl + 1) * 128], rhs=MRG[:, kt, tbs(tb)],
                                           start=(kt == 0), stop=(kt == NKT - 1))
                        return ins

                    S.op("pe", fm, reads=[("w", sm)] + [("mrg", kt, tb) for kt in range(NKT)], writes=[("ps", pi)])
                    S.op("dve", lambda e, pi=pi, f2=f2, tb=tb: e.tensor_tensor(out=xT[:, f2, tbs(tb)], in0=xT[:, f2, tbs(tb)],
                                                                              in1=PS[:, pi, :], op=ALU.add),
                         reads=[("ps", pi), ("x", f2, tb)], writes=[("x", f2, tb)])
        S.barrier()
        A.pop()

        S.enabled = DBG >= 6
        rmsnorm_to_h(l, P_GFFN, "n2")
        A.push()
        SGT = [A.alloc("sgt", [128, TBW], F32) for _ in range(3)]
        wfi = w_fi_d[l].rearrange("(kt p) n -> p kt n", p=128)
        scnt = 0
        for grp in range(2):
            j0 = grp * 11
            jl = 0
            while jl < 11:
                nj = min(2, 11 - jl)
                j = j0 + jl
                sf = wload([(0, NKT, nj * 128, wfi[:, :, j * 128:(j + nj) * 128]),
                            (NKT * nj * 128, NKT, nj * 128, wfi[:, :, FFH + j * 128:FFH + (j + nj) * 128])], None)
                wgt = wview(sf, NKT, nj * 128, 0)
                wup = wview(sf, NKT, nj * 128, NKT * nj * 128)
                for jj in range(nj):
                    for tb in range(NTB):
                        pg = nb()
                        pu = nb()

                        def fgu(e, pg=pg, pu=pu, jj=jj, tb=tb, wgt=wgt, wup=wup):
                            ins = None
                            for kt in range(NKT):
                                e.matmul(PS[:, pg, :], lhsT=wgt[:, kt, jj * 128:(jj + 1) * 128], rhs=hT[:, kt, tbs(tb)],
                                         start=(kt == 0), stop=(kt == NKT - 1))
                            for kt in range(NKT):
                                ins = e.matmul(PS[:, pu, :], lhsT=wup[:, kt, jj * 128:(jj + 1) * 128], rhs=hT[:, kt, tbs(tb)],
                                               start=(kt == 0), stop=(kt == NKT - 1))
                            return ins

                        S.op("pe", fgu, reads=[("w", sf)] + [("h", kt, tb) for kt in range(NKT)],
                             writes=[("ps", pg), ("ps", pu)])
                        q = scnt % 3
                        scnt += 1
                        S.op("act", lambda e, q=q, pg=pg: e.activation(out=SGT[q][:], in_=PS[:, pg, :], func=AF.Silu),
                             reads=[("ps", pg)], writes=[("sgt", q)])
                        S.op("dve", lambda e, q=q, pu=pu, a=jl + jj, tb=tb: e.tensor_tensor(out=FFA[:, a, tbs(tb)], in0=PS[:, pu, :],
                                                                                           in1=SGT[q][:], op=ALU.mult),
                             reads=[("ps", pu), ("sgt", q)], writes=[("ffa", jl + jj, tb)])
                jl += nj
            for fp in range(4):
                so_ = wload([(0, 11, 256, w_fo_d[l][j0 * 128:(j0 + 11) * 128, fp * 256:(fp + 1) * 256].rearrange(
                    "(j p) n -> p j n", p=128))], None)
                wo_ = wview(so_, 11, 256)
                for fl in range(2):
                    f = fp * 2 + fl
                    for tb in range(NTB):
                        pi = nb()

                        def ffo(e, pi=pi, fl=fl, tb=tb, wo_=wo_):
                            ins = None
                            for a in range(11):
                                ins = e.matmul(PS[:, pi, :], lhsT=wo_[:, a, fl * 128:(fl + 1) * 128], rhs=FFA[:, a, tbs(tb)],
                                               start=(a == 0), stop=(a == 10))
                            return ins

                        S.op("pe", ffo, reads=[("w", so_)] + [("ffa", a, tb) for a in range(11)], writes=[("ps", pi)])
                        S.op("dve", lambda e, pi=pi, f=f, tb=tb: e.tensor_tensor(out=xT[:, f, tbs(tb)], in0=xT[:, f, tbs(tb)],
                                                                                in1=PS[:, pi, :], op=ALU.add),
                             reads=[("ps", pi), ("x", f, tb)], writes=[("x", f, tb)])
        S.barrier()
        A.pop()

    for l_ in range(NL):
        layer(l_)

    S.enabled = True
    A.push()
    SQ = [A.alloc("sq", [128, TBW], BF16) for _ in range(3)]
    MS = [A.alloc("ms", [128, TBW], F32) for _ in range(2)]
    OST = [A.alloc("ost", [128, TBW], F32) for _ in range(4)]
    ocnt = 0
    outs = []
    for tb in range(NTB):
        pi = nb()
        for kt in range(NKT):
            q = (tb * NKT + kt) % 3
            S.op("act", lambda e, q=q, kt=kt, tb=tb: e.activation(out=SQ[q][:], in_=xT[:, kt, tbs(tb)], func=AF.Square),
                 reads=[("x", kt, tb)], writes=[("sq", q)])
            S.op("pe", lambda e, q=q, kt=kt, pi=pi: e.matmul(PS[:, pi, :], lhsT=ONES[:], rhs=SQ[q][:], start=(kt == 0),
                                                            stop=(kt == NKT - 1)),
                 reads=[("sq", q), "ones"], writes=[("ps", pi)])
        m = tb % 2
        S.op("dve", lambda e, m=m, pi=pi: e.tensor_scalar(out=MS[m][:], in0=PS[:, pi, :], scalar1=1.0 / D, scalar2=EPS,
                                                         op0=ALU.mult, op1=ALU.add),
             reads=[("ps", pi)], writes=[("ms", m)])
        S.op("act", lambda e, m=m: e.activation(out=MS[m][:], in_=MS[m][:], func=AF.Sqrt), reads=[("ms", m)], writes=[("ms", m)])
        S.op("dve", lambda e, m=m: e.reciprocal(out=MS[m][:], in_=MS[m][:]), reads=[("ms", m)], writes=[("ms", m)])
        for kt in range(NKT):
            oq = ocnt % 4
            ocnt += 1
            S.op("dve", lambda e, m=m, kt=kt, tb=tb, oq=oq: e.scalar_tensor_tensor(
                out=OST[oq][:], in0=xT[:, kt, tbs(tb)], scalar=PRM[:, 0, P_GFIN + kt:P_GFIN + kt + 1], in1=MS[m][:],
                op0=ALU.mult, op1=ALU.mult),
                 reads=[("x", kt, tb), ("ms", m), "prm"], writes=[("ost", oq)])
            o = S.op("sp", lambda e, kt=kt, tb=tb, oq=oq: e.dma_start(out=outT_d[kt * 128:(kt + 1) * 128, tbs(tb)], in_=OST[oq][:]),
                     reads=[("ost", oq)], writes=[("out", kt, tb)], dma=True, chan=f"o{oq}")
            outs.append(o)
    S.op("sp", lambda e: e.nop(), extra=outs)
    A.pop()
    S.emit(nc)
    return nc


def _host_consts():
    cst = np.zeros((128, CSTW), np.float32)
    cst[:, 1152 + 2 * SEQ:1152 + 2 * SEQ + SCH] = np.arange(SCH, dtype=np.float32)[None, :]
    cst[:, CSTW - 128:] = np.eye(128, dtype=np.float32)
    j = np.arange(128)[:, None]
    i = np.arange(128)[None, :]
    cur = np.where(j <= i, 0.0, -240000.0).astype(np.float32)
    prev = np.where(j > i, 0.0, -240000.0).astype(np.float32)
    cst[:, 0:512] = np.tile(cur, (1, 4))
    cst[:, 512:1024] = np.tile(prev, (1, 4))
    perm = np.zeros((128, 128), np.float32)
    for h in range(2):
        for ii in range(8):
            perm[h * 64 + ii + 8, h * 64 + ii] = -1.0
            perm[h * 64 + ii, h * 64 + ii + 8] = 1.0
    cst[:, 1024:1152] = perm
    pos = np.arange(SEQ, dtype=np.float32)
    inv_freq = (np.float32(500000.0) ** (-np.arange(0, 16, 2, dtype=np.float32) / np.float32(16))).astype(np.float32)
    ang = pos[:, None] * inv_freq[None, :]
    c = np.cos(ang).astype(np.float32).T
    s = np.sin(ang).astype(np.float32).T
    cos_t = np.ones((128, SEQ), np.float32)
    sin_t = np.zeros((128, SEQ), np.float32)
    for h in range(2):
        cos_t[h * 64:h * 64 + 8] = c
        cos_t[h * 64 + 8:h * 64 + 16] = c
        sin_t[h * 64:h * 64 + 8] = s
        sin_t[h * 64 + 8:h * 64 + 16] = s
    cst[:, 1152:1152 + SEQ] = cos_t
    cst[:, 1152 + SEQ:1152 + 2 * SEQ] = sin_t
    return cst


def _prep_inputs(inp, NL):
    f = lambda a: np.ascontiguousarray(np.asarray(a, dtype=np.float32))
    w_in = f(inp["w_in"])[:NL].copy()
    w_in[:, :, 0:512] = w_in[:, :, 0:512].reshape(NL, D, 2, 4, 64).transpose(0, 1, 3, 2, 4).reshape(NL, D, 512)
    w_ao = f(inp["w_attn_o"])[:NL].reshape(NL, 2, 4, 64, D).transpose(0, 2, 1, 3, 4).reshape(NL, 512, D)
    prm = np.zeros((NL, 128, NPRM), np.float32)
    t8 = lambda v: v.reshape(-1, 128).T
    a_re = f(inp["ssm_a_re"])
    a_im = f(inp["ssm_a_im"])
    ldt = f(inp["ssm_log_dt"])
    b_re = f(inp["ssm_b_re"])
    b_im = f(inp["ssm_b_im"])
    c_re = f(inp["ssm_c_re"])
    c_im = f(inp["ssm_c_im"])
    ssmB = np.zeros((NL, 128, 2, 2048), np.float32)
    ssmC = np.zeros((NL, 128, 2, 16, 64), np.float32)
    sinks = f(inp["attn_sinks"])
    for l in range(NL):
        prm[l, :, P_GMIX:P_GMIX + 8] = t8(f(inp["norm_mix"])[l])
        prm[l, :, P_GFFN:P_GFFN + 8] = t8(f(inp["norm_ffn"])[l])
        prm[l, :, P_BG:P_BG + 24] = t8(f(inp["b_gate"])[l])
        cw = f(inp["conv_w"])[l]
        for j in range(3):
            prm[l, :, P_CONVW + j * 4:P_CONVW + j * 4 + 4] = t8(cw[j])
        prm[l, :, P_SSMD:P_SSMD + 4] = t8(f(inp["ssm_d"])[l])
        for kvh in range(2):
            prm[l, kvh * 64:(kvh + 1) * 64, P_SINK:P_SINK + 4] = sinks[l, kvh * 4:(kvh + 1) * 4][None, :]
        arA = a_re[l].reshape(16, 2, 64).transpose(1, 2, 0).reshape(128, 16)
        aiA = a_im[l].reshape(16, 2, 64).transpose(1, 2, 0).reshape(128, 16)
        ldA = np.broadcast_to(ldt[l].reshape(16, 2, 1), (16, 2, 64)).transpose(1, 2, 0).reshape(128, 16)
        prm[l, :, P_AA:P_AA + 16] = arA
        prm[l, :, P_AI:P_AI + 16] = aiA
        prm[l, :, P_LDT:P_LDT + 16] = ldA
        prm[l, :, P_GFIN:P_GFIN + 8] = t8(f(inp["norm_final"]))
        for ct in range(4):
            for gl in range(8):
                g = ct * 8 + gl
                c0 = g * 64
                ssmB[l, gl * 16:(gl + 1) * 16, 0, c0:c0 + 64] = b_re[l, g].T
                ssmB[l, gl * 16:(gl + 1) * 16, 1, c0:c0 + 64] = b_im[l, g].T
        for pr in range(16):
            plh = (pr % 4) % 2
            for gsel in range(2):
                g = 2 * pr + gsel
                c0 = plh * 32 + gsel * 16
                ssmC[l, gsel * 64:(gsel + 1) * 64, 0, pr, c0:c0 + 16] = c_re[l, g].T
                ssmC[l, gsel * 64:(gsel + 1) * 64, 1, pr, c0:c0 + 16] = c_im[l, g].T
    shared = {
        "w_in": w_in, "w_attn_o": np.ascontiguousarray(w_ao), "w_conv_o": f(inp["w_conv_o"])[:NL],
        "w_ssm_glu": f(inp["w_ssm_glu"])[:NL], "w_ssm_o": f(inp["w_ssm_o"])[:NL], "w_mix_o": f(inp["w_mix_o"])[:NL],
        "w_ffn_in": f(inp["w_ffn_in"])[:NL], "w_ffn_out": f(inp["w_ffn_out"])[:NL],
        "prm": prm, "ssmB": ssmB, "ssmC": ssmC.reshape(NL, 128, 2, 1024), "cst": _host_consts(),
    }
    return shared


_NC_CACHE = {}


def kernel(NL=NLAYER, DBG=99, **inp):
    x = np.asarray(inp["x"], dtype=np.float32)
    B = x.shape[0]
    shared = _prep_inputs(inp, NL)
    if (NL, DBG) not in _NC_CACHE:
        _NC_CACHE[(NL, DBG)] = build_program(NL, DBG)
    nc = _NC_CACHE[(NL, DBG)]
    in_maps = []
    for b in range(B):
        m = dict(shared)
        m["xT"] = np.ascontiguousarray(x[b].T)
        in_maps.append(m)
    res = run_bass_kernel_spmd(nc, in_maps, core_ids=list(range(B)))
    out = np.stack([np.ascontiguousarray(r["outT"].T) for r in res.results], axis=0)
    return out.astype(np.float32)
```

```python
import contextlib
import math
import numpy as np
import concourse.bass as bass
import concourse.mybir as mybir
from concourse.bass_utils import run_bass_kernel_spmd

F32 = mybir.dt.float32
BF16 = mybir.dt.bfloat16
AF = mybir.ActivationFunctionType
ALU = mybir.AluOpType

D = 1024
SEQ = 2048
NLAYER = 4
NKT = 8
NTB = 4
TBW = 512
FFH = 2816
INC = 5888
EPS = 1e-6
SCH = 64
NCH = SEQ // SCH


class Op:
    __slots__ = ("eng", "fn", "deps", "dma", "chan", "ticket", "need_inc", "idx", "ndma")

    def __init__(self, eng, fn, dma, chan, ndma):
        self.eng = eng
        self.fn = fn
        self.dma = dma
        self.chan = chan
        self.ndma = ndma
        self.deps = set()
        self.ticket = None
        self.need_inc = False


class Sched:
    ENGS = ("pe", "act", "dve", "pool", "sp")

    def __init__(self):
        self.ops = []
        self.last_w = {}
        self.readers = {}
        self.last_by_eng = {}
        self.dmas_since = []

    enabled = True

    def op(self, eng, fn, reads=(), writes=(), dma=False, chan=None, ndma=1, extra=()):
        if not self.enabled:
            return None
        o = Op(eng, fn, dma, chan, ndma)
        o.idx = len(self.ops)
        deps = set(extra)
        for k in reads:
            w = self.last_w.get(k)
            if w is not None:
                deps.add(w)
        for k in writes:
            w = self.last_w.get(k)
            if w is not None:
                deps.add(w)
            deps.update(self.readers.get(k, ()))
        for k in reads:
            self.readers.setdefault(k, []).append(o)
        for k in writes:
            self.last_w[k] = o
            self.readers[k] = []
        deps.discard(o)
        o.deps = deps
        self.ops.append(o)
        if dma:
            self.dmas_since.append(o)
        else:
            self.last_by_eng[eng] = o
        return o

    def barrier(self):
        if not self.enabled:
            return
        allops = [o for o in self.last_by_eng.values()] + list(self.dmas_since)
        self.last_w = {}
        self.readers = {}
        self.dmas_since = []
        for e in self.ENGS:
            self.op(e, lambda eng: eng.nop(), extra=[o for o in allops])

    def emit(self, nc):
        for o in self.ops:
            for d in o.deps:
                if d.dma:
                    continue
                if d.eng != o.eng or d.eng != "pe":
                    d.need_inc = True
        counts = {e: 0 for e in self.ENGS}
        chan_counts = {}
        for o in self.ops:
            if o.dma:
                c = chan_counts.get(o.chan, 0) + o.ndma
                chan_counts[o.chan] = c
                o.ticket = ("c:" + o.chan, 16 * c)
            elif o.need_inc:
                counts[o.eng] += 1
                o.ticket = ("e:" + o.eng, counts[o.eng])
        sem_names = ["e:" + e for e in self.ENGS] + ["c:" + c for c in chan_counts] + ["k:" + e for e in self.ENGS]
        with contextlib.ExitStack() as st:
            sems = {}
            for n in sem_names:
                sems[n] = st.enter_context(nc.semaphore(n.replace(":", "_")))
            block = st.enter_context(nc.Block())
            per_eng = {e: [o for o in self.ops if o.eng == e] for e in self.ENGS}

            def run(engname, eng):
                waited = {}
                CH.sem = sems["k:" + engname]
                CH.cnt = 0
                for o in per_eng[engname]:
                    need = {}
                    for d in o.deps:
                        if (not d.dma) and d.eng == engname and engname == "pe":
                            continue
                        s, v = d.ticket
                        if waited.get(s, 0) >= v:
                            continue
                        if need.get(s, 0) < v:
                            need[s] = v
                    for s, v in need.items():
                        eng.wait_ge(sems[s], v)
                        waited[s] = v
                    if engname in ("act", "dve", "pool") and not o.dma:
                        ins = o.fn(EngProxy(eng))
                    else:
                        ins = o.fn(eng)
                    if o.dma:
                        if not isinstance(ins, (list, tuple)):
                            ins = [ins]
                        assert len(ins) == o.ndma
                        for i_ in ins:
                            i_.then_inc(sems[o.ticket[0]], 16)
                    elif o.need_inc:
                        ins.then_inc(sems[o.ticket[0]], 1)

            @block.tensor
            def _(e):
                run("pe", e)

            @block.scalar
            def _(e):
                run("act", e)

            @block.vector
            def _(e):
                run("dve", e)

            @block.gpsimd
            def _(e):
                run("pool", e)

            @block.sync
            def _(e):
                run("sp", e)


class _Chain:
    sem = None
    cnt = 0


CH = _Chain()


def C(e, ins):
    CH.cnt += 1
    ins.then_inc(CH.sem, 1)
    e.wait_ge(CH.sem, CH.cnt)
    return ins


class EngProxy:
    def __init__(self, e):
        self._e = e
        self._last = None

    def __getattr__(self, name):
        real = getattr(self._e, name)

        def w(*a, **k):
            if self._last is not None:
                C(self._e, self._last)
            ins = real(*a, **k)
            self._last = ins
            return ins

        return w


def red_angle(e, x, tmpf, tmpi):
    PI = math.pi
    e.tensor_scalar(out=tmpf, in0=x, scalar1=1.0 / (2 * PI), scalar2=0.5, op0=ALU.mult, op1=ALU.add)
    e.tensor_copy(out=tmpi, in_=tmpf)
    e.tensor_copy(out=tmpf, in_=tmpi)
    e.scalar_tensor_tensor(out=x, in0=tmpf, scalar=-2 * PI, in1=x, op0=ALU.mult, op1=ALU.add)
    e.tensor_scalar(out=tmpf, in0=x, scalar1=-PI, scalar2=2 * PI, op0=ALU.is_lt, op1=ALU.mult)
    e.tensor_tensor(out=x, in0=x, in1=tmpf, op=ALU.add)
    e.tensor_scalar(out=tmpf, in0=x, scalar1=PI, scalar2=-2 * PI, op0=ALU.is_gt, op1=ALU.mult)
    return e.tensor_tensor(out=x, in0=x, in1=tmpf, op=ALU.add)


class Arena:
    def __init__(self, nc, lo=16512, hi=225792):
        self.nc = nc
        self.lo = lo
        self.hi = hi
        self.top = lo
        self.n = 0
        self.stack = []
        self.offs = {}
        self.peak = lo

    def alloc(self, name, shape, dtype):
        nbytes = int(np.prod(shape[1:])) * mybir.dt.size(dtype)
        off = (self.top + 63) // 64 * 64
        assert off + nbytes <= self.hi, (name, off, nbytes, self.hi)
        t = self.nc.alloc_sbuf_tensor_at(f"{name}_{self.n}", list(shape), dtype, offset=off)
        self.offs[name] = off
        self.top = off + nbytes
        self.peak = max(self.peak, self.top)
        self.n += 1
        return t

    def push(self):
        self.stack.append(self.top)

    def pop(self):
        self.top = self.stack.pop()


P_GMIX = 0
P_GFFN = 8
P_BG = 16
P_CONVW = 40
P_SSMD = 52
P_SINK = 56
P_AA = 60
P_AI = 76
P_LDT = 92
P_GFIN = 108
NPRM = 116

W_SLOTS = 3
CSTW = 2 * 512 + 128 + 2 * SEQ + SCH + 128


def build_program(NL=NLAYER, DBG=99):
    nc = bass.Bass("TRN2", target_bir_lowering=False)
    dt_in = lambda name, shape: nc.dram_tensor(name, list(shape), F32, kind="ExternalInput").ap()
    xT_d = dt_in("xT", [D, SEQ])
    w_in_d = dt_in("w_in", [NL, D, INC])
    w_ao_d = dt_in("w_attn_o", [NL, 512, D])
    w_co_d = dt_in("w_conv_o", [NL, 512, D])
    w_glu_d = dt_in("w_ssm_glu", [NL, 512, 512])
    w_so_d = dt_in("w_ssm_o", [NL, 512, D])
    w_mix_d = dt_in("w_mix_o", [NL, D, D])
    w_fi_d = dt_in("w_ffn_in", [NL, D, 2 * FFH])
    w_fo_d = dt_in("w_ffn_out", [NL, FFH, D])
    prm_d = dt_in("prm", [NL, 128, NPRM])
    ssmB_d = dt_in("ssmB", [NL, 128, 2, 2048])
    ssmC_d = dt_in("ssmC", [NL, 128, 2, 1024])
    cst_d = dt_in("cst", [128, CSTW])
    outT_d = nc.dram_tensor("outT", [D, SEQ], F32, kind="ExternalOutput").ap()

    S = Sched()
    A = Arena(nc)
    PS = nc.alloc_psum_tensor("ps", [128, 8, 512], F32)

    xT = A.alloc("xT", [128, NKT, SEQ], F32)
    hT = A.alloc("hT", [128, NKT, SEQ], BF16)
    WS = [A.alloc(f"ws{i}", [128, 4096], BF16) for i in range(W_SLOTS)]
    mrg_off = (A.top + 63) // 64 * 64
    MRG = A.alloc("mrg", [128, NKT, SEQ], BF16)
    BR = A.alloc("br", [128, 4, SEQ], BF16)
    FFA = nc.alloc_sbuf_tensor_at("ffa", [128, 11, SEQ], BF16, offset=mrg_off)
    PRM = A.alloc("prm", [128, NL, NPRM], F32)
    ONES = A.alloc("ones", [128, 128], BF16)
    PERM = A.alloc("perm", [128, 128], BF16)
    ESK = A.alloc("esk", [128, 4], F32)
    JI = A.alloc("ji", [128, SCH], F32)
    IDN = A.alloc("idn", [128, 128], F32)
    ONESF = A.alloc("onesf", [128, 128], F32)

    psc = [0]

    def nb(n=1):
        i = psc[0] % 8
        psc[0] += 1
        return i

    wsc = [0]

    def wload(views, rshape):
        s = wsc[0] % W_SLOTS
        wsc[0] += 1

        def fn(e, s=s, views=views):
            out = []
            for (c0, a, b, src) in views:
                dst = WS[s][:, c0:c0 + a * b].rearrange("p (a b) -> p a b", a=a)
                for ai in range(a):
                    out.append(e.dma_start(out=dst[:, ai, :], in_=src[:, ai, :]))
            return out

        S.op("pool", fn, writes=[("w", s)], dma=True, chan=f"w{s}", ndma=sum(v[1] for v in views))
        return s

    def wview(s, a, b, c0=0):
        return WS[s][:, c0:c0 + a * b].rearrange("p (a b) -> p a b", a=a)

    def tbs(tb):
        return slice(tb * TBW, (tb + 1) * TBW)

    for kt in range(NKT):
        S.op("sp", lambda e, kt=kt: e.dma_start(out=xT[:, kt, :], in_=xT_d[kt * 128:(kt + 1) * 128, :]),
             writes=[("x", kt, tb) for tb in range(NTB)], dma=True, chan=f"x{kt}")
    S.op("sp", lambda e: e.dma_start(out=PRM[:], in_=prm_d.rearrange("l p c -> p l c")), writes=["prm"], dma=True,
         chan="prm")
    S.op("sp", lambda e: e.dma_start(out=JI[:], in_=cst_d[:, 1152 + 2 * SEQ:1152 + 2 * SEQ + SCH]), writes=["ji"], dma=True,
         chan="ji")
    S.op("sp", lambda e: e.dma_start(out=IDN[:], in_=cst_d[:, CSTW - 128:CSTW]), writes=["idn"], dma=True, chan="idn")
    S.op("dve", lambda e: e.memset(ONESF[:], 1.0), writes=["onesf"])
    S.op("dve", lambda e: e.memset(ONES[:], 1.0), writes=["ones"])
    S.op("pool", lambda e: e.dma_start(out=PERM[:], in_=cst_d[:, 1024:1152]), writes=["perm"], dma=True, chan="perm")

    def rmsnorm_to_h(l, gcol, name):
        A.push()
        SQ = [A.alloc("sq", [128, TBW], BF16) for _ in range(3)]
        MS = [A.alloc("ms", [128, TBW], F32) for _ in range(2)]
        for tb in range(NTB):
            pi = nb()
            for kt in range(NKT):
                q = (tb * NKT + kt) % 3
                S.op("act", lambda e, q=q, kt=kt, tb=tb: e.activation(out=SQ[q][:], in_=xT[:, kt, tbs(tb)], func=AF.Square),
                     reads=[("x", kt, tb)], writes=[("sq", q)])
                S.op("pe", lambda e, q=q, kt=kt, pi=pi: e.matmul(PS[:, pi, :], lhsT=ONES[:], rhs=SQ[q][:], start=(kt == 0),
                                                                stop=(kt == NKT - 1)),
                     reads=[("sq", q), "ones"], writes=[("ps", pi)])
            m = tb % 2
            S.op("dve", lambda e, m=m, pi=pi: e.tensor_scalar(out=MS[m][:], in0=PS[:, pi, :], scalar1=1.0 / D, scalar2=EPS,
                                                             op0=ALU.mult, op1=ALU.add),
                 reads=[("ps", pi)], writes=[("ms", m)])
            S.op("act", lambda e, m=m: e.activation(out=MS[m][:], in_=MS[m][:], func=AF.Sqrt), reads=[("ms", m)],
                 writes=[("ms", m)])
            S.op("dve", lambda e, m=m: e.reciprocal(out=MS[m][:], in_=MS[m][:]), reads=[("ms", m)], writes=[("ms", m)])
            for kt in range(NKT):
                S.op("dve", lambda e, m=m, kt=kt, tb=tb: e.scalar_tensor_tensor(
                    out=hT[:, kt, tbs(tb)], in0=xT[:, kt, tbs(tb)], scalar=PRM[:, l, gcol + kt:gcol + kt + 1], in1=MS[m][:],
                    op0=ALU.mult, op1=ALU.mult),
                     reads=[("x", kt, tb), ("ms", m), "prm"], writes=[("h", kt, tb)])
        S.barrier()
        A.pop()

    def proj_group(pi, s, c0, tb, ncols_slot=512):
        wv = wview(s, NKT, ncols_slot)

        def fn(e):
            ins = None
            for kt in range(NKT):
                ins = e.matmul(PS[:, pi, :], lhsT=wv[:, kt, c0:c0 + 128], rhs=hT[:, kt, tbs(tb)], start=(kt == 0),
                               stop=(kt == NKT - 1))
            return ins

        S.op("pe", fn, reads=[("w", s)] + [("h", kt, tb) for kt in range(NKT)], writes=[("ps", pi)])

    def win_view(l, c0, ncols):
        return w_in_d[l].rearrange("(kt p) n -> p kt n", p=128)[:, :, c0:c0 + ncols]

    def merge_branch(l, b, wo_d):
        A.push()
        SG = [A.alloc("sg", [128, TBW], F32) for _ in range(2)]
        TMP = [A.alloc("tmp", [128, TBW], F32) for _ in range(2)]
        so = wload([(0, 4, 1024, wo_d[l].rearrange("(kt p) n -> p kt n", p=128))], None)
        wo = wview(so, 4, 1024)
        for half in range(2):
            sg_ = wload([(0, NKT, 512, win_view(l, 2816 + b * 1024 + half * 512, 512))], None)
            for fl in range(4):
                f = half * 4 + fl
                for tb in range(NTB):
                    py = nb()

                    def fy(e, py=py, f=f, tb=tb):
                        ins = None
                        for kt in range(4):
                            ins = e.matmul(PS[:, py, :], lhsT=wo[:, kt, f * 128:(f + 1) * 128], rhs=BR[:, kt, tbs(tb)],
                                           start=(kt == 0), stop=(kt == 3))
                        return ins

                    S.op("pe", fy, reads=[("w", so)] + [("br", kt, tb) for kt in range(4)], writes=[("ps", py)])
                    pg = nb()
                    proj_group(pg, sg_, fl * 128, tb)
                    q = (f * NTB + tb) % 2
                    S.op("act", lambda e, q=q, pg=pg, f=f: e.activation(out=SG[q][:], in_=PS[:, pg, :], func=AF.Sigmoid,
                                                                         bias=PRM[:, l, P_BG + b * 8 + f:P_BG + b * 8 + f + 1]),
                         reads=[("ps", pg), "prm"], writes=[("sg", q)])
                    if b == 0:
                        S.op("dve", lambda e, q=q, py=py, f=f, tb=tb: e.tensor_tensor(out=MRG[:, f, tbs(tb)], in0=PS[:, py, :],
                                                                                      in1=SG[q][:], op=ALU.mult),
                             reads=[("ps", py), ("sg", q)], writes=[("mrg", f, tb)])
                    else:
                        S.op("dve", lambda e, q=q, py=py: e.tensor_tensor(out=TMP[q][:], in0=PS[:, py, :], in1=SG[q][:],
                                                                          op=ALU.mult),
                             reads=[("ps", py), ("sg", q)], writes=[("tmp", q)])
                        S.op("dve", lambda e, q=q, f=f, tb=tb: e.tensor_tensor(out=MRG[:, f, tbs(tb)], in0=MRG[:, f, tbs(tb)],
                                                                               in1=TMP[q][:], op=ALU.add),
                             reads=[("tmp", q), ("mrg", f, tb)], writes=[("mrg", f, tb)])
        S.barrier()
        A.pop()

    def layer(l):
        S.enabled = DBG >= 1
        rmsnorm_to_h(l, P_GMIX, "n1")

        A.push()

        S.enabled = DBG >= 2
        A.push()
        Q = nc.alloc_sbuf_tensor_at(f"q_l{l}", [128, 4, SEQ], BF16, offset=mrg_off)
        Kt = A.alloc("k", [128, SEQ], BF16)
        V = A.alloc("v", [128, 16, 128], BF16)
        A.push()
        COS = [A.alloc("cos", [128, TBW], BF16) for _ in range(2)]
        SIN = [A.alloc("sin", [128, TBW], BF16) for _ in range(2)]
        QR = [A.alloc("qr", [128, TBW], BF16) for _ in range(2)]
        T1 = A.alloc("t1", [128, TBW], F32)
        T2 = A.alloc("t2", [128, TBW], F32)
        sq_ = wload([(0, NKT, 512, win_view(l, 0, 512))], None)
        skv = wload([(0, NKT, 256, win_view(l, 512, 256))], None)

        def rope_block(pi, tb, dst_ap, dst_key, cnt):
            q = cnt % 2
            cq = tb % 2
            if DBG < 2.1:
                return
            S.op("dve", lambda e: e.tensor_copy(out=QR[q][:], in_=PS[:, pi, :]), reads=[("ps", pi)],
                 writes=[("qr", q)])
            p2 = nb()
            S.op("pe", lambda e: e.matmul(PS[:, p2, :], lhsT=PERM[:], rhs=QR[q][:], start=True, stop=True),
                 reads=["perm", ("qr", q)], writes=[("ps", p2)])
            S.op("dve", lambda e: e.tensor_tensor(out=T1[:], in0=PS[:, pi, :], in1=COS[cq][:], op=ALU.mult),
                 reads=[("ps", pi), ("cos", cq)], writes=["t1"])
            S.op("dve", lambda e: e.tensor_tensor(out=T2[:], in0=PS[:, p2, :], in1=SIN[cq][:], op=ALU.mult),
                 reads=[("ps", p2), ("sin", cq)], writes=["t2"])
            S.op("dve", lambda e: e.tensor_tensor(out=dst_ap, in0=T1[:], in1=T2[:], op=ALU.add), reads=["t1", "t2"],
                 writes=[dst_key])

        cnt = 0
        for tb in range(NTB):
            cq = tb % 2
            S.op("pool", lambda e, tb=tb, cq=cq: e.dma_start(out=COS[cq][:], in_=cst_d[:, 1152 + tb * TBW:1152 + (tb + 1) * TBW]),
                 writes=[("cos", cq)], dma=True, chan=f"cos{cq}")
            S.op("pool", lambda e, tb=tb, cq=cq: e.dma_start(out=SIN[cq][:], in_=cst_d[:, 1152 + SEQ + tb * TBW:1152 + SEQ + (tb + 1) * TBW]),
                 writes=[("sin", cq)], dma=True, chan=f"sin{cq}")
            for g in range(4):
                pi = nb()
                proj_group(pi, sq_, g * 128, tb)
                rope_block(pi, tb, Q[:, g, tbs(tb)], ("q", g, tb), cnt)
                cnt += 1
            pi = nb()
            proj_group(pi, skv, 0, tb, ncols_slot=256)
            rope_block(pi, tb, Kt[:, tbs(tb)], ("k", tb), cnt)
            cnt += 1
        S.enabled = DBG >= 2.2
        kvv = wview(skv, NKT, 256)
        for t4 in range(4):
            pi = nb()

            def fv(e, pi=pi, t4=t4):
                ins = None
                for j in range(4):
                    tt = t4 * 4 + j
                    for kt in range(NKT):
                        ins = e.matmul(PS[:, pi, j * 128:(j + 1) * 128], lhsT=hT[:, kt, tt * 128:(tt + 1) * 128],
                                       rhs=kvv[:, kt, 128:256], start=(kt == 0), stop=(kt == NKT - 1))
                return ins

            S.op("pe", fv, reads=[("w", skv)] + [("h", kt, t4) for kt in range(NKT)], writes=[("ps", pi)])
            S.op("dve", lambda e, pi=pi, t4=t4: e.tensor_copy(out=V[:, t4 * 4:(t4 + 1) * 4, :],
                                                               in_=PS[:, pi, :].rearrange("p (a b) -> p a b", a=4)),
                 reads=[("ps", pi)], writes=[("v", t4)])
        S.barrier()
        A.pop()
        S.enabled = DBG >= 2.5
        MK = A.alloc("mk", [128, 2, 512], BF16)
        IDB = A.alloc("idb", [128, 128], BF16)
        PB = [A.alloc("pb", [128, TBW], BF16) for _ in range(8)]
        DEN = [A.alloc("den", [128, TBW], F32) for _ in range(1)]
        S.op("pool", lambda e: e.dma_start(out=MK[:], in_=cst_d[:, 0:1024].rearrange("p (a b) -> p a b", a=2)),
             writes=["mk"], dma=True, chan="mk")
        S.op("dve", lambda e: e.tensor_copy(out=IDB[:], in_=IDN[:]), reads=["idn"], writes=["idb"])
        S.op("act", lambda e: e.activation(out=ESK[:], in_=PRM[:, l, P_SINK:P_SINK + 4], func=AF.Exp), reads=["prm"],
             writes=["esk"])
        pcnt = [0]

        def att_s(qb):
            qs = slice(qb * 128, (qb + 1) * 128)
            pbs = {}
            kts = [qb] if qb == 0 else [qb - 1, qb]
            k_ = 0
            for kvh in range(2):
                hs = slice(kvh * 64, (kvh + 1) * 64)
                for kt_ in kts:
                    pa = (k_ if qb else 2 * k_) % 4
                    k_ += 1
                    ks = slice(kt_ * 128, (kt_ + 1) * 128)
                    mi = 0 if kt_ == qb else 1

                    def fs(e, pa=pa, hs=hs, ks=ks, qs=qs, kvh=kvh, mi=mi):
                        e.matmul(PS[:, pa, :], lhsT=IDB[:], rhs=MK[:, mi, :], start=True, stop=False)
                        return e.matmul(PS[:, pa, :].rearrange("p (a b) -> p a b", a=4), lhsT=Kt[hs, ks], rhs=Q[hs, :, qs],
                                        start=False, stop=True, tile_position=(kvh * 64, 0))

                    S.op("pe", fs, reads=[("k", kt_ // 4), "idb", "mk"] + [("q", g, qb // 4) for g in range(4)],
                         writes=[("ps", pa)])
                    pq = pcnt[0] % 8
                    pcnt[0] += 1
                    S.op("act", lambda e, pa=pa, pq=pq: e.activation(out=PB[pq][:], in_=PS[:, pa, :], func=AF.Exp, scale=0.125),
                         reads=[("ps", pa)], writes=[("pb", pq)])
                    pbs[(kvh, kt_)] = pq
            return pbs

        def att_pv(qb, pbs):
            qs = slice(qb * 128, (qb + 1) * 128)
            po = 4 + 2 * (qb % 2)
            pd = po + 1
            kts = [qb] if qb == 0 else [qb - 1, qb]
            n_ = len(kts)
            for kvh in range(2):
                hs = slice(kvh * 64, (kvh + 1) * 64)
                for i_, kt_ in enumerate(kts):
                    pq = pbs[(kvh, kt_)]
                    S.op("pe", lambda e, pq=pq, hs=hs, kt_=kt_, i_=i_, n_=n_, kvh=kvh, po=po: e.matmul(
                        PS[hs, po, :], lhsT=V[:, kt_, hs], rhs=PB[pq][:], start=(i_ == 0), stop=(i_ == n_ - 1),
                        tile_position=(0, kvh * 64)),
                         reads=[("pb", pq), ("v", kt_ // 4)], writes=[("ps", po)])
                    S.op("pe", lambda e, pq=pq, hs=hs, i_=i_, n_=n_, kvh=kvh, pd=pd: e.matmul(
                        PS[hs, pd, :], lhsT=ONES[:, 0:64], rhs=PB[pq][:], start=(i_ == 0), stop=(i_ == n_ - 1),
                        tile_position=(0, kvh * 64)),
                         reads=[("pb", pq), "ones"], writes=[("ps", pd)])
            dq = 0
            S.op("dve", lambda e, dq=dq, pd=pd: e.tensor_tensor(
                out=DEN[dq][:].rearrange("p (a b) -> p a b", a=4), in0=PS[:, pd, :].rearrange("p (a b) -> p a b", a=4),
                in1=ESK[:].unsqueeze(2).to_broadcast([128, 4, 128]), op=ALU.add),
                 reads=[("ps", pd), "esk"], writes=[("den", dq)])
            S.op("dve", lambda e, dq=dq: e.reciprocal(out=DEN[dq][:], in_=DEN[dq][:]), reads=[("den", dq)],
                 writes=[("den", dq)])
            S.op("dve", lambda e, dq=dq, po=po, qs=qs: e.tensor_tensor(
                out=BR[:, :, qs], in0=PS[:, po, :].rearrange("p (a b) -> p a b", a=4),
                in1=DEN[dq][:].rearrange("p (a b) -> p a b", a=4), op=ALU.mult),
                 reads=[("ps", po), ("den", dq)], writes=[("br", g, qb // 4) for g in range(4)])

        prev_pbs = att_s(0)
        for qb in range(16):
            nxt = att_s(qb + 1) if qb + 1 < 16 else None
            att_pv(qb, prev_pbs)
            prev_pbs = nxt
        S.barrier()
        A.pop()
        S.enabled = DBG >= 2.8
        merge_branch(l, 0, w_ao_d)

        S.enabled = DBG >= 3
        A.push()
        Z = A.alloc("z", [128, SEQ + 2], F32)
        Y1 = [A.alloc("y1", [128, TBW], F32) for _ in range(2)]
        S.op("dve", lambda e: e.memset(Z[:, 0:2], 0.0), writes=["z0"])
        scc = wload([(0, NKT, 512, win_view(l, 1280, 512))], None)
        for f in range(4):
            for tb in range(NTB):
                pi = nb()
                proj_group(pi, scc, f * 128, tb)
                S.op("dve", lambda e, pi=pi, f=f, tb=tb: e.tensor_copy(out=BR[:, f, tbs(tb)], in_=PS[:, pi, :]),
                     reads=[("ps", pi)], writes=[("br", f, tb)])
        scx = wload([(0, NKT, 512, win_view(l, 1792, 512))], None)
        for f in range(4):
            for tb in range(NTB):
                pi = nb()
                proj_group(pi, scx, f * 128, tb)
                a0 = tb * TBW
                S.op("dve", lambda e, pi=pi, f=f, tb=tb, a0=a0: e.tensor_tensor(out=Z[:, 2 + a0:2 + a0 + TBW], in0=PS[:, pi, :],
                                                                               in1=BR[:, f, tbs(tb)], op=ALU.mult),
                     reads=[("ps", pi), ("br", f, tb)], writes=[("z", tb)])
                yq = tb % 2
                cw = lambda j, f=f: PRM[:, l, P_CONVW + j * 4 + f:P_CONVW + j * 4 + f + 1]

                def fconv(e, a0=a0, yq=yq, f=f, tb=tb, cw=cw):
                    e.tensor_scalar(out=Y1[yq][:], in0=Z[:, 2 + a0:2 + a0 + TBW], scalar1=cw(2), scalar2=None, op0=ALU.mult)
                    e.scalar_tensor_tensor(out=Y1[yq][:], in0=Z[:, 1 + a0:1 + a0 + TBW], scalar=cw(1), in1=Y1[yq][:],
                                           op0=ALU.mult, op1=ALU.add)
                    return e.scalar_tensor_tensor(out=BR[:, f, tbs(tb)], in0=Z[:, a0:a0 + TBW], scalar=cw(0), in1=Y1[yq][:],
                                                  op0=ALU.mult, op1=ALU.add)

                S.op("dve", fconv, reads=[("z", tb), ("z", tb - 1), "z0", "prm"], writes=[("y1", yq), ("br", f, tb)])
        scb = wload([(0, NKT, 512, win_view(l, 768, 512))], None)
        for f in range(4):
            for tb in range(NTB):
                pi = nb()
                proj_group(pi, scb, f * 128, tb)
                S.op("dve", lambda e, pi=pi, f=f, tb=tb: e.tensor_tensor(out=BR[:, f, tbs(tb)], in0=PS[:, pi, :],
                                                                        in1=BR[:, f, tbs(tb)], op=ALU.mult),
                     reads=[("ps", pi), ("br", f, tb)], writes=[("br", f, tb)])
        S.barrier()
        A.pop()
        merge_branch(l, 1, w_co_d)

        S.enabled = DBG >= 4
        A.push()
        WBU = [A.alloc("wbu", [128, 2048], BF16) for _ in range(2)]
        CW = A.alloc("cw", [128, 16, 2, 64], BF16)
        DIAGD = A.alloc("diagd", [128, 4, 128], BF16)
        L1 = A.alloc("l1", [128, 2, 16], F32)
        L2 = A.alloc("l2", [128, 2, 16], F32)
        XC = A.alloc("xc", [128, 2, 16], F32)
        su = wload([(0, NKT, 512, win_view(l, 2304, 512))], None)
        for ct in range(4):
            for tb in range(NTB):
                pi = nb()
                proj_group(pi, su, ct * 128, tb)
                S.op("dve", lambda e, pi=pi, ct=ct, tb=tb: e.tensor_copy(out=BR[:, ct, tbs(tb)], in_=PS[:, pi, :]),
                     reads=[("ps", pi)], writes=[("br", ct, tb)])

        S.barrier()
        def coeffs(eng_name, are, aim, ldt, shape, tmps, key):
            dt_, lr, li, t3, t4, qr, qi, t7 = tmps[:8]
            rd = [key + "_in"]
            wr = [key]
            PI = math.pi
            S.op("act", lambda e: e.activation(out=dt_, in_=ldt, func=AF.Exp), reads=rd, writes=wr)

            ti = tmps[8]

            def red(e, x):
                e.tensor_scalar(out=dt_, in0=x, scalar1=1.0 / (2 * PI), scalar2=0.5, op0=ALU.mult, op1=ALU.add)
                e.tensor_copy(out=ti, in_=dt_)
                e.tensor_copy(out=dt_, in_=ti)
                e.scalar_tensor_tensor(out=x, in0=dt_, scalar=-2 * PI, in1=x, op0=ALU.mult, op1=ALU.add)
                e.tensor_scalar(out=dt_, in0=x, scalar1=-PI, scalar2=2 * PI, op0=ALU.is_lt, op1=ALU.mult)
                e.tensor_tensor(out=x, in0=x, in1=dt_, op=ALU.add)
                e.tensor_scalar(out=dt_, in0=x, scalar1=PI, scalar2=-2 * PI, op0=ALU.is_gt, op1=ALU.mult)
                return e.tensor_tensor(out=x, in0=x, in1=dt_, op=ALU.add)

            def f1(e):
                e.tensor_tensor(out=t3, in0=are, in1=dt_, op=ALU.mult)
                e.tensor_tensor(out=t4, in0=aim, in1=dt_, op=ALU.mult)
                e.tensor_scalar(out=t7, in0=t4, scalar1=0.5 * PI, scalar2=None, op0=ALU.add)
                red(e, t7)
                return red(e, t4)

            S.op(eng_name, f1, reads=wr, writes=wr)

            def f3(e):
                e.activation(out=t3, in_=t3, func=AF.Exp)
                e.activation(out=t7, in_=t7, func=AF.Sin)
                return e.activation(out=t4, in_=t4, func=AF.Sin)

            S.op("act", f3, reads=wr, writes=wr)

            def f4(e):
                e.tensor_tensor(out=lr, in0=t3, in1=t7, op=ALU.mult)
                e.tensor_tensor(out=li, in0=t3, in1=t4, op=ALU.mult)
                e.tensor_scalar(out=t3, in0=lr, scalar1=-1.0, scalar2=None, op0=ALU.add)
                e.tensor_tensor(out=t4, in0=are, in1=are, op=ALU.mult)
                e.tensor_tensor(out=t7, in0=aim, in1=aim, op=ALU.mult)
                e.tensor_tensor(out=t4, in0=t4, in1=t7, op=ALU.add)
                e.reciprocal(out=t4, in_=t4)
                e.tensor_tensor(out=qr, in0=t3, in1=are, op=ALU.mult)
                e.tensor_tensor(out=t7, in0=li, in1=aim, op=ALU.mult)
                e.tensor_tensor(out=qr, in0=qr, in1=t7, op=ALU.add)
                e.tensor_tensor(out=qr, in0=qr, in1=t4, op=ALU.mult)
                e.tensor_tensor(out=qi, in0=li, in1=are, op=ALU.mult)
                e.tensor_tensor(out=t7, in0=t3, in1=aim, op=ALU.mult)
                e.tensor_tensor(out=qi, in0=qi, in1=t7, op=ALU.subtract)
                return e.tensor_tensor(out=qi, in0=qi, in1=t4, op=ALU.mult)

            S.op(eng_name, f4, reads=wr, writes=wr)
            return lr, li, qr, qi

        A.push()
        TA = [A.alloc("ta", [128, 16], F32) for _ in range(8)] + [A.alloc("tai", [128, 16], mybir.dt.int32)]
        lrA, liA, qrA, qiA = coeffs("dve", PRM[:, l, P_AA:P_AA + 16], PRM[:, l, P_AI:P_AI + 16], PRM[:, l, P_LDT:P_LDT + 16],
                                [128, 16], [t[:] for t in TA], "cfA")

        def fL(e):
            e.tensor_copy(out=L1[:, 0, :], in_=lrA)
            e.tensor_copy(out=L1[:, 1, :], in_=lrA)
            e.tensor_scalar(out=L2[:, 0, :], in0=liA, scalar1=-1.0, scalar2=None, op0=ALU.mult)
            e.tensor_copy(out=L2[:, 1, :], in_=liA)
            return e.memset(XC[:], 0.0)

        S.op("dve", fL, reads=["cfA"], writes=["L", "xc"])
        wbase = A.offs["ws0"]
        U4 = 16 * SCH * 4
        mkb = lambda nm, off, dt=F32, shape=None: nc.alloc_sbuf_tensor_at(f"{nm}_l{l}", shape or [128, 16, SCH], dt,
                                                                          offset=wbase + off)
        CJ = mkb("cj", 0, BF16)
        SJ = mkb("sj", U4 // 2, BF16)
        DEC = mkb("dec", U4)
        T1s = mkb("t1s", 2 * U4)
        T2s = mkb("t2s", 3 * U4)
        T3s = mkb("t3s", 4 * U4)
        XB = mkb("xb", 5 * U4, BF16, [128, 2, 16, SCH])
        TIs = mkb("tis", 5 * U4, mybir.dt.int32)
        PHI = A.alloc("phi", [128, 16], F32)
        RR = A.alloc("rr", [128, 16], F32)
        S.op("act", lambda e: e.activation(out=PHI[:], in_=PRM[:, l, P_LDT:P_LDT + 16], func=AF.Exp), reads=["prm"],
             writes=["tabp"])

        def ft1(e):
            e.tensor_tensor(out=RR[:], in0=PRM[:, l, P_AA:P_AA + 16], in1=PHI[:], op=ALU.mult)
            return e.tensor_tensor(out=PHI[:], in0=PRM[:, l, P_AI:P_AI + 16], in1=PHI[:], op=ALU.mult)

        S.op("dve", ft1, reads=["tabp", "prm"], writes=["tabp"])
        S.op("act", lambda e: e.activation(out=RR[:], in_=RR[:], func=AF.Exp), reads=["tabp"], writes=["tabp"])

        def ft2(e):
            e.tensor_tensor(out=T2s[:], in0=PHI[:].unsqueeze(2).to_broadcast([128, 16, SCH]),
                            in1=JI[:].unsqueeze(1).to_broadcast([128, 16, SCH]), op=ALU.mult)
            e.tensor_scalar(out=T3s[:], in0=T2s[:], scalar1=0.5 * math.pi, scalar2=None, op0=ALU.add)
            red_angle(e, T2s[:], T1s[:], TIs[:])
            red_angle(e, T3s[:], T1s[:], TIs[:])
            e.tensor_copy(out=DEC[:], in_=RR[:].unsqueeze(2).to_broadcast([128, 16, SCH]))
            return e.memset(DEC[:, :, 0:1], 0.0)

        S.op("dve", ft2, reads=["tabp", "ji"], writes=["tab"])

        def ft3(e):
            e.activation(out=SJ[:], in_=T2s[:], func=AF.Sin)
            return e.activation(out=CJ[:], in_=T3s[:], func=AF.Sin)

        S.op("act", ft3, reads=["tab"], writes=["tab"])
        def fCd(e):
            return [e.dma_start(out=CW[:, :, 0, :], in_=ssmC_d[l][:, 0, :].rearrange("p (a b) -> p a b", a=16)),
                    e.dma_start(out=CW[:, :, 1, :], in_=ssmC_d[l][:, 1, :].rearrange("p (a b) -> p a b", a=16))]

        S.op("pool", fCd, writes=["cw"], dma=True, chan="cw", ndma=2)
        S.op("pool", lambda e: e.tensor_scalar(out=CW[:, :, 1, :], in0=CW[:, :, 1, :], scalar1=-1.0, scalar2=None,
                                               op0=ALU.mult), reads=["cw"], writes=["cw"])
        def fdd(e):
            ins = None
            for ct in range(4):
                ins = e.tensor_scalar(out=DIAGD[:, ct, :], in0=IDN[:], scalar1=PRM[:, l, P_SSMD + ct:P_SSMD + ct + 1],
                                      scalar2=None, op0=ALU.mult)
            return ins

        S.op("dve", fdd, reads=["idn", "prm"], writes=["diagd"])
        DG = A.alloc("dg", [128, 4, 128], F32)
        BB = A.alloc("bb", [128, 2, 512], F32)
        TQ = [A.alloc("tq", [128, 512], F32) for _ in range(2)]
        for qt in range(4):
            S.op("sp", lambda e, qt=qt: e.dma_start(out=BB[:], in_=ssmB_d[l][:, :, qt * 512:(qt + 1) * 512]),
                 writes=["bb"], dma=True, chan="bb")
            for qi_, qsrc in enumerate((qrA, qiA)):
                def fdg(e, qsrc=qsrc, qt=qt):
                    ins = None
                    for j in range(4):
                        ins = e.tensor_scalar(out=DG[:, j, :], in0=IDN[:], scalar1=qsrc[:, qt * 4 + j:qt * 4 + j + 1],
                                              scalar2=None, op0=ALU.mult)
                    return ins

                S.op("dve", fdg, reads=["cfA", "idn"], writes=["dg"])

                def fqb(e, qi_=qi_):
                    ins = None
                    for j in range(4):
                        ins = e.matmul(PS[:, qi_, j * 128:(j + 1) * 128], lhsT=ONESF[:], rhs=DG[:, j, :], start=True, stop=True)
                    return ins

                S.op("pe", fqb, reads=["dg", "onesf"], writes=[("ps", qi_)])

            def fB(e, qt=qt):
                hs_ = slice(qt * 512, (qt + 1) * 512)
                QBr = PS[:, 0, :]
                QBi = PS[:, 1, :]
                e.tensor_tensor(out=TQ[0][:], in0=QBr, in1=BB[:, 0, :], op=ALU.mult)
                e.tensor_tensor(out=TQ[1][:], in0=QBi, in1=BB[:, 1, :], op=ALU.mult)
                e.tensor_tensor(out=WBU[0][:, hs_], in0=TQ[0][:], in1=TQ[1][:], op=ALU.subtract)
                e.tensor_tensor(out=TQ[0][:], in0=QBr, in1=BB[:, 1, :], op=ALU.mult)
                e.tensor_tensor(out=TQ[1][:], in0=QBi, in1=BB[:, 0, :], op=ALU.mult)
                return e.tensor_tensor(out=WBU[1][:, hs_], in0=TQ[0][:], in1=TQ[1][:], op=ALU.add)

            S.op("dve", fB, reads=[("ps", 0), ("ps", 1), "bb"], writes=[("wbu", k) for k in range(8)] + ["tq"])
        S.barrier()
        A.pop()

        A.push()
        XS2 = [A.alloc("xs", [128, 2, 16, SCH], F32) for _ in range(2)]
        TM1 = A.alloc("tm1", [128, 2, 16], F32)
        TM2 = A.alloc("tm2", [128, 2, 16], F32)
        G1 = A.alloc("g1", [128, 4, SCH], F32)
        flat2 = lambda ap: ap.rearrange("p a b -> p (a b)")
        PSBU = [("ps", 4), ("ps", 5), ("ps", 6), ("ps", 7)]

        def stage_a1(c):
            cs_ = slice(c * SCH, (c + 1) * SCH)
            tbk = (c * SCH) // TBW
            xk = c % 2
            XS = XS2[xk]

            def fbu(e, cs_=cs_):
                ins = None
                for pr in range(16):
                    ct = pr // 4
                    for ri in range(2):
                        o0 = (ri * 16 + pr) * SCH
                        bank = 4 + o0 // 512
                        oo = o0 % 512
                        ins = e.matmul(PS[:, bank, oo:oo + SCH], lhsT=WBU[ri][:, pr * 128:(pr + 1) * 128], rhs=BR[:, ct, cs_],
                                       start=True, stop=True)
                return ins

            S.op("pe", fbu, reads=[("wbu", ct) for ct in range(8)] + [("br", ct, tbk) for ct in range(4)], writes=PSBU)
            S.op("act", lambda e: e.activation(out=XS[:].rearrange("p a b c -> p (a b c)"),
                                               in_=PS[:, 4:8, :].rearrange("p a b -> p (a b)"), func=AF.Identity),
                 reads=PSBU, writes=[("xs_re", xk), ("xs_im", xk)])

        def stage_a2(c):
            xk = c % 2
            XS = XS2[xk]
            BRe = XS[:, 0, :, :]
            BIm = XS[:, 1, :, :]
            kre, kim = ("xs_re", xk), ("xs_im", xk)

            def fscan(e):
                e.tensor_tensor(out=T1s[:], in0=BRe, in1=CJ[:], op=ALU.mult)
                e.tensor_tensor(out=T2s[:], in0=BIm, in1=SJ[:], op=ALU.mult)
                e.tensor_tensor(out=T1s[:], in0=T1s[:], in1=T2s[:], op=ALU.add)
                e.tensor_tensor(out=T2s[:], in0=BRe, in1=SJ[:], op=ALU.mult)
                e.tensor_tensor(out=BIm, in0=BIm, in1=CJ[:], op=ALU.mult)
                e.tensor_tensor(out=BIm, in0=BIm, in1=T2s[:], op=ALU.subtract)
                e.tensor_tensor(out=TM1[:], in0=XC[:], in1=L1[:], op=ALU.mult)
                e.tensor_tensor(out=TM2[:], in0=XC[:, ::-1, :], in1=L2[:], op=ALU.mult)
                e.tensor_tensor(out=TM1[:], in0=TM1[:], in1=TM2[:], op=ALU.add)
                e.tensor_tensor(out=T1s[:, :, 0], in0=T1s[:, :, 0], in1=TM1[:, 0, :], op=ALU.add)
                e.tensor_tensor(out=BIm[:, :, 0], in0=BIm[:, :, 0], in1=TM1[:, 1, :], op=ALU.add)
                e.tensor_tensor_scan(out=flat2(BRe), data0=flat2(DEC[:]), data1=flat2(T1s[:]), initial=0.0, op0=ALU.mult,
                                     op1=ALU.add)
                e.tensor_tensor_scan(out=flat2(T2s[:]), data0=flat2(DEC[:]), data1=flat2(BIm), initial=0.0, op0=ALU.mult,
                                     op1=ALU.add)
                e.tensor_tensor(out=T1s[:], in0=T2s[:], in1=SJ[:], op=ALU.mult)
                e.tensor_tensor(out=BIm, in0=BRe, in1=CJ[:], op=ALU.mult)
                e.tensor_tensor(out=XB[:, 0, :, :], in0=BIm, in1=T1s[:], op=ALU.subtract)
                e.tensor_tensor(out=XC[:, 0, :], in0=BIm[:, :, SCH - 1], in1=T1s[:, :, SCH - 1], op=ALU.subtract)
                e.tensor_tensor(out=T1s[:], in0=BRe, in1=SJ[:], op=ALU.mult)
                e.tensor_tensor(out=BIm, in0=T2s[:], in1=CJ[:], op=ALU.mult)
                e.tensor_tensor(out=XC[:, 1, :], in0=BIm[:, :, SCH - 1], in1=T1s[:, :, SCH - 1], op=ALU.add)
                return e.tensor_tensor(out=XB[:, 1, :, :], in0=BIm, in1=T1s[:], op=ALU.add)

            S.op("dve", fscan, reads=[kre, kim, "L", "xc", "tab"], writes=[kre, kim, "t1", "t2", "xc", "xb"])
            py = nb() % 4

            cs_ = slice(c * SCH, (c + 1) * SCH)
            tbk = (c * SCH) // TBW

            def fcm(e, py=py, cs_=cs_):
                ins = None
                for ct in range(4):
                    e.matmul(PS[:, py, ct * SCH:(ct + 1) * SCH], lhsT=DIAGD[:, ct, :], rhs=BR[:, ct, cs_], start=True, stop=False)
                    for half in range(2):
                        k = 0
                        for pl in range(2):
                            pr = ct * 4 + half * 2 + pl
                            for ri in range(2):
                                ins = e.matmul(PS[half * 64:(half + 1) * 64, py, ct * SCH:(ct + 1) * SCH],
                                               lhsT=CW[:, pr, ri, :], rhs=XB[:, ri, pr, :], start=False, stop=(k == 3),
                                               tile_position=(0, half * 64))
                                k += 1
                return ins

            S.op("pe", fcm, reads=["xb", "cw", "diagd"] + [("br", ct, tbk) for ct in range(4)], writes=[("ps", py)])
            return py

        def stage_b(c, py):
            cs_ = slice(c * SCH, (c + 1) * SCH)
            tbk = (c * SCH) // TBW

            PY = PS[:, py, 0:4 * SCH].rearrange("p (a b) -> p a b", a=4)
            S.op("act", lambda e, PY=PY: e.activation(out=G1[:], in_=PY, func=AF.Square), reads=[("ps", py)], writes=["g1"])

            def fys(e, PY=PY):
                e.tensor_scalar(out=G1[:], in0=G1[:], scalar1=0.044715, scalar2=1.0, op0=ALU.mult, op1=ALU.add)
                return e.tensor_tensor(out=G1[:], in0=G1[:], in1=PY, op=ALU.mult)

            S.op("dve", fys, reads=[("ps", py), "g1"], writes=["g1"])
            S.op("act", lambda e: e.activation(out=G1[:], in_=G1[:], func=AF.Sigmoid, scale=2.0 * math.sqrt(2.0 / math.pi)),
                 reads=["g1"], writes=["g1"])
            S.op("dve", lambda e, cs_=cs_, PY=PY: e.tensor_tensor(out=BR[:, :, cs_], in0=PY, in1=G1[:], op=ALU.mult),
                 reads=[("ps", py), "g1"], writes=[("yso", c)])

        pys = {}
        stage_a1(0)
        for c in range(NCH + 1):
            if c + 1 < NCH:
                stage_a1(c + 1)
            if c < NCH:
                pys[c] = stage_a2(c)
            if c >= 1:
                stage_b(c - 1, pys[c - 1])
        S.barrier()
        A.pop()
        SG = [A.alloc("sg", [128, TBW], F32) for _ in range(2)]
        sgl = wload([(0, 4, 512, w_glu_d[l].rearrange("(kt p) n -> p kt n", p=128))], None)
        wg = wview(sgl, 4, 512)
        for tb in range(NTB):
            pis = []
            for f in range(4):
                pi = nb()
                pis.append(pi)

                def fg(e, pi=pi, f=f, tb=tb):
                    ins = None
                    for kt in range(4):
                        ins = e.matmul(PS[:, pi, :], lhsT=wg[:, kt, f * 128:(f + 1) * 128], rhs=BR[:, kt, tbs(tb)],
                                       start=(kt == 0), stop=(kt == 3))
                    return ins

                S.op("pe", fg, reads=[("w", sgl)] + [("br", kt, tb) for kt in range(4)], writes=[("ps", pi)])
            for f in range(4):
                q = f % 2
                S.op("act", lambda e, q=q, pi=pis[f]: e.activation(out=SG[q][:], in_=PS[:, pi, :], func=AF.Sigmoid),
                     reads=[("ps", pis[f])], writes=[("sg", q)])
                S.op("dve", lambda e, q=q, f=f, tb=tb: e.tensor_tensor(out=BR[:, f, tbs(tb)], in0=BR[:, f, tbs(tb)], in1=SG[q][:],
                                                                      op=ALU.mult),
                     reads=[("sg", q), ("br", f, tb)], writes=[("br", f, tb)])
        S.barrier()
        A.pop()
        merge_branch(l, 2, w_so_d)

        S.enabled = DBG >= 5
        for half in range(2):
            sm = wload([(0, NKT, 512, w_mix_d[l].rearrange("(kt p) n -> p kt n", p=128)[:, :, half * 512:(half + 1) * 512])],
                       None)
            wm = wview(sm, NKT, 512)
            for fl in range(4):
                f2 = half * 4 + fl
                for tb in range(NTB):
                    pi = nb()

                    def fm(e, pi=pi, fl=fl, tb=tb, wm=wm):
                        ins = None
                        for kt in range(NKT):
                            ins = e.matmul(PS[:, pi, :], lhsT=wm[:, kt, fl * 128:(fl + 1) * 128], rhs=MRG[:, kt, tbs(tb)],
                                           start=(kt == 0), stop=(kt == NKT - 1))
                        return ins

                    S.op("pe", fm, reads=[("w", sm)] + [("mrg", kt, tb) for kt in range(NKT)], writes=[("ps", pi)])
                    S.op("dve", lambda e, pi=pi, f2=f2, tb=tb: e.tensor_tensor(out=xT[:, f2, tbs(tb)], in0=xT[:, f2, tbs(tb)],
                                                                              in1=PS[:, pi, :], op=ALU.add),
                         reads=[("ps", pi), ("x", f2, tb)], writes=[("x", f2, tb)])
        S.barrier()
        A.pop()

        S.enabled = DBG >= 6
        rmsnorm_to_h(l, P_GFFN, "n2")
        A.push()
        SGT = [A.alloc("sgt", [128, TBW], F32) for _ in range(3)]
        wfi = w_fi_d[l].rearrange("(kt p) n -> p kt n", p=128)
        scnt = 0
        for grp in range(2):
            j0 = grp * 11
            jl = 0
            while jl < 11:
                nj = min(2, 11 - jl)
                j = j0 + jl
                sf = wload([(0, NKT, nj * 128, wfi[:, :, j * 128:(j + nj) * 128]),
                            (NKT * nj * 128, NKT, nj * 128, wfi[:, :, FFH + j * 128:FFH + (j + nj) * 128])], None)
                wgt = wview(sf, NKT, nj * 128, 0)
                wup = wview(sf, NKT, nj * 128, NKT * nj * 128)
                for jj in range(nj):
                    for tb in range(NTB):
                        pg = nb()
                        pu = nb()

                        def fgu(e, pg=pg, pu=pu, jj=jj, tb=tb, wgt=wgt, wup=wup):
                            ins = None
                            for kt in range(NKT):
                                e.matmul(PS[:, pg, :], lhsT=wgt[:, kt, jj * 128:(jj + 1) * 128], rhs=hT[:, kt, tbs(tb)],
                                         start=(kt == 0), stop=(kt == NKT - 1))
                            for kt in range(NKT):
                                ins = e.matmul(PS[:, pu, :], lhsT=wup[:, kt, jj * 128:(jj + 1) * 128], rhs=hT[:, kt, tbs(tb)],
                                               start=(kt == 0), stop=(kt == NKT - 1))
                            return ins

                        S.op("pe", fgu, reads=[("w", sf)] + [("h", kt, tb) for kt in range(NKT)],
                             writes=[("ps", pg), ("ps", pu)])
                        q = scnt % 3
                        scnt += 1
                        S.op("act", lambda e, q=q, pg=pg: e.activation(out=SGT[q][:], in_=PS[:, pg, :], func=AF.Silu),
                             reads=[("ps", pg)], writes=[("sgt", q)])
                        S.op("dve", lambda e, q=q, pu=pu, a=jl + jj, tb=tb: e.tensor_tensor(out=FFA[:, a, tbs(tb)], in0=PS[:, pu, :],
                                                                                           in1=SGT[q][:], op=ALU.mult),
                             reads=[("ps", pu), ("sgt", q)], writes=[("ffa", jl + jj, tb)])
                jl += nj
            for fp in range(4):
                so_ = wload([(0, 11, 256, w_fo_d[l][j0 * 128:(j0 + 11) * 128, fp * 256:(fp + 1) * 256].rearrange(
                    "(j p) n -> p j n", p=128))], None)
                wo_ = wview(so_, 11, 256)
                for fl in range(2):
                    f = fp * 2 + fl
                    for tb in range(NTB):
                        pi = nb()

                        def ffo(e, pi=pi, fl=fl, tb=tb, wo_=wo_):
                            ins = None
                            for a in range(11):
                                ins = e.matmul(PS[:, pi, :], lhsT=wo_[:, a, fl * 128:(fl + 1) * 128], rhs=FFA[:, a, tbs(tb)],
                                               start=(a == 0), stop=(a == 10))
                            return ins

                        S.op("pe", ffo, reads=[("w", so_)] + [("ffa", a, tb) for a in range(11)], writes=[("ps", pi)])
                        S.op("dve", lambda e, pi=pi, f=f, tb=tb: e.tensor_tensor(out=xT[:, f, tbs(tb)], in0=xT[:, f, tbs(tb)],
                                                                                in1=PS[:, pi, :], op=ALU.add),
                             reads=[("ps", pi), ("x", f, tb)], writes=[("x", f, tb)])
        S.barrier()
        A.pop()

    for l_ in range(NL):
        layer(l_)

    S.enabled = True
    A.push()
    SQ = [A.alloc("sq", [128, TBW], BF16) for _ in range(3)]
    MS = [A.alloc("ms", [128, TBW], F32) for _ in range(2)]
    OST = [A.alloc("ost", [128, TBW], F32) for _ in range(4)]
    ocnt = 0
    outs = []
    for tb in range(NTB):
        pi = nb()
        for kt in range(NKT):
            q = (tb * NKT + kt) % 3
            S.op("act", lambda e, q=q, kt=kt, tb=tb: e.activation(out=SQ[q][:], in_=xT[:, kt, tbs(tb)], func=AF.Square),
                 reads=[("x", kt, tb)], writes=[("sq", q)])
            S.op("pe", lambda e, q=q, kt=kt, pi=pi: e.matmul(PS[:, pi, :], lhsT=ONES[:], rhs=SQ[q][:], start=(kt == 0),
                                                            stop=(kt == NKT - 1)),
                 reads=[("sq", q), "ones"], writes=[("ps", pi)])
        m = tb % 2
        S.op("dve", lambda e, m=m, pi=pi: e.tensor_scalar(out=MS[m][:], in0=PS[:, pi, :], scalar1=1.0 / D, scalar2=EPS,
                                                         op0=ALU.mult, op1=ALU.add),
             reads=[("ps", pi)], writes=[("ms", m)])
        S.op("act", lambda e, m=m: e.activation(out=MS[m][:], in_=MS[m][:], func=AF.Sqrt), reads=[("ms", m)], writes=[("ms", m)])
        S.op("dve", lambda e, m=m: e.reciprocal(out=MS[m][:], in_=MS[m][:]), reads=[("ms", m)], writes=[("ms", m)])
        for kt in range(NKT):
            oq = ocnt % 4
            ocnt += 1
            S.op("dve", lambda e, m=m, kt=kt, tb=tb, oq=oq: e.scalar_tensor_tensor(
                out=OST[oq][:], in0=xT[:, kt, tbs(tb)], scalar=PRM[:, 0, P_GFIN + kt:P_GFIN + kt + 1], in1=MS[m][:],
                op0=ALU.mult, op1=ALU.mult),
                 reads=[("x", kt, tb), ("ms", m), "prm"], writes=[("ost", oq)])
            o = S.op("sp", lambda e, kt=kt, tb=tb, oq=oq: e.dma_start(out=outT_d[kt * 128:(kt + 1) * 128, tbs(tb)], in_=OST[oq][:]),
                     reads=[("ost", oq)], writes=[("out", kt, tb)], dma=True, chan=f"o{oq}")
            outs.append(o)
    S.op("sp", lambda e: e.nop(), extra=outs)
    A.pop()
    S.emit(nc)
    return nc


def _host_consts():
    cst = np.zeros((128, CSTW), np.float32)
    cst[:, 1152 + 2 * SEQ:1152 + 2 * SEQ + SCH] = np.arange(SCH, dtype=np.float32)[None, :]
    cst[:, CSTW - 128:] = np.eye(128, dtype=np.float32)
    j = np.arange(128)[:, None]
    i = np.arange(128)[None, :]
    cur = np.where(j <= i, 0.0, -240000.0).astype(np.float32)
    prev = np.where(j > i, 0.0, -240000.0).astype(np.float32)
    cst[:, 0:512] = np.tile(cur, (1, 4))
    cst[:, 512:1024] = np.tile(prev, (1, 4))
    perm = np.zeros((128, 128), np.float32)
    for h in range(2):
        for ii in range(8):
            perm[h * 64 + ii + 8, h * 64 + ii] = -1.0
            perm[h * 64 + ii, h * 64 + ii + 8] = 1.0
    cst[:, 1024:1152] = perm
    pos = np.arange(SEQ, dtype=np.float32)
    inv_freq = (np.float32(500000.0) ** (-np.arange(0, 16, 2, dtype=np.float32) / np.float32(16))).astype(np.float32)
    ang = pos[:, None] * inv_freq[None, :]
    c = np.cos(ang).astype(np.float32).T
    s = np.sin(ang).astype(np.float32).T
    cos_t = np.ones((128, SEQ), np.float32)
    sin_t = np.zeros((128, SEQ), np.float32)
    for h in range(2):
        cos_t[h * 64:h * 64 + 8] = c
        cos_t[h * 64 + 8:h * 64 + 16] = c
        sin_t[h * 64:h * 64 + 8] = s
        sin_t[h * 64 + 8:h * 64 + 16] = s
    cst[:, 1152:1152 + SEQ] = cos_t
    cst[:, 1152 + SEQ:1152 + 2 * SEQ] = sin_t
    return cst


def _prep_inputs(inp, NL):
    f = lambda a: np.ascontiguousarray(np.asarray(a, dtype=np.float32))
    w_in = f(inp["w_in"])[:NL].copy()
    w_in[:, :, 0:512] = w_in[:, :, 0:512].reshape(NL, D, 2, 4, 64).transpose(0, 1, 3, 2, 4).reshape(NL, D, 512)
    w_ao = f(inp["w_attn_o"])[:NL].reshape(NL, 2, 4, 64, D).transpose(0, 2, 1, 3, 4).reshape(NL, 512, D)
    prm = np.zeros((NL, 128, NPRM), np.float32)
    t8 = lambda v: v.reshape(-1, 128).T
    a_re = f(inp["ssm_a_re"])
    a_im = f(inp["ssm_a_im"])
    ldt = f(inp["ssm_log_dt"])
    b_re = f(inp["ssm_b_re"])
    b_im = f(inp["ssm_b_im"])
    c_re = f(inp["ssm_c_re"])
    c_im = f(inp["ssm_c_im"])
    ssmB = np.zeros((NL, 128, 2, 2048), np.float32)
    ssmC = np.zeros((NL, 128, 2, 16, 64), np.float32)
    sinks = f(inp["attn_sinks"])
    for l in range(NL):
        prm[l, :, P_GMIX:P_GMIX + 8] = t8(f(inp["norm_mix"])[l])
        prm[l, :, P_GFFN:P_GFFN + 8] = t8(f(inp["norm_ffn"])[l])
        prm[l, :, P_BG:P_BG + 24] = t8(f(inp["b_gate"])[l])
        cw = f(inp["conv_w"])[l]
        for j in range(3):
            prm[l, :, P_CONVW + j * 4:P_CONVW + j * 4 + 4] = t8(cw[j])
        prm[l, :, P_SSMD:P_SSMD + 4] = t8(f(inp["ssm_d"])[l])
        for kvh in range(2):
            prm[l, kvh * 64:(kvh + 1) * 64, P_SINK:P_SINK + 4] = sinks[l, kvh * 4:(kvh + 1) * 4][None, :]
        arA = a_re[l].reshape(16, 2, 64).transpose(1, 2, 0).reshape(128, 16)
        aiA = a_im[l].reshape(16, 2, 64).transpose(1, 2, 0).reshape(128, 16)
        ldA = np.broadcast_to(ldt[l].reshape(16, 2, 1), (16, 2, 64)).transpose(1, 2, 0).reshape(128, 16)
        prm[l, :, P_AA:P_AA + 16] = arA
        prm[l, :, P_AI:P_AI + 16] = aiA
        prm[l, :, P_LDT:P_LDT + 16] = ldA
        prm[l, :, P_GFIN:P_GFIN + 8] = t8(f(inp["norm_final"]))
        for ct in range(4):
            for gl in range(8):
                g = ct * 8 + gl
                c0 = g * 64
                ssmB[l, gl * 16:(gl + 1) * 16, 0, c0:c0 + 64] = b_re[l, g].T
                ssmB[l, gl * 16:(gl + 1) * 16, 1, c0:c0 + 64] = b_im[l, g].T
        for pr in range(16):
            plh = (pr % 4) % 2
            for gsel in range(2):
                g = 2 * pr + gsel
                c0 = plh * 32 + gsel * 16
                ssmC[l, gsel * 64:(gsel + 1) * 64, 0, pr, c0:c0 + 16] = c_re[l, g].T
                ssmC[l, gsel * 64:(gsel + 1) * 64, 1, pr, c0:c0 + 16] = c_im[l, g].T
    shared = {
        "w_in": w_in, "w_attn_o": np.ascontiguousarray(w_ao), "w_conv_o": f(inp["w_conv_o"])[:NL],
        "w_ssm_glu": f(inp["w_ssm_glu"])[:NL], "w_ssm_o": f(inp["w_ssm_o"])[:NL], "w_mix_o": f(inp["w_mix_o"])[:NL],
        "w_ffn_in": f(inp["w_ffn_in"])[:NL], "w_ffn_out": f(inp["w_ffn_out"])[:NL],
        "prm": prm, "ssmB": ssmB, "ssmC": ssmC.reshape(NL, 128, 2, 1024), "cst": _host_consts(),
    }
    return shared


_NC_CACHE = {}


def kernel(NL=NLAYER, DBG=99, **inp):
    x = np.asarray(inp["x"], dtype=np.float32)
    B = x.shape[0]
    shared = _prep_inputs(inp, NL)
    if (NL, DBG) not in _NC_CACHE:
        _NC_CACHE[(NL, DBG)] = build_program(NL, DBG)
    nc = _NC_CACHE[(NL, DBG)]
    in_maps = []
    for b in range(B):
        m = dict(shared)
        m["xT"] = np.ascontiguousarray(x[b].T)
        in_maps.append(m)
    res = run_bass_kernel_spmd(nc, in_maps, core_ids=list(range(B)))
    out = np.stack([np.ascontiguousarray(r["outT"].T) for r in res.results], axis=0)
    return out.astype(np.float32)
```

```python
import contextlib
import math
import numpy as np
import concourse.bass as bass
import concourse.mybir as mybir
from concourse.bass_utils import run_bass_kernel_spmd

F32 = mybir.dt.float32
BF16 = mybir.dt.bfloat16
AF = mybir.ActivationFunctionType
ALU = mybir.AluOpType

D = 1024
SEQ = 2048
NLAYER = 4
NKT = 8
NTB = 4
TBW = 512
FFH = 2816
INC = 5888
EPS = 1e-6
SCH = 64
NCH = SEQ // SCH


class Op:
    __slots__ = ("eng", "fn", "deps", "dma", "chan", "ticket", "need_inc", "idx", "ndma")

    def __init__(self, eng, fn, dma, chan, ndma):
        self.eng = eng
        self.fn = fn
        self.dma = dma
        self.chan = chan
        self.ndma = ndma
        self.deps = set()
        self.ticket = None
        self.need_inc = False


class Sched:
    ENGS = ("pe", "act", "dve", "pool", "sp")

    def __init__(self):
        self.ops = []
        self.last_w = {}
        self.readers = {}
        self.last_by_eng = {}
        self.dmas_since = []

    enabled = True

    def op(self, eng, fn, reads=(), writes=(), dma=False, chan=None, ndma=1, extra=()):
        if not self.enabled:
            return None
        o = Op(eng, fn, dma, chan, ndma)
        o.idx = len(self.ops)
        deps = set(extra)
        for k in reads:
            w = self.last_w.get(k)
            if w is not None:
                deps.add(w)
        for k in writes:
            w = self.last_w.get(k)
            if w is not None:
                deps.add(w)
            deps.update(self.readers.get(k, ()))
        for k in reads:
            self.readers.setdefault(k, []).append(o)
        for k in writes:
            self.last_w[k] = o
            self.readers[k] = []
        deps.discard(o)
        o.deps = deps
        self.ops.append(o)
        if dma:
            self.dmas_since.append(o)
        else:
            self.last_by_eng[eng] = o
        return o

    def barrier(self):
        if not self.enabled:
            return
        allops = [o for o in self.last_by_eng.values()] + list(self.dmas_since)
        self.last_w = {}
        self.readers = {}
        self.dmas_since = []
        for e in self.ENGS:
            self.op(e, lambda eng: eng.nop(), extra=[o for o in allops])

    def emit(self, nc):
        for o in self.ops:
            for d in o.deps:
                if d.dma:
                    continue
                if d.eng != o.eng or d.eng != "pe":
                    d.need_inc = True
        counts = {e: 0 for e in self.ENGS}
        chan_counts = {}
        for o in self.ops:
            if o.dma:
                c = chan_counts.get(o.chan, 0) + o.ndma
                chan_counts[o.chan] = c
                o.ticket = ("c:" + o.chan, 16 * c)
            elif o.need_inc:
                counts[o.eng] += 1
                o.ticket = ("e:" + o.eng, counts[o.eng])
        sem_names = ["e:" + e for e in self.ENGS] + ["c:" + c for c in chan_counts] + ["k:" + e for e in self.ENGS]
        with contextlib.ExitStack() as st:
            sems = {}
            for n in sem_names:
                sems[n] = st.enter_context(nc.semaphore(n.replace(":", "_")))
            block = st.enter_context(nc.Block())
            per_eng = {e: [o for o in self.ops if o.eng == e] for e in self.ENGS}

            def run(engname, eng):
                waited = {}
                CH.sem = sems["k:" + engname]
                CH.cnt = 0
                for o in per_eng[engname]:
                    need = {}
                    for d in o.deps:
                        if (not d.dma) and d.eng == engname and engname == "pe":
                            continue
                        s, v = d.ticket
                        if waited.get(s, 0) >= v:
                            continue
                        if need.get(s, 0) < v:
                            need[s] = v
                    for s, v in need.items():
                        eng.wait_ge(sems[s], v)
                        waited[s] = v
                    if engname in ("act", "dve", "pool") and not o.dma:
                        ins = o.fn(EngProxy(eng))
                    else:
                        ins = o.fn(eng)
                    if o.dma:
                        if not isinstance(ins, (list, tuple)):
                            ins = [ins]
                        assert len(ins) == o.ndma
                        for i_ in ins:
                            i_.then_inc(sems[o.ticket[0]], 16)
                    elif o.need_inc:
                        ins.then_inc(sems[o.ticket[0]], 1)

            @block.tensor
            def _(e):
                run("pe", e)

            @block.scalar
            def _(e):
                run("act", e)

            @block.vector
            def _(e):
                run("dve", e)

            @block.gpsimd
            def _(e):
                run("pool", e)

            @block.sync
            def _(e):
                run("sp", e)


class _Chain:
    sem = None
    cnt = 0


CH = _Chain()


def C(e, ins):
    CH.cnt += 1
    ins.then_inc(CH.sem, 1)
    e.wait_ge(CH.sem, CH.cnt)
    return ins


class EngProxy:
    def __init__(self, e):
        self._e = e
        self._last = None

    def __getattr__(self, name):
        real = getattr(self._e, name)

        def w(*a, **k):
            if self._last is not None:
                C(self._e, self._last)
            ins = real(*a, **k)
            self._last = ins
            return ins

        return w


def red_angle(e, x, tmpf, tmpi):
    PI = math.pi
    e.tensor_scalar(out=tmpf, in0=x, scalar1=1.0 / (2 * PI), scalar2=0.5, op0=ALU.mult, op1=ALU.add)
    e.tensor_copy(out=tmpi, in_=tmpf)
    e.tensor_copy(out=tmpf, in_=tmpi)
    e.scalar_tensor_tensor(out=x, in0=tmpf, scalar=-2 * PI, in1=x, op0=ALU.mult, op1=ALU.add)
    e.tensor_scalar(out=tmpf, in0=x, scalar1=-PI, scalar2=2 * PI, op0=ALU.is_lt, op1=ALU.mult)
    e.tensor_tensor(out=x, in0=x, in1=tmpf, op=ALU.add)
    e.tensor_scalar(out=tmpf, in0=x, scalar1=PI, scalar2=-2 * PI, op0=ALU.is_gt, op1=ALU.mult)
    return e.tensor_tensor(out=x, in0=x, in1=tmpf, op=ALU.add)


class Arena:
    def __init__(self, nc, lo=16512, hi=225792):
        self.nc = nc
        self.lo = lo
        self.hi = hi
        self.top = lo
        self.n = 0
        self.stack = []
        self.offs = {}
        self.peak = lo

    def alloc(self, name, shape, dtype):
        nbytes = int(np.prod(shape[1:])) * mybir.dt.size(dtype)
        off = (self.top + 63) // 64 * 64
        assert off + nbytes <= self.hi, (name, off, nbytes, self.hi)
        t = self.nc.alloc_sbuf_tensor_at(f"{name}_{self.n}", list(shape), dtype, offset=off)
        self.offs[name] = off
        self.top = off + nbytes
        self.peak = max(self.peak, self.top)
        self.n += 1
        return t

    def push(self):
        self.stack.append(self.top)

    def pop(self):
        self.top = self.stack.pop()


P_GMIX = 0
P_GFFN = 8
P_BG = 16
P_CONVW = 40
P_SSMD = 52
P_SINK = 56
P_AA = 60
P_AI = 76
P_LDT = 92
P_GFIN = 108
NPRM = 116

W_SLOTS = 3
CSTW = 2 * 512 + 128 + 2 * SEQ + SCH + 128


def build_program(NL=NLAYER, DBG=99):
    nc = bass.Bass("TRN2", target_bir_lowering=False)
    dt_in = lambda name, shape: nc.dram_tensor(name, list(shape), F32, kind="ExternalInput").ap()
    xT_d = dt_in("xT", [D, SEQ])
    w_in_d = dt_in("w_in", [NL, D, INC])
    w_ao_d = dt_in("w_attn_o", [NL, 512, D])
    w_co_d = dt_in("w_conv_o", [NL, 512, D])
    w_glu_d = dt_in("w_ssm_glu", [NL, 512, 512])
    w_so_d = dt_in("w_ssm_o", [NL, 512, D])
    w_mix_d = dt_in("w_mix_o", [NL, D, D])
    w_fi_d = dt_in("w_ffn_in", [NL, D, 2 * FFH])
    w_fo_d = dt_in("w_ffn_out", [NL, FFH, D])
    prm_d = dt_in("prm", [NL, 128, NPRM])
    ssmB_d = dt_in("ssmB", [NL, 128, 2, 2048])
    ssmC_d = dt_in("ssmC", [NL, 128, 2, 1024])
    cst_d = dt_in("cst", [128, CSTW])
    outT_d = nc.dram_tensor("outT", [D, SEQ], F32, kind="ExternalOutput").ap()

    S = Sched()
    A = Arena(nc)
    PS = nc.alloc_psum_tensor("ps", [128, 8, 512], F32)

    xT = A.alloc("xT", [128, NKT, SEQ], F32)
    hT = A.alloc("hT", [128, NKT, SEQ], BF16)
    WS = [A.alloc(f"ws{i}", [128, 4096], BF16) for i in range(W_SLOTS)]
    mrg_off = (A.top + 63) // 64 * 64
    MRG = A.alloc("mrg", [128, NKT, SEQ], BF16)
    BR = A.alloc("br", [128, 4, SEQ], BF16)
    FFA = nc.alloc_sbuf_tensor_at("ffa", [128, 11, SEQ], BF16, offset=mrg_off)
    PRM = A.alloc("prm", [128, NL, NPRM], F32)
    ONES = A.alloc("ones", [128, 128], BF16)
    PERM = A.alloc("perm", [128, 128], BF16)
    ESK = A.alloc("esk", [128, 4], F32)
    JI = A.alloc("ji", [128, SCH], F32)
    IDN = A.alloc("idn", [128, 128], F32)
    ONESF = A.alloc("onesf", [128, 128], F32)

    psc = [0]

    def nb(n=1):
        i = psc[0] % 8
        psc[0] += 1
        return i

    wsc = [0]

    def wload(views, rshape):
        s = wsc[0] % W_SLOTS
        wsc[0] += 1

        def fn(e, s=s, views=views):
            out = []
            for (c0, a, b, src) in views:
                dst = WS[s][:, c0:c0 + a * b].rearrange("p (a b) -> p a b", a=a)
                for ai in range(a):
                    out.append(e.dma_start(out=dst[:, ai, :], in_=src[:, ai, :]))
            return out

        S.op("pool", fn, writes=[("w", s)], dma=True, chan=f"w{s}", ndma=sum(v[1] for v in views))
        return s

    def wview(s, a, b, c0=0):
        return WS[s][:, c0:c0 + a * b].rearrange("p (a b) -> p a b", a=a)

    def tbs(tb):
        return slice(tb * TBW, (tb + 1) * TBW)

    for kt in range(NKT):
        S.op("sp", lambda e, kt=kt: e.dma_start(out=xT[:, kt, :], in_=xT_d[kt * 128:(kt + 1) * 128, :]),
             writes=[("x", kt, tb) for tb in range(NTB)], dma=True, chan=f"x{kt}")
    S.op("sp", lambda e: e.dma_start(out=PRM[:], in_=prm_d.rearrange("l p c -> p l c")), writes=["prm"], dma=True,
         chan="prm")
    S.op("sp", lambda e: e.dma_start(out=JI[:], in_=cst_d[:, 1152 + 2 * SEQ:1152 + 2 * SEQ + SCH]), writes=["ji"], dma=True,
         chan="ji")
    S.op("sp", lambda e: e.dma_start(out=IDN[:], in_=cst_d[:, CSTW - 128:CSTW]), writes=["idn"], dma=True, chan="idn")
    S.op("dve", lambda e: e.memset(ONESF[:], 1.0), writes=["onesf"])
    S.op("dve", lambda e: e.memset(ONES[:], 1.0), writes=["ones"])
    S.op("pool", lambda e: e.dma_start(out=PERM[:], in_=cst_d[:, 1024:1152]), writes=["perm"], dma=True, chan="perm")

    def rmsnorm_to_h(l, gcol, name):
        nb0 = A.offs["br"] + 3 * SEQ * 2
        SQ = [nc.alloc_sbuf_tensor_at(f"nsq{i}_{name}_l{l}", [128, TBW], BF16, offset=nb0 + i * TBW * 2) for i in range(2)]
        MS = [nc.alloc_sbuf_tensor_at(f"nms_{name}_l{l}", [128, TBW], F32, offset=nb0 + 2 * TBW * 2)] * 2
        for tb in range(NTB):
            pi = nb()
            for kt in range(NKT):
                q = (tb * NKT + kt) % 2
                S.op("act", lambda e, q=q, kt=kt, tb=tb: e.activation(out=SQ[q][:], in_=xT[:, kt, tbs(tb)], func=AF.Square),
                     reads=[("x", kt, tb)], writes=[("sq", q)])
                S.op("pe", lambda e, q=q, kt=kt, pi=pi: e.matmul(PS[:, pi, :], lhsT=ONES[:], rhs=SQ[q][:], start=(kt == 0),
                                                                stop=(kt == NKT - 1)),
                     reads=[("sq", q), "ones"], writes=[("ps", pi)])
            m = 0
            S.op("dve", lambda e, m=m, pi=pi: e.tensor_scalar(out=MS[m][:], in0=PS[:, pi, :], scalar1=1.0 / D, scalar2=EPS,
                                                             op0=ALU.mult, op1=ALU.add),
                 reads=[("ps", pi)], writes=[("ms", m)])
            S.op("act", lambda e, m=m: e.activation(out=MS[m][:], in_=MS[m][:], func=AF.Sqrt), reads=[("ms", m)],
                 writes=[("ms", m)])
            S.op("dve", lambda e, m=m: e.reciprocal(out=MS[m][:], in_=MS[m][:]), reads=[("ms", m)], writes=[("ms", m)])
            for kt in range(NKT):
                S.op("dve", lambda e, m=m, kt=kt, tb=tb: e.scalar_tensor_tensor(
                    out=hT[:, kt, tbs(tb)], in0=xT[:, kt, tbs(tb)], scalar=PRM[:, l, gcol + kt:gcol + kt + 1], in1=MS[m][:],
                    op0=ALU.mult, op1=ALU.mult),
                     reads=[("x", kt, tb), ("ms", m), "prm"], writes=[("h", kt, tb)])

    def proj_group(pi, s, c0, tb, ncols_slot=512):
        wv = wview(s, NKT, ncols_slot)

        def fn(e):
            ins = None
            for kt in range(NKT):
                ins = e.matmul(PS[:, pi, :], lhsT=wv[:, kt, c0:c0 + 128], rhs=hT[:, kt, tbs(tb)], start=(kt == 0),
                               stop=(kt == NKT - 1))
            return ins

        S.op("pe", fn, reads=[("w", s)] + [("h", kt, tb) for kt in range(NKT)], writes=[("ps", pi)])

    def win_view(l, c0, ncols):
        return w_in_d[l].rearrange("(kt p) n -> p kt n", p=128)[:, :, c0:c0 + ncols]

    def merge_branch(l, b, wo_d):
        A.push()
        SG = [A.alloc("sg", [128, TBW], F32) for _ in range(2)]
        TMP = [A.alloc("tmp", [128, TBW], F32) for _ in range(2)]
        so = wload([(0, 4, 1024, wo_d[l].rearrange("(kt p) n -> p kt n", p=128))], None)
        wo = wview(so, 4, 1024)
        for half in range(2):
            sg_ = wload([(0, NKT, 512, win_view(l, 2816 + b * 1024 + half * 512, 512))], None)
            for fl in range(4):
                f = half * 4 + fl
                for tb in range(NTB):
                    py = nb()

                    def fy(e, py=py, f=f, tb=tb):
                        ins = None
                        for kt in range(4):
                            ins = e.matmul(PS[:, py, :], lhsT=wo[:, kt, f * 128:(f + 1) * 128], rhs=BR[:, kt, tbs(tb)],
                                           start=(kt == 0), stop=(kt == 3))
                        return ins

                    S.op("pe", fy, reads=[("w", so)] + [("br", kt, tb) for kt in range(4)], writes=[("ps", py)])
                    pg = nb()
                    proj_group(pg, sg_, fl * 128, tb)
                    q = (f * NTB + tb) % 2
                    S.op("act", lambda e, q=q, pg=pg, f=f: e.activation(out=SG[q][:], in_=PS[:, pg, :], func=AF.Sigmoid,
                                                                         bias=PRM[:, l, P_BG + b * 8 + f:P_BG + b * 8 + f + 1]),
                         reads=[("ps", pg), "prm"], writes=[("sg", q)])
                    if b == 0:
                        S.op("dve", lambda e, q=q, py=py, f=f, tb=tb: e.tensor_tensor(out=MRG[:, f, tbs(tb)], in0=PS[:, py, :],
                                                                                      in1=SG[q][:], op=ALU.mult),
                             reads=[("ps", py), ("sg", q)], writes=[("mrg", f, tb)])
                    else:
                        S.op("dve", lambda e, q=q, py=py: e.tensor_tensor(out=TMP[q][:], in0=PS[:, py, :], in1=SG[q][:],
                                                                          op=ALU.mult),
                             reads=[("ps", py), ("sg", q)], writes=[("tmp", q)])
                        S.op("dve", lambda e, q=q, f=f, tb=tb: e.tensor_tensor(out=MRG[:, f, tbs(tb)], in0=MRG[:, f, tbs(tb)],
                                                                               in1=TMP[q][:], op=ALU.add),
                             reads=[("tmp", q), ("mrg", f, tb)], writes=[("mrg", f, tb)])
        S.barrier()
        A.pop()

    def layer(l):
        S.enabled = DBG >= 1
        rmsnorm_to_h(l, P_GMIX, "n1")

        A.push()

        S.enabled = DBG >= 2
        A.push()
        Q = nc.alloc_sbuf_tensor_at(f"q_l{l}", [128, 4, SEQ], BF16, offset=mrg_off)
        Kt = A.alloc("k", [128, SEQ], BF16)
        V = A.alloc("v", [128, 16, 128], BF16)
        A.push()
        COS = [A.alloc("cos", [128, TBW], BF16) for _ in range(2)]
        SIN = [A.alloc("sin", [128, TBW], BF16) for _ in range(2)]
        QR = [A.alloc("qr", [128, TBW], BF16) for _ in range(2)]
        T1 = A.alloc("t1", [128, TBW], F32)
        T2 = A.alloc("t2", [128, TBW], F32)
        sq_ = wload([(0, NKT, 512, win_view(l, 0, 512))], None)
        skv = wload([(0, NKT, 256, win_view(l, 512, 256))], None)

        def rope_block(pi, tb, dst_ap, dst_key, cnt):
            q = cnt % 2
            cq = tb % 2
            if DBG < 2.1:
                return
            S.op("dve", lambda e: e.tensor_copy(out=QR[q][:], in_=PS[:, pi, :]), reads=[("ps", pi)],
                 writes=[("qr", q)])
            p2 = nb()
            S.op("pe", lambda e: e.matmul(PS[:, p2, :], lhsT=PERM[:], rhs=QR[q][:], start=True, stop=True),
                 reads=["perm", ("qr", q)], writes=[("ps", p2)])
            S.op("dve", lambda e: e.tensor_tensor(out=T1[:], in0=PS[:, pi, :], in1=COS[cq][:], op=ALU.mult),
                 reads=[("ps", pi), ("cos", cq)], writes=["t1"])
            S.op("dve", lambda e: e.tensor_tensor(out=T2[:], in0=PS[:, p2, :], in1=SIN[cq][:], op=ALU.mult),
                 reads=[("ps", p2), ("sin", cq)], writes=["t2"])
            S.op("dve", lambda e: e.tensor_tensor(out=dst_ap, in0=T1[:], in1=T2[:], op=ALU.add), reads=["t1", "t2"],
                 writes=[dst_key])

        cnt = 0
        for tb in range(NTB):
            cq = tb % 2
            S.op("pool", lambda e, tb=tb, cq=cq: e.dma_start(out=COS[cq][:], in_=cst_d[:, 1152 + tb * TBW:1152 + (tb + 1) * TBW]),
                 writes=[("cos", cq)], dma=True, chan=f"cos{cq}")
            S.op("pool", lambda e, tb=tb, cq=cq: e.dma_start(out=SIN[cq][:], in_=cst_d[:, 1152 + SEQ + tb * TBW:1152 + SEQ + (tb + 1) * TBW]),
                 writes=[("sin", cq)], dma=True, chan=f"sin{cq}")
            for g in range(4):
                pi = nb()
                proj_group(pi, sq_, g * 128, tb)
                rope_block(pi, tb, Q[:, g, tbs(tb)], ("q", g, tb), cnt)
                cnt += 1
            pi = nb()
            proj_group(pi, skv, 0, tb, ncols_slot=256)
            rope_block(pi, tb, Kt[:, tbs(tb)], ("k", tb), cnt)
            cnt += 1
        S.enabled = DBG >= 2.2
        kvv = wview(skv, NKT, 256)
        for t4 in range(4):
            pi = nb()

            def fv(e, pi=pi, t4=t4):
                ins = None
                for j in range(4):
                    tt = t4 * 4 + j
                    for kt in range(NKT):
                        ins = e.matmul(PS[:, pi, j * 128:(j + 1) * 128], lhsT=hT[:, kt, tt * 128:(tt + 1) * 128],
                                       rhs=kvv[:, kt, 128:256], start=(kt == 0), stop=(kt == NKT - 1))
                return ins

            S.op("pe", fv, reads=[("w", skv)] + [("h", kt, t4) for kt in range(NKT)], writes=[("ps", pi)])
            S.op("dve", lambda e, pi=pi, t4=t4: e.tensor_copy(out=V[:, t4 * 4:(t4 + 1) * 4, :],
                                                               in_=PS[:, pi, :].rearrange("p (a b) -> p a b", a=4)),
                 reads=[("ps", pi)], writes=[("v", t4)])
        S.barrier()
        A.pop()
        S.enabled = DBG >= 2.5
        MK = A.alloc("mk", [128, 2, 512], BF16)
        IDB = A.alloc("idb", [128, 128], BF16)
        PB = [A.alloc("pb", [128, TBW], BF16) for _ in range(8)]
        DEN = [A.alloc("den", [128, TBW], F32) for _ in range(1)]
        S.op("pool", lambda e: e.dma_start(out=MK[:], in_=cst_d[:, 0:1024].rearrange("p (a b) -> p a b", a=2)),
             writes=["mk"], dma=True, chan="mk")
        S.op("dve", lambda e: e.tensor_copy(out=IDB[:], in_=IDN[:]), reads=["idn"], writes=["idb"])
        S.op("act", lambda e: e.activation(out=ESK[:], in_=PRM[:, l, P_SINK:P_SINK + 4], func=AF.Exp), reads=["prm"],
             writes=["esk"])
        pcnt = [0]

        def att_s(qb):
            qs = slice(qb * 128, (qb + 1) * 128)
            pbs = {}
            kts = [qb] if qb == 0 else [qb - 1, qb]
            k_ = 0
            for kvh in range(2):
                hs = slice(kvh * 64, (kvh + 1) * 64)
                for kt_ in kts:
                    pa = (k_ if qb else 2 * k_) % 4
                    k_ += 1
                    ks = slice(kt_ * 128, (kt_ + 1) * 128)
                    mi = 0 if kt_ == qb else 1

                    def fs(e, pa=pa, hs=hs, ks=ks, qs=qs, kvh=kvh, mi=mi):
                        e.matmul(PS[:, pa, :], lhsT=IDB[:], rhs=MK[:, mi, :], start=True, stop=False)
                        return e.matmul(PS[:, pa, :].rearrange("p (a b) -> p a b", a=4), lhsT=Kt[hs, ks], rhs=Q[hs, :, qs],
                                        start=False, stop=True, tile_position=(kvh * 64, 0))

                    S.op("pe", fs, reads=[("k", kt_ // 4), "idb", "mk"] + [("q", g, qb // 4) for g in range(4)],
                         writes=[("ps", pa)])
                    pq = pcnt[0] % 8
                    pcnt[0] += 1
                    S.op("act", lambda e, pa=pa, pq=pq: e.activation(out=PB[pq][:], in_=PS[:, pa, :], func=AF.Exp, scale=0.125),
                         reads=[("ps", pa)], writes=[("pb", pq)])
                    pbs[(kvh, kt_)] = pq
            return pbs

        def att_pv(qb, pbs):
            qs = slice(qb * 128, (qb + 1) * 128)
            po = 4 + 2 * (qb % 2)
            pd = po + 1
            kts = [qb] if qb == 0 else [qb - 1, qb]
            n_ = len(kts)
            for kvh in range(2):
                hs = slice(kvh * 64, (kvh + 1) * 64)
                for i_, kt_ in enumerate(kts):
                    pq = pbs[(kvh, kt_)]
                    S.op("pe", lambda e, pq=pq, hs=hs, kt_=kt_, i_=i_, n_=n_, kvh=kvh, po=po: e.matmul(
                        PS[hs, po, :], lhsT=V[:, kt_, hs], rhs=PB[pq][:], start=(i_ == 0), stop=(i_ == n_ - 1),
                        tile_position=(0, kvh * 64)),
                         reads=[("pb", pq), ("v", kt_ // 4)], writes=[("ps", po)])
                    S.op("pe", lambda e, pq=pq, hs=hs, i_=i_, n_=n_, kvh=kvh, pd=pd: e.matmul(
                        PS[hs, pd, :], lhsT=ONES[:, 0:64], rhs=PB[pq][:], start=(i_ == 0), stop=(i_ == n_ - 1),
                        tile_position=(0, kvh * 64)),
                         reads=[("pb", pq), "ones"], writes=[("ps", pd)])
            dq = 0
            S.op("dve", lambda e, dq=dq, pd=pd: e.tensor_tensor(
                out=DEN[dq][:].rearrange("p (a b) -> p a b", a=4), in0=PS[:, pd, :].rearrange("p (a b) -> p a b", a=4),
                in1=ESK[:].unsqueeze(2).to_broadcast([128, 4, 128]), op=ALU.add),
                 reads=[("ps", pd), "esk"], writes=[("den", dq)])
            S.op("dve", lambda e, dq=dq: e.reciprocal(out=DEN[dq][:], in_=DEN[dq][:]), reads=[("den", dq)],
                 writes=[("den", dq)])
            S.op("dve", lambda e, dq=dq, po=po, qs=qs: e.tensor_tensor(
                out=BR[:, :, qs], in0=PS[:, po, :].rearrange("p (a b) -> p a b", a=4),
                in1=DEN[dq][:].rearrange("p (a b) -> p a b", a=4), op=ALU.mult),
                 reads=[("ps", po), ("den", dq)], writes=[("br", g, qb // 4) for g in range(4)])

        prev_pbs = att_s(0)
        for qb in range(16):
            nxt = att_s(qb + 1) if qb + 1 < 16 else None
            att_pv(qb, prev_pbs)
            prev_pbs = nxt
        S.barrier()
        A.pop()
        S.enabled = DBG >= 2.8
        merge_branch(l, 0, w_ao_d)

        S.enabled = DBG >= 3
        A.push()
        Z = A.alloc("z", [128, SEQ + 2], F32)
        Y1 = [A.alloc("y1", [128, TBW], F32) for _ in range(2)]
        S.op("dve", lambda e: e.memset(Z[:, 0:2], 0.0), writes=["z0"])
        scc = wload([(0, NKT, 512, win_view(l, 1280, 512))], None)
        for f in range(4):
            for tb in range(NTB):
                pi = nb()
                proj_group(pi, scc, f * 128, tb)
                S.op("dve", lambda e, pi=pi, f=f, tb=tb: e.tensor_copy(out=BR[:, f, tbs(tb)], in_=PS[:, pi, :]),
                     reads=[("ps", pi)], writes=[("br", f, tb)])
        scx = wload([(0, NKT, 512, win_view(l, 1792, 512))], None)
        for f in range(4):
            for tb in range(NTB):
                pi = nb()
                proj_group(pi, scx, f * 128, tb)
                a0 = tb * TBW
                S.op("dve", lambda e, pi=pi, f=f, tb=tb, a0=a0: e.tensor_tensor(out=Z[:, 2 + a0:2 + a0 + TBW], in0=PS[:, pi, :],
                                                                               in1=BR[:, f, tbs(tb)], op=ALU.mult),
                     reads=[("ps", pi), ("br", f, tb)], writes=[("z", tb)])
                yq = tb % 2
                cw = lambda j, f=f: PRM[:, l, P_CONVW + j * 4 + f:P_CONVW + j * 4 + f + 1]

                def fconv(e, a0=a0, yq=yq, f=f, tb=tb, cw=cw):
                    e.tensor_scalar(out=Y1[yq][:], in0=Z[:, 2 + a0:2 + a0 + TBW], scalar1=cw(2), scalar2=None, op0=ALU.mult)
                    e.scalar_tensor_tensor(out=Y1[yq][:], in0=Z[:, 1 + a0:1 + a0 + TBW], scalar=cw(1), in1=Y1[yq][:],
                                           op0=ALU.mult, op1=ALU.add)
                    return e.scalar_tensor_tensor(out=BR[:, f, tbs(tb)], in0=Z[:, a0:a0 + TBW], scalar=cw(0), in1=Y1[yq][:],
                                                  op0=ALU.mult, op1=ALU.add)

                S.op("dve", fconv, reads=[("z", tb), ("z", tb - 1), "z0", "prm"], writes=[("y1", yq), ("br", f, tb)])
        scb = wload([(0, NKT, 512, win_view(l, 768, 512))], None)
        for f in range(4):
            for tb in range(NTB):
                pi = nb()
                proj_group(pi, scb, f * 128, tb)
                S.op("dve", lambda e, pi=pi, f=f, tb=tb: e.tensor_tensor(out=BR[:, f, tbs(tb)], in0=PS[:, pi, :],
                                                                        in1=BR[:, f, tbs(tb)], op=ALU.mult),
                     reads=[("ps", pi), ("br", f, tb)], writes=[("br", f, tb)])
        S.barrier()
        A.pop()
        merge_branch(l, 1, w_co_d)

        S.enabled = DBG >= 4
        A.push()
        WBU = [A.alloc("wbu", [128, 2048], BF16) for _ in range(2)]
        CW = A.alloc("cw", [128, 16, 2, 64], BF16)
        DIAGD = A.alloc("diagd", [128, 4, 128], BF16)
        L1 = A.alloc("l1", [128, 2, 16], F32)
        L2 = A.alloc("l2", [128, 2, 16], F32)
        XC = A.alloc("xc", [128, 2, 16], F32)
        su = wload([(0, NKT, 512, win_view(l, 2304, 512))], None)
        for ct in range(4):
            for tb in range(NTB):
                pi = nb()
                proj_group(pi, su, ct * 128, tb)
                S.op("dve", lambda e, pi=pi, ct=ct, tb=tb: e.tensor_copy(out=BR[:, ct, tbs(tb)], in_=PS[:, pi, :]),
                     reads=[("ps", pi)], writes=[("br", ct, tb)])

        S.barrier()
        def coeffs(eng_name, are, aim, ldt, shape, tmps, key):
            dt_, lr, li, t3, t4, qr, qi, t7 = tmps[:8]
            rd = [key + "_in"]
            wr = [key]
            PI = math.pi
            S.op("act", lambda e: e.activation(out=dt_, in_=ldt, func=AF.Exp), reads=rd, writes=wr)

            ti = tmps[8]

            def red(e, x):
                e.tensor_scalar(out=dt_, in0=x, scalar1=1.0 / (2 * PI), scalar2=0.5, op0=ALU.mult, op1=ALU.add)
                e.tensor_copy(out=ti, in_=dt_)
                e.tensor_copy(out=dt_, in_=ti)
                e.scalar_tensor_tensor(out=x, in0=dt_, scalar=-2 * PI, in1=x, op0=ALU.mult, op1=ALU.add)
                e.tensor_scalar(out=dt_, in0=x, scalar1=-PI, scalar2=2 * PI, op0=ALU.is_lt, op1=ALU.mult)
                e.tensor_tensor(out=x, in0=x, in1=dt_, op=ALU.add)
                e.tensor_scalar(out=dt_, in0=x, scalar1=PI, scalar2=-2 * PI, op0=ALU.is_gt, op1=ALU.mult)
                return e.tensor_tensor(out=x, in0=x, in1=dt_, op=ALU.add)

            def f1(e):
                e.tensor_tensor(out=t3, in0=are, in1=dt_, op=ALU.mult)
                e.tensor_tensor(out=t4, in0=aim, in1=dt_, op=ALU.mult)
                e.tensor_scalar(out=t7, in0=t4, scalar1=0.5 * PI, scalar2=None, op0=ALU.add)
                red(e, t7)
                return red(e, t4)

            S.op(eng_name, f1, reads=wr, writes=wr)

            def f3(e):
                e.activation(out=t3, in_=t3, func=AF.Exp)
                e.activation(out=t7, in_=t7, func=AF.Sin)
                return e.activation(out=t4, in_=t4, func=AF.Sin)

            S.op("act", f3, reads=wr, writes=wr)

            def f4(e):
                e.tensor_tensor(out=lr, in0=t3, in1=t7, op=ALU.mult)
                e.tensor_tensor(out=li, in0=t3, in1=t4, op=ALU.mult)
                e.tensor_scalar(out=t3, in0=lr, scalar1=-1.0, scalar2=None, op0=ALU.add)
                e.tensor_tensor(out=t4, in0=are, in1=are, op=ALU.mult)
                e.tensor_tensor(out=t7, in0=aim, in1=aim, op=ALU.mult)
                e.tensor_tensor(out=t4, in0=t4, in1=t7, op=ALU.add)
                e.reciprocal(out=t4, in_=t4)
                e.tensor_tensor(out=qr, in0=t3, in1=are, op=ALU.mult)
                e.tensor_tensor(out=t7, in0=li, in1=aim, op=ALU.mult)
                e.tensor_tensor(out=qr, in0=qr, in1=t7, op=ALU.add)
                e.tensor_tensor(out=qr, in0=qr, in1=t4, op=ALU.mult)
                e.tensor_tensor(out=qi, in0=li, in1=are, op=ALU.mult)
                e.tensor_tensor(out=t7, in0=t3, in1=aim, op=ALU.mult)
                e.tensor_tensor(out=qi, in0=qi, in1=t7, op=ALU.subtract)
                return e.tensor_tensor(out=qi, in0=qi, in1=t4, op=ALU.mult)

            S.op(eng_name, f4, reads=wr, writes=wr)
            return lr, li, qr, qi

        A.push()
        TA = [A.alloc("ta", [128, 16], F32) for _ in range(8)] + [A.alloc("tai", [128, 16], mybir.dt.int32)]
        lrA, liA, qrA, qiA = coeffs("dve", PRM[:, l, P_AA:P_AA + 16], PRM[:, l, P_AI:P_AI + 16], PRM[:, l, P_LDT:P_LDT + 16],
                                [128, 16], [t[:] for t in TA], "cfA")

        def fL(e):
            e.tensor_copy(out=L1[:, 0, :], in_=lrA)
            e.tensor_copy(out=L1[:, 1, :], in_=lrA)
            e.tensor_scalar(out=L2[:, 0, :], in0=liA, scalar1=-1.0, scalar2=None, op0=ALU.mult)
            e.tensor_copy(out=L2[:, 1, :], in_=liA)
            return e.memset(XC[:], 0.0)

        S.op("dve", fL, reads=["cfA"], writes=["L", "xc"])
        wbase = A.offs["ws0"]
        U4 = 16 * SCH * 4
        mkb = lambda nm, off, dt=F32, shape=None: nc.alloc_sbuf_tensor_at(f"{nm}_l{l}", shape or [128, 16, SCH], dt,
                                                                          offset=wbase + off)
        CJ = mkb("cj", 0, BF16)
        SJ = mkb("sj", U4 // 2, BF16)
        DEC = mkb("dec", U4)
        T1s = mkb("t1s", 2 * U4)
        T2s = mkb("t2s", 3 * U4)
        T3s = mkb("t3s", 4 * U4)
        XB = mkb("xb", 5 * U4, BF16, [128, 2, 16, SCH])
        TIs = mkb("tis", 5 * U4, mybir.dt.int32)
        PHI = A.alloc("phi", [128, 16], F32)
        RR = A.alloc("rr", [128, 16], F32)
        S.op("act", lambda e: e.activation(out=PHI[:], in_=PRM[:, l, P_LDT:P_LDT + 16], func=AF.Exp), reads=["prm"],
             writes=["tabp"])

        def ft1(e):
            e.tensor_tensor(out=RR[:], in0=PRM[:, l, P_AA:P_AA + 16], in1=PHI[:], op=ALU.mult)
            return e.tensor_tensor(out=PHI[:], in0=PRM[:, l, P_AI:P_AI + 16], in1=PHI[:], op=ALU.mult)

        S.op("dve", ft1, reads=["tabp", "prm"], writes=["tabp"])
        S.op("act", lambda e: e.activation(out=RR[:], in_=RR[:], func=AF.Exp), reads=["tabp"], writes=["tabp"])

        def ft2(e):
            e.tensor_tensor(out=T2s[:], in0=PHI[:].unsqueeze(2).to_broadcast([128, 16, SCH]),
                            in1=JI[:].unsqueeze(1).to_broadcast([128, 16, SCH]), op=ALU.mult)
            e.tensor_scalar(out=T3s[:], in0=T2s[:], scalar1=0.5 * math.pi, scalar2=None, op0=ALU.add)
            red_angle(e, T2s[:], T1s[:], TIs[:])
            red_angle(e, T3s[:], T1s[:], TIs[:])
            e.tensor_copy(out=DEC[:], in_=RR[:].unsqueeze(2).to_broadcast([128, 16, SCH]))
            return e.memset(DEC[:, :, 0:1], 0.0)

        S.op("dve", ft2, reads=["tabp", "ji"], writes=["tab"])

        def ft3(e):
            e.activation(out=SJ[:], in_=T2s[:], func=AF.Sin)
            return e.activation(out=CJ[:], in_=T3s[:], func=AF.Sin)

        S.op("act", ft3, reads=["tab"], writes=["tab"])
        def fCd(e):
            return [e.dma_start(out=CW[:, :, 0, :], in_=ssmC_d[l][:, 0, :].rearrange("p (a b) -> p a b", a=16)),
                    e.dma_start(out=CW[:, :, 1, :], in_=ssmC_d[l][:, 1, :].rearrange("p (a b) -> p a b", a=16))]

        S.op("pool", fCd, writes=["cw"], dma=True, chan="cw", ndma=2)
        S.op("pool", lambda e: e.tensor_scalar(out=CW[:, :, 1, :], in0=CW[:, :, 1, :], scalar1=-1.0, scalar2=None,
                                               op0=ALU.mult), reads=["cw"], writes=["cw"])
        def fdd(e):
            ins = None
            for ct in range(4):
                ins = e.tensor_scalar(out=DIAGD[:, ct, :], in0=IDN[:], scalar1=PRM[:, l, P_SSMD + ct:P_SSMD + ct + 1],
                                      scalar2=None, op0=ALU.mult)
            return ins

        S.op("dve", fdd, reads=["idn", "prm"], writes=["diagd"])
        DG = A.alloc("dg", [128, 4, 128], F32)
        BB = A.alloc("bb", [128, 2, 512], F32)
        TQ = [A.alloc("tq", [128, 512], F32) for _ in range(2)]
        for qt in range(4):
            S.op("sp", lambda e, qt=qt: e.dma_start(out=BB[:], in_=ssmB_d[l][:, :, qt * 512:(qt + 1) * 512]),
                 writes=["bb"], dma=True, chan="bb")
            for qi_, qsrc in enumerate((qrA, qiA)):
                def fdg(e, qsrc=qsrc, qt=qt):
                    ins = None
                    for j in range(4):
                        ins = e.tensor_scalar(out=DG[:, j, :], in0=IDN[:], scalar1=qsrc[:, qt * 4 + j:qt * 4 + j + 1],
                                              scalar2=None, op0=ALU.mult)
                    return ins

                S.op("dve", fdg, reads=["cfA", "idn"], writes=["dg"])

                def fqb(e, qi_=qi_):
                    ins = None
                    for j in range(4):
                        ins = e.matmul(PS[:, qi_, j * 128:(j + 1) * 128], lhsT=ONESF[:], rhs=DG[:, j, :], start=True, stop=True)
                    return ins

                S.op("pe", fqb, reads=["dg", "onesf"], writes=[("ps", qi_)])

            def fB(e, qt=qt):
                hs_ = slice(qt * 512, (qt + 1) * 512)
                QBr = PS[:, 0, :]
                QBi = PS[:, 1, :]
                e.tensor_tensor(out=TQ[0][:], in0=QBr, in1=BB[:, 0, :], op=ALU.mult)
                e.tensor_tensor(out=TQ[1][:], in0=QBi, in1=BB[:, 1, :], op=ALU.mult)
                e.tensor_tensor(out=WBU[0][:, hs_], in0=TQ[0][:], in1=TQ[1][:], op=ALU.subtract)
                e.tensor_tensor(out=TQ[0][:], in0=QBr, in1=BB[:, 1, :], op=ALU.mult)
                e.tensor_tensor(out=TQ[1][:], in0=QBi, in1=BB[:, 0, :], op=ALU.mult)
                return e.tensor_tensor(out=WBU[1][:, hs_], in0=TQ[0][:], in1=TQ[1][:], op=ALU.add)

            S.op("dve", fB, reads=[("ps", 0), ("ps", 1), "bb"], writes=[("wbu", k) for k in range(8)] + ["tq"])
        S.barrier()
        A.pop()

        A.push()
        XS2 = [A.alloc("xs", [128, 2, 16, SCH], F32) for _ in range(2)]
        TM1 = A.alloc("tm1", [128, 2, 16], F32)
        TM2 = A.alloc("tm2", [128, 2, 16], F32)
        G1 = A.alloc("g1", [128, 4, SCH], F32)
        flat2 = lambda ap: ap.rearrange("p a b -> p (a b)")
        PSBU = [("ps", 4), ("ps", 5), ("ps", 6), ("ps", 7)]

        def stage_a1(c):
            cs_ = slice(c * SCH, (c + 1) * SCH)
            tbk = (c * SCH) // TBW
            xk = c % 2
            XS = XS2[xk]

            def fbu(e, cs_=cs_):
                ins = None
                for pr in range(16):
                    ct = pr // 4
                    for ri in range(2):
                        o0 = (ri * 16 + pr) * SCH
                        bank = 4 + o0 // 512
                        oo = o0 % 512
                        ins = e.matmul(PS[:, bank, oo:oo + SCH], lhsT=WBU[ri][:, pr * 128:(pr + 1) * 128], rhs=BR[:, ct, cs_],
                                       start=True, stop=True)
                return ins

            S.op("pe", fbu, reads=[("wbu", ct) for ct in range(8)] + [("br", ct, tbk) for ct in range(4)], writes=PSBU)
            S.op("act", lambda e: e.activation(out=XS[:].rearrange("p a b c -> p (a b c)"),
                                               in_=PS[:, 4:8, :].rearrange("p a b -> p (a b)"), func=AF.Identity),
                 reads=PSBU, writes=[("xs_re", xk), ("xs_im", xk)])

        def stage_a2(c):
            xk = c % 2
            XS = XS2[xk]
            BRe = XS[:, 0, :, :]
            BIm = XS[:, 1, :, :]
            kre, kim = ("xs_re", xk), ("xs_im", xk)

            def fscan(e):
                e.tensor_tensor(out=T1s[:], in0=BRe, in1=CJ[:], op=ALU.mult)
                e.tensor_tensor(out=T2s[:], in0=BIm, in1=SJ[:], op=ALU.mult)
                e.tensor_tensor(out=T1s[:], in0=T1s[:], in1=T2s[:], op=ALU.add)
                e.tensor_tensor(out=T2s[:], in0=BRe, in1=SJ[:], op=ALU.mult)
                e.tensor_tensor(out=BIm, in0=BIm, in1=CJ[:], op=ALU.mult)
                e.tensor_tensor(out=BIm, in0=BIm, in1=T2s[:], op=ALU.subtract)
                e.tensor_tensor(out=TM1[:], in0=XC[:], in1=L1[:], op=ALU.mult)
                e.tensor_tensor(out=TM2[:], in0=XC[:, ::-1, :], in1=L2[:], op=ALU.mult)
                e.tensor_tensor(out=TM1[:], in0=TM1[:], in1=TM2[:], op=ALU.add)
                e.tensor_tensor(out=T1s[:, :, 0], in0=T1s[:, :, 0], in1=TM1[:, 0, :], op=ALU.add)
                e.tensor_tensor(out=BIm[:, :, 0], in0=BIm[:, :, 0], in1=TM1[:, 1, :], op=ALU.add)
                e.tensor_tensor_scan(out=flat2(BRe), data0=flat2(DEC[:]), data1=flat2(T1s[:]), initial=0.0, op0=ALU.mult,
                                     op1=ALU.add)
                e.tensor_tensor_scan(out=flat2(T2s[:]), data0=flat2(DEC[:]), data1=flat2(BIm), initial=0.0, op0=ALU.mult,
                                     op1=ALU.add)
                e.tensor_tensor(out=T1s[:], in0=T2s[:], in1=SJ[:], op=ALU.mult)
                e.tensor_tensor(out=BIm, in0=BRe, in1=CJ[:], op=ALU.mult)
                e.tensor_tensor(out=XB[:, 0, :, :], in0=BIm, in1=T1s[:], op=ALU.subtract)
                e.tensor_tensor(out=XC[:, 0, :], in0=BIm[:, :, SCH - 1], in1=T1s[:, :, SCH - 1], op=ALU.subtract)
                e.tensor_tensor(out=T1s[:], in0=BRe, in1=SJ[:], op=ALU.mult)
                e.tensor_tensor(out=BIm, in0=T2s[:], in1=CJ[:], op=ALU.mult)
                e.tensor_tensor(out=XC[:, 1, :], in0=BIm[:, :, SCH - 1], in1=T1s[:, :, SCH - 1], op=ALU.add)
                return e.tensor_tensor(out=XB[:, 1, :, :], in0=BIm, in1=T1s[:], op=ALU.add)

            S.op("dve", fscan, reads=[kre, kim, "L", "xc", "tab"], writes=[kre, kim, "t1", "t2", "xc", "xb"])
            py = nb() % 4

            cs_ = slice(c * SCH, (c + 1) * SCH)
            tbk = (c * SCH) // TBW

            def fcm(e, py=py, cs_=cs_):
                ins = None
                for ct in range(4):
                    e.matmul(PS[:, py, ct * SCH:(ct + 1) * SCH], lhsT=DIAGD[:, ct, :], rhs=BR[:, ct, cs_], start=True, stop=False)
                    for half in range(2):
                        k = 0
                        for pl in range(2):
                            pr = ct * 4 + half * 2 + pl
                            for ri in range(2):
                                ins = e.matmul(PS[half * 64:(half + 1) * 64, py, ct * SCH:(ct + 1) * SCH],
                                               lhsT=CW[:, pr, ri, :], rhs=XB[:, ri, pr, :], start=False, stop=(k == 3),
                                               tile_position=(0, half * 64))
                                k += 1
                return ins

            S.op("pe", fcm, reads=["xb", "cw", "diagd"] + [("br", ct, tbk) for ct in range(4)], writes=[("ps", py)])
            return py

        def stage_b(c, py):
            cs_ = slice(c * SCH, (c + 1) * SCH)
            tbk = (c * SCH) // TBW

            PY = PS[:, py, 0:4 * SCH].rearrange("p (a b) -> p a b", a=4)
            S.op("act", lambda e, PY=PY: e.activation(out=G1[:], in_=PY, func=AF.Square), reads=[("ps", py)], writes=["g1"])

            def fys(e, PY=PY):
                e.tensor_scalar(out=G1[:], in0=G1[:], scalar1=0.044715, scalar2=1.0, op0=ALU.mult, op1=ALU.add)
                return e.tensor_tensor(out=G1[:], in0=G1[:], in1=PY, op=ALU.mult)

            S.op("dve", fys, reads=[("ps", py), "g1"], writes=["g1"])
            S.op("act", lambda e: e.activation(out=G1[:], in_=G1[:], func=AF.Sigmoid, scale=2.0 * math.sqrt(2.0 / math.pi)),
                 reads=["g1"], writes=["g1"])
            S.op("dve", lambda e, cs_=cs_, PY=PY: e.tensor_tensor(out=BR[:, :, cs_], in0=PY, in1=G1[:], op=ALU.mult),
                 reads=[("ps", py), "g1"], writes=[("yso", c)])

        pys = {}
        stage_a1(0)
        for c in range(NCH + 1):
            if c + 1 < NCH:
                stage_a1(c + 1)
            if c < NCH:
                pys[c] = stage_a2(c)
            if c >= 1:
                stage_b(c - 1, pys[c - 1])
        S.barrier()
        A.pop()
        SG = [A.alloc("sg", [128, TBW], F32) for _ in range(2)]
        sgl = wload([(0, 4, 512, w_glu_d[l].rearrange("(kt p) n -> p kt n", p=128))], None)
        wg = wview(sgl, 4, 512)
        for tb in range(NTB):
            pis = []
            for f in range(4):
                pi = nb()
                pis.append(pi)

                def fg(e, pi=pi, f=f, tb=tb):
                    ins = None
                    for kt in range(4):
                        ins = e.matmul(PS[:, pi, :], lhsT=wg[:, kt, f * 128:(f + 1) * 128], rhs=BR[:, kt, tbs(tb)],
                                       start=(kt == 0), stop=(kt == 3))
                    return ins

                S.op("pe", fg, reads=[("w", sgl)] + [("br", kt, tb) for kt in range(4)], writes=[("ps", pi)])
            for f in range(4):
                q = f % 2
                S.op("act", lambda e, q=q, pi=pis[f]: e.activation(out=SG[q][:], in_=PS[:, pi, :], func=AF.Sigmoid),
                     reads=[("ps", pis[f])], writes=[("sg", q)])
                S.op("dve", lambda e, q=q, f=f, tb=tb: e.tensor_tensor(out=BR[:, f, tbs(tb)], in0=BR[:, f, tbs(tb)], in1=SG[q][:],
                                                                      op=ALU.mult),
                     reads=[("sg", q), ("br", f, tb)], writes=[("br", f, tb)])
        S.barrier()
        A.pop()
        merge_branch(l, 2, w_so_d)

        S.enabled = DBG >= 5
        for half in range(2):
            sm = wload([(0, NKT, 512, w_mix_d[l].rearrange("(kt p) n -> p kt n", p=128)[:, :, half * 512:(half + 1) * 512])],
                       None)
            wm = wview(sm, NKT, 512)
            for fl in range(4):
                f2 = half * 4 + fl
                for tb in range(NTB):
                    pi = nb()

                    def fm(e, pi=pi, fl=fl, tb=tb, wm=wm):
                        ins = None
                        for kt in range(NKT):
                            ins = e.matmul(PS[:, pi, :], lhsT=wm[:, kt, fl * 128:(fl + 1) * 128], rhs=MRG[:, kt, tbs(tb)],
                                           start=(kt == 0), stop=(kt == NKT - 1))
                        return ins

                    S.op("pe", fm, reads=[("w", sm)] + [("mrg", kt, tb) for kt in range(NKT)], writes=[("ps", pi)])
                    S.op("dve", lambda e, pi=pi, f2=f2, tb=tb: e.tensor_tensor(out=xT[:, f2, tbs(tb)], in0=xT[:, f2, tbs(tb)],
                                                                              in1=PS[:, pi, :], op=ALU.add),
                         reads=[("ps", pi), ("x", f2, tb)], writes=[("x", f2, tb)])
        S.barrier()
        A.pop()

        S.enabled = DBG >= 6
        rmsnorm_to_h(l, P_GFFN, "n2")
        A.push()
        SGT = [A.alloc("sgt", [128, TBW], F32) for _ in range(3)]
        wfi = w_fi_d[l].rearrange("(kt p) n -> p kt n", p=128)
        scnt = 0
        for grp in range(2):
            j0 = grp * 11
            jl = 0
            while jl < 11:
                nj = min(2, 11 - jl)
                j = j0 + jl
                sf = wload([(0, NKT, nj * 128, wfi[:, :, j * 128:(j + nj) * 128]),
                            (NKT * nj * 128, NKT, nj * 128, wfi[:, :, FFH + j * 128:FFH + (j + nj) * 128])], None)
                wgt = wview(sf, NKT, nj * 128, 0)
                wup = wview(sf, NKT, nj * 128, NKT * nj * 128)
                for jj in range(nj):
                    for tb in range(NTB):
                        pg = nb()
                        pu = nb()

                        def fgu(e, pg=pg, pu=pu, jj=jj, tb=tb, wgt=wgt, wup=wup):
                            ins = None
                            for kt in range(NKT):
                                e.matmul(PS[:, pg, :], lhsT=wgt[:, kt, jj * 128:(jj + 1) * 128], rhs=hT[:, kt, tbs(tb)],
                                         start=(kt == 0), stop=(kt == NKT - 1))
                            for kt in range(NKT):
                                ins = e.matmul(PS[:, pu, :], lhsT=wup[:, kt, jj * 128:(jj + 1) * 128], rhs=hT[:, kt, tbs(tb)],
                                               start=(kt == 0), stop=(kt == NKT - 1))
                            return ins

                        S.op("pe", fgu, reads=[("w", sf)] + [("h", kt, tb) for kt in range(NKT)],
                             writes=[("ps", pg), ("ps", pu)])
                        q = scnt % 3
                        scnt += 1
                        S.op("act", lambda e, q=q, pg=pg: e.activation(out=SGT[q][:], in_=PS[:, pg, :], func=AF.Silu),
                             reads=[("ps", pg)], writes=[("sgt", q)])
                        S.op("dve", lambda e, q=q, pu=pu, a=jl + jj, tb=tb: e.tensor_tensor(out=FFA[:, a, tbs(tb)], in0=PS[:, pu, :],
                                                                                           in1=SGT[q][:], op=ALU.mult),
                             reads=[("ps", pu), ("sgt", q)], writes=[("ffa", jl + jj, tb)])
                jl += nj
            for fp in range(4):
                so_ = wload([(0, 11, 256, w_fo_d[l][j0 * 128:(j0 + 11) * 128, fp * 256:(fp + 1) * 256].rearrange(
                    "(j p) n -> p j n", p=128))], None)
                wo_ = wview(so_, 11, 256)
                for fl in range(2):
                    f = fp * 2 + fl
                    for tb in range(NTB):
                        pi = nb()

                        def ffo(e, pi=pi, fl=fl, tb=tb, wo_=wo_):
                            ins = None
                            for a in range(11):
                                ins = e.matmul(PS[:, pi, :], lhsT=wo_[:, a, fl * 128:(fl + 1) * 128], rhs=FFA[:, a, tbs(tb)],
                                               start=(a == 0), stop=(a == 10))
                            return ins

                        S.op("pe", ffo, reads=[("w", so_)] + [("ffa", a, tb) for a in range(11)], writes=[("ps", pi)])
                        S.op("dve", lambda e, pi=pi, f=f, tb=tb: e.tensor_tensor(out=xT[:, f, tbs(tb)], in0=xT[:, f, tbs(tb)],
                                                                                in1=PS[:, pi, :], op=ALU.add),
                             reads=[("ps", pi), ("x", f, tb)], writes=[("x", f, tb)])
        S.barrier()
        A.pop()

    for l_ in range(NL):
        layer(l_)

    S.enabled = True
    A.push()
    SQ = [A.alloc("sq", [128, TBW], BF16) for _ in range(3)]
    MS = [A.alloc("ms", [128, TBW], F32) for _ in range(2)]
    OST = [A.alloc("ost", [128, TBW], F32) for _ in range(4)]
    ocnt = 0
    outs = []
    for tb in range(NTB):
        pi = nb()
        for kt in range(NKT):
            q = (tb * NKT + kt) % 3
            S.op("act", lambda e, q=q, kt=kt, tb=tb: e.activation(out=SQ[q][:], in_=xT[:, kt, tbs(tb)], func=AF.Square),
                 reads=[("x", kt, tb)], writes=[("sq", q)])
            S.op("pe", lambda e, q=q, kt=kt, pi=pi: e.matmul(PS[:, pi, :], lhsT=ONES[:], rhs=SQ[q][:], start=(kt == 0),
                                                            stop=(kt == NKT - 1)),
                 reads=[("sq", q), "ones"], writes=[("ps", pi)])
        m = tb % 2
        S.op("dve", lambda e, m=m, pi=pi: e.tensor_scalar(out=MS[m][:], in0=PS[:, pi, :], scalar1=1.0 / D, scalar2=EPS,
                                                         op0=ALU.mult, op1=ALU.add),
             reads=[("ps", pi)], writes=[("ms", m)])
        S.op("act", lambda e, m=m: e.activation(out=MS[m][:], in_=MS[m][:], func=AF.Sqrt), reads=[("ms", m)], writes=[("ms", m)])
        S.op("dve", lambda e, m=m: e.reciprocal(out=MS[m][:], in_=MS[m][:]), reads=[("ms", m)], writes=[("ms", m)])
        for kt in range(NKT):
            oq = ocnt % 4
            ocnt += 1
            S.op("dve", lambda e, m=m, kt=kt, tb=tb, oq=oq: e.scalar_tensor_tensor(
                out=OST[oq][:], in0=xT[:, kt, tbs(tb)], scalar=PRM[:, 0, P_GFIN + kt:P_GFIN + kt + 1], in1=MS[m][:],
                op0=ALU.mult, op1=ALU.mult),
                 reads=[("x", kt, tb), ("ms", m), "prm"], writes=[("ost", oq)])
            o = S.op("sp", lambda e, kt=kt, tb=tb, oq=oq: e.dma_start(out=outT_d[kt * 128:(kt + 1) * 128, tbs(tb)], in_=OST[oq][:]),
                     reads=[("ost", oq)], writes=[("out", kt, tb)], dma=True, chan=f"o{oq}")
            outs.append(o)
    S.op("sp", lambda e: e.nop(), extra=outs)
    A.pop()
    S.emit(nc)
    return nc


def _host_consts():
    cst = np.zeros((128, CSTW), np.float32)
    cst[:, 1152 + 2 * SEQ:1152 + 2 * SEQ + SCH] = np.arange(SCH, dtype=np.float32)[None, :]
    cst[:, CSTW - 128:] = np.eye(128, dtype=np.float32)
    j = np.arange(128)[:, None]
    i = np.arange(128)[None, :]
    cur = np.where(j <= i, 0.0, -240000.0).astype(np.float32)
    prev = np.where(j > i, 0.0, -240000.0).astype(np.float32)
    cst[:, 0:512] = np.tile(cur, (1, 4))
    cst[:, 512:1024] = np.tile(prev, (1, 4))
    perm = np.zeros((128, 128), np.float32)
    for h in range(2):
        for ii in range(8):
            perm[h * 64 + ii + 8, h * 64 + ii] = -1.0
            perm[h * 64 + ii, h * 64 + ii + 8] = 1.0
    cst[:, 1024:1152] = perm
    pos = np.arange(SEQ, dtype=np.float32)
    inv_freq = (np.float32(500000.0) ** (-np.arange(0, 16, 2, dtype=np.float32) / np.float32(16))).astype(np.float32)
    ang = pos[:, None] * inv_freq[None, :]
    c = np.cos(ang).astype(np.float32).T
    s = np.sin(ang).astype(np.float32).T
    cos_t = np.ones((128, SEQ), np.float32)
    sin_t = np.zeros((128, SEQ), np.float32)
    for h in range(2):
        cos_t[h * 64:h * 64 + 8] = c
        cos_t[h * 64 + 8:h * 64 + 16] = c
        sin_t[h * 64:h * 64 + 8] = s
        sin_t[h * 64 + 8:h * 64 + 16] = s
    cst[:, 1152:1152 + SEQ] = cos_t
    cst[:, 1152 + SEQ:1152 + 2 * SEQ] = sin_t
    return cst


def _prep_inputs(inp, NL):
    f = lambda a: np.ascontiguousarray(np.asarray(a, dtype=np.float32))
    w_in = f(inp["w_in"])[:NL].copy()
    w_in[:, :, 0:512] = w_in[:, :, 0:512].reshape(NL, D, 2, 4, 64).transpose(0, 1, 3, 2, 4).reshape(NL, D, 512)
    w_ao = f(inp["w_attn_o"])[:NL].reshape(NL, 2, 4, 64, D).transpose(0, 2, 1, 3, 4).reshape(NL, 512, D)
    prm = np.zeros((NL, 128, NPRM), np.float32)
    t8 = lambda v: v.reshape(-1, 128).T
    a_re = f(inp["ssm_a_re"])
    a_im = f(inp["ssm_a_im"])
    ldt = f(inp["ssm_log_dt"])
    b_re = f(inp["ssm_b_re"])
    b_im = f(inp["ssm_b_im"])
    c_re = f(inp["ssm_c_re"])
    c_im = f(inp["ssm_c_im"])
    ssmB = np.zeros((NL, 128, 2, 2048), np.float32)
    ssmC = np.zeros((NL, 128, 2, 16, 64), np.float32)
    sinks = f(inp["attn_sinks"])
    for l in range(NL):
        prm[l, :, P_GMIX:P_GMIX + 8] = t8(f(inp["norm_mix"])[l])
        prm[l, :, P_GFFN:P_GFFN + 8] = t8(f(inp["norm_ffn"])[l])
        prm[l, :, P_BG:P_BG + 24] = t8(f(inp["b_gate"])[l])
        cw = f(inp["conv_w"])[l]
        for j in range(3):
            prm[l, :, P_CONVW + j * 4:P_CONVW + j * 4 + 4] = t8(cw[j])
        prm[l, :, P_SSMD:P_SSMD + 4] = t8(f(inp["ssm_d"])[l])
        for kvh in range(2):
            prm[l, kvh * 64:(kvh + 1) * 64, P_SINK:P_SINK + 4] = sinks[l, kvh * 4:(kvh + 1) * 4][None, :]
        arA = a_re[l].reshape(16, 2, 64).transpose(1, 2, 0).reshape(128, 16)
        aiA = a_im[l].reshape(16, 2, 64).transpose(1, 2, 0).reshape(128, 16)
        ldA = np.broadcast_to(ldt[l].reshape(16, 2, 1), (16, 2, 64)).transpose(1, 2, 0).reshape(128, 16)
        prm[l, :, P_AA:P_AA + 16] = arA
        prm[l, :, P_AI:P_AI + 16] = aiA
        prm[l, :, P_LDT:P_LDT + 16] = ldA
        prm[l, :, P_GFIN:P_GFIN + 8] = t8(f(inp["norm_final"]))
        for ct in range(4):
            for gl in range(8):
                g = ct * 8 + gl
                c0 = g * 64
                ssmB[l, gl * 16:(gl + 1) * 16, 0, c0:c0 + 64] = b_re[l, g].T
                ssmB[l, gl * 16:(gl + 1) * 16, 1, c0:c0 + 64] = b_im[l, g].T
        for pr in range(16):
            plh = (pr % 4) % 2
            for gsel in range(2):
                g = 2 * pr + gsel
                c0 = plh * 32 + gsel * 16
                ssmC[l, gsel * 64:(gsel + 1) * 64, 0, pr, c0:c0 + 16] = c_re[l, g].T
                ssmC[l, gsel * 64:(gsel + 1) * 64, 1, pr, c0:c0 + 16] = c_im[l, g].T
    shared = {
        "w_in": w_in, "w_attn_o": np.ascontiguousarray(w_ao), "w_conv_o": f(inp["w_conv_o"])[:NL],
        "w_ssm_glu": f(inp["w_ssm_glu"])[:NL], "w_ssm_o": f(inp["w_ssm_o"])[:NL], "w_mix_o": f(inp["w_mix_o"])[:NL],
        "w_ffn_in": f(inp["w_ffn_in"])[:NL], "w_ffn_out": f(inp["w_ffn_out"])[:NL],
        "prm": prm, "ssmB": ssmB, "ssmC": ssmC.reshape(NL, 128, 2, 1024), "cst": _host_consts(),
    }
    return shared


_NC_CACHE = {}


def kernel(NL=NLAYER, DBG=99, **inp):
    x = np.asarray(inp["x"], dtype=np.float32)
    B = x.shape[0]
    shared = _prep_inputs(inp, NL)
    if (NL, DBG) not in _NC_CACHE:
        _NC_CACHE[(NL, DBG)] = build_program(NL, DBG)
    nc = _NC_CACHE[(NL, DBG)]
    in_maps = []
    for b in range(B):
        m = dict(shared)
        m["xT"] = np.ascontiguousarray(x[b].T)
        in_maps.append(m)
    res = run_bass_kernel_spmd(nc, in_maps, core_ids=list(range(B)))
    out = np.stack([np.ascontiguousarray(r["outT"].T) for r in res.results], axis=0)
    return out.astype(np.float32)
```

```python
import contextlib
import math
import numpy as np
import concourse.bass as bass
import concourse.mybir as mybir
from concourse.bass_utils import run_bass_kernel_spmd

F32 = mybir.dt.float32
BF16 = mybir.dt.bfloat16
AF = mybir.ActivationFunctionType
ALU = mybir.AluOpType

D = 1024
SEQ = 2048
NLAYER = 4
NKT = 8
NTB = 4
TBW = 512
FFH = 2816
INC = 5888
EPS = 1e-6
SCH = 64
NCH = SEQ // SCH


class Op:
    __slots__ = ("eng", "fn", "deps", "dma", "chan", "ticket", "need_inc", "idx", "ndma")

    def __init__(self, eng, fn, dma, chan, ndma):
        self.eng = eng
        self.fn = fn
        self.dma = dma
        self.chan = chan
        self.ndma = ndma
        self.deps = set()
        self.ticket = None
        self.need_inc = False


class Sched:
    ENGS = ("pe", "act", "dve", "pool", "sp")

    def __init__(self):
        self.ops = []
        self.last_w = {}
        self.readers = {}
        self.last_by_eng = {}
        self.dmas_since = []

    enabled = True

    def op(self, eng, fn, reads=(), writes=(), dma=False, chan=None, ndma=1, extra=()):
        if not self.enabled:
            return None
        o = Op(eng, fn, dma, chan, ndma)
        o.idx = len(self.ops)
        deps = set(extra)
        for k in reads:
            w = self.last_w.get(k)
            if w is not None:
                deps.add(w)
        for k in writes:
            w = self.last_w.get(k)
            if w is not None:
                deps.add(w)
            deps.update(self.readers.get(k, ()))
        for k in reads:
            self.readers.setdefault(k, []).append(o)
        for k in writes:
            self.last_w[k] = o
            self.readers[k] = []
        deps.discard(o)
        o.deps = deps
        self.ops.append(o)
        if dma:
            self.dmas_since.append(o)
        else:
            self.last_by_eng[eng] = o
        return o

    def barrier(self):
        if not self.enabled:
            return
        allops = [o for o in self.last_by_eng.values()] + list(self.dmas_since)
        self.last_w = {}
        self.readers = {}
        self.dmas_since = []
        for e in self.ENGS:
            self.op(e, lambda eng: eng.nop(), extra=[o for o in allops])

    def emit(self, nc):
        for o in self.ops:
            for d in o.deps:
                if d.dma:
                    continue
                if d.eng != o.eng or d.eng != "pe":
                    d.need_inc = True
        counts = {e: 0 for e in self.ENGS}
        chan_counts = {}
        for o in self.ops:
            if o.dma:
                c = chan_counts.get(o.chan, 0) + o.ndma
                chan_counts[o.chan] = c
                o.ticket = ("c:" + o.chan, 16 * c)
            elif o.need_inc:
                counts[o.eng] += 1
                o.ticket = ("e:" + o.eng, counts[o.eng])
        sem_names = ["e:" + e for e in self.ENGS] + ["c:" + c for c in chan_counts] + ["k:" + e for e in self.ENGS]
        with contextlib.ExitStack() as st:
            sems = {}
            for n in sem_names:
                sems[n] = st.enter_context(nc.semaphore(n.replace(":", "_")))
            block = st.enter_context(nc.Block())
            per_eng = {e: [o for o in self.ops if o.eng == e] for e in self.ENGS}

            def run(engname, eng):
                waited = {}
                CH.sem = sems["k:" + engname]
                CH.cnt = 0
                for o in per_eng[engname]:
                    need = {}
                    for d in o.deps:
                        if (not d.dma) and d.eng == engname and engname == "pe":
                            continue
                        s, v = d.ticket
                        if waited.get(s, 0) >= v:
                            continue
                        if need.get(s, 0) < v:
                            need[s] = v
                    for s, v in need.items():
                        eng.wait_ge(sems[s], v)
                        waited[s] = v
                    if engname in ("act", "dve", "pool") and not o.dma:
                        ins = o.fn(EngProxy(eng))
                    else:
                        ins = o.fn(eng)
                    if o.dma:
                        if not isinstance(ins, (list, tuple)):
                            ins = [ins]
                        assert len(ins) == o.ndma
                        for i_ in ins:
                            i_.then_inc(sems[o.ticket[0]], 16)
                    elif o.need_inc:
                        ins.then_inc(sems[o.ticket[0]], 1)

            @block.tensor
            def _(e):
                run("pe", e)

            @block.scalar
            def _(e):
                run("act", e)

            @block.vector
            def _(e):
                run("dve", e)

            @block.gpsimd
            def _(e):
                run("pool", e)

            @block.sync
            def _(e):
                run("sp", e)


class _Chain:
    sem = None
    cnt = 0


CH = _Chain()


def C(e, ins):
    CH.cnt += 1
    ins.then_inc(CH.sem, 1)
    e.wait_ge(CH.sem, CH.cnt)
    return ins


class EngProxy:
    def __init__(self, e):
        self._e = e
        self._last = None
        self._skip = False

    def nosync(self):
        self._skip = True

    def __getattr__(self, name):
        real = getattr(self._e, name)

        def w(*a, **k):
            if self._last is not None and not self._skip:
                C(self._e, self._last)
            self._skip = False
            ins = real(*a, **k)
            self._last = ins
            return ins

        return w


def red_angle(e, x, tmpf, tmpi):
    PI = math.pi
    e.tensor_scalar(out=tmpf, in0=x, scalar1=1.0 / (2 * PI), scalar2=0.5, op0=ALU.mult, op1=ALU.add)
    e.tensor_copy(out=tmpi, in_=tmpf)
    e.tensor_copy(out=tmpf, in_=tmpi)
    e.scalar_tensor_tensor(out=x, in0=tmpf, scalar=-2 * PI, in1=x, op0=ALU.mult, op1=ALU.add)
    e.tensor_scalar(out=tmpf, in0=x, scalar1=-PI, scalar2=2 * PI, op0=ALU.is_lt, op1=ALU.mult)
    e.tensor_tensor(out=x, in0=x, in1=tmpf, op=ALU.add)
    e.tensor_scalar(out=tmpf, in0=x, scalar1=PI, scalar2=-2 * PI, op0=ALU.is_gt, op1=ALU.mult)
    return e.tensor_tensor(out=x, in0=x, in1=tmpf, op=ALU.add)


class Arena:
    def __init__(self, nc, lo=16512, hi=225792):
        self.nc = nc
        self.lo = lo
        self.hi = hi
        self.top = lo
        self.n = 0
        self.stack = []
        self.offs = {}
        self.peak = lo

    def alloc(self, name, shape, dtype):
        nbytes = int(np.prod(shape[1:])) * mybir.dt.size(dtype)
        off = (self.top + 63) // 64 * 64
        assert off + nbytes <= self.hi, (name, off, nbytes, self.hi)
        t = self.nc.alloc_sbuf_tensor_at(f"{name}_{self.n}", list(shape), dtype, offset=off)
        self.offs[name] = off
        self.top = off + nbytes
        self.peak = max(self.peak, self.top)
        self.n += 1
        return t

    def push(self):
        self.stack.append(self.top)

    def pop(self):
        self.top = self.stack.pop()


P_GMIX = 0
P_GFFN = 8
P_BG = 16
P_CONVW = 40
P_SSMD = 52
P_SINK = 56
P_AA = 60
P_AI = 76
P_LDT = 92
P_GFIN = 108
NPRM = 116

W_SLOTS = 3
CSTW = 2 * 512 + 128 + 2 * SEQ + SCH + 128


def build_program(NL=NLAYER, DBG=99):
    nc = bass.Bass("TRN2", target_bir_lowering=False)
    dt_in = lambda name, shape: nc.dram_tensor(name, list(shape), F32, kind="ExternalInput").ap()
    xT_d = dt_in("xT", [D, SEQ])
    w_in_d = dt_in("w_in", [NL, D, INC])
    w_ao_d = dt_in("w_attn_o", [NL, 512, D])
    w_co_d = dt_in("w_conv_o", [NL, 512, D])
    w_glu_d = dt_in("w_ssm_glu", [NL, 512, 512])
    w_so_d = dt_in("w_ssm_o", [NL, 512, D])
    w_mix_d = dt_in("w_mix_o", [NL, D, D])
    w_fi_d = dt_in("w_ffn_in", [NL, D, 2 * FFH])
    w_fo_d = dt_in("w_ffn_out", [NL, FFH, D])
    prm_d = dt_in("prm", [NL, 128, NPRM])
    ssmB_d = dt_in("ssmB", [NL, 128, 2, 2048])
    ssmC_d = dt_in("ssmC", [NL, 128, 2, 1024])
    cst_d = dt_in("cst", [128, CSTW])
    outT_d = nc.dram_tensor("outT", [D, SEQ], F32, kind="ExternalOutput").ap()

    S = Sched()
    A = Arena(nc)
    PS = nc.alloc_psum_tensor("ps", [128, 8, 512], F32)

    xT = A.alloc("xT", [128, NKT, SEQ], F32)
    hT = A.alloc("hT", [128, NKT, SEQ], BF16)
    WS = [A.alloc(f"ws{i}", [128, 4096], BF16) for i in range(W_SLOTS)]
    mrg_off = (A.top + 63) // 64 * 64
    MRG = A.alloc("mrg", [128, NKT, SEQ], BF16)
    BR = A.alloc("br", [128, 4, SEQ], BF16)
    FFA = nc.alloc_sbuf_tensor_at("ffa", [128, 11, SEQ], BF16, offset=mrg_off)
    PRM = A.alloc("prm", [128, NL, NPRM], F32)
    ONES = A.alloc("ones", [128, 128], BF16)
    PERM = A.alloc("perm", [128, 128], BF16)
    ESK = A.alloc("esk", [128, 4], F32)
    JI = A.alloc("ji", [128, SCH], F32)
    IDN = A.alloc("idn", [128, 128], F32)
    ONESF = A.alloc("onesf", [128, 128], F32)

    psc = [0]

    def nb(n=1):
        i = psc[0] % 8
        psc[0] += 1
        return i

    wsc = [0]

    def wload(views, rshape):
        s = wsc[0] % W_SLOTS
        wsc[0] += 1

        def fn(e, s=s, views=views):
            out = []
            for (c0, a, b, src) in views:
                dst = WS[s][:, c0:c0 + a * b].rearrange("p (a b) -> p a b", a=a)
                for ai in range(a):
                    out.append(e.dma_start(out=dst[:, ai, :], in_=src[:, ai, :]))
            return out

        S.op("pool", fn, writes=[("w", s)], dma=True, chan=f"w{s}", ndma=sum(v[1] for v in views))
        return s

    def wview(s, a, b, c0=0):
        return WS[s][:, c0:c0 + a * b].rearrange("p (a b) -> p a b", a=a)

    def tbs(tb):
        return slice(tb * TBW, (tb + 1) * TBW)

    for kt in range(NKT):
        S.op("sp", lambda e, kt=kt: e.dma_start(out=xT[:, kt, :], in_=xT_d[kt * 128:(kt + 1) * 128, :]),
             writes=[("x", kt, tb) for tb in range(NTB)], dma=True, chan=f"x{kt}")
    S.op("sp", lambda e: e.dma_start(out=PRM[:], in_=prm_d.rearrange("l p c -> p l c")), writes=["prm"], dma=True,
         chan="prm")
    S.op("sp", lambda e: e.dma_start(out=JI[:], in_=cst_d[:, 1152 + 2 * SEQ:1152 + 2 * SEQ + SCH]), writes=["ji"], dma=True,
         chan="ji")
    S.op("sp", lambda e: e.dma_start(out=IDN[:], in_=cst_d[:, CSTW - 128:CSTW]), writes=["idn"], dma=True, chan="idn")
    S.op("dve", lambda e: e.memset(ONESF[:], 1.0), writes=["onesf"])
    S.op("dve", lambda e: e.memset(ONES[:], 1.0), writes=["ones"])
    S.op("pool", lambda e: e.dma_start(out=PERM[:], in_=cst_d[:, 1024:1152]), writes=["perm"], dma=True, chan="perm")

    def rmsnorm_to_h(l, gcol, name):
        nb0 = A.offs["br"] + 3 * SEQ * 2
        SQ = [nc.alloc_sbuf_tensor_at(f"nsq{i}_{name}_l{l}", [128, TBW], BF16, offset=nb0 + i * TBW * 2) for i in range(2)]
        MS = [nc.alloc_sbuf_tensor_at(f"nms_{name}_l{l}", [128, TBW], F32, offset=nb0 + 2 * TBW * 2)] * 2
        for tb in range(NTB):
            pi = nb()
            for kt in range(NKT):
                q = (tb * NKT + kt) % 2
                S.op("act", lambda e, q=q, kt=kt, tb=tb: e.activation(out=SQ[q][:], in_=xT[:, kt, tbs(tb)], func=AF.Square),
                     reads=[("x", kt, tb)], writes=[("sq", q)])
                S.op("pe", lambda e, q=q, kt=kt, pi=pi: e.matmul(PS[:, pi, :], lhsT=ONES[:], rhs=SQ[q][:], start=(kt == 0),
                                                                stop=(kt == NKT - 1)),
                     reads=[("sq", q), "ones"], writes=[("ps", pi)])
            m = 0
            S.op("dve", lambda e, m=m, pi=pi: e.tensor_scalar(out=MS[m][:], in0=PS[:, pi, :], scalar1=1.0 / D, scalar2=EPS,
                                                             op0=ALU.mult, op1=ALU.add),
                 reads=[("ps", pi)], writes=[("ms", m)])
            S.op("act", lambda e, m=m: e.activation(out=MS[m][:], in_=MS[m][:], func=AF.Sqrt), reads=[("ms", m)],
                 writes=[("ms", m)])
            S.op("dve", lambda e, m=m: e.reciprocal(out=MS[m][:], in_=MS[m][:]), reads=[("ms", m)], writes=[("ms", m)])
            for kt in range(NKT):
                S.op("dve", lambda e, m=m, kt=kt, tb=tb: e.scalar_tensor_tensor(
                    out=hT[:, kt, tbs(tb)], in0=xT[:, kt, tbs(tb)], scalar=PRM[:, l, gcol + kt:gcol + kt + 1], in1=MS[m][:],
                    op0=ALU.mult, op1=ALU.mult),
                     reads=[("x", kt, tb), ("ms", m), "prm"], writes=[("h", kt, tb)])

    def proj_group(pi, s, c0, tb, ncols_slot=512):
        wv = wview(s, NKT, ncols_slot)

        def fn(e):
            ins = None
            for kt in range(NKT):
                ins = e.matmul(PS[:, pi, :], lhsT=wv[:, kt, c0:c0 + 128], rhs=hT[:, kt, tbs(tb)], start=(kt == 0),
                               stop=(kt == NKT - 1))
            return ins

        S.op("pe", fn, reads=[("w", s)] + [("h", kt, tb) for kt in range(NKT)], writes=[("ps", pi)])

    def win_view(l, c0, ncols):
        return w_in_d[l].rearrange("(kt p) n -> p kt n", p=128)[:, :, c0:c0 + ncols]

    def merge_branch(l, b, wo_d):
        A.push()
        SG = [A.alloc("sg", [128, TBW], F32) for _ in range(2)]
        TMP = [A.alloc("tmp", [128, TBW], F32) for _ in range(2)]
        so = wload([(0, 4, 1024, wo_d[l].rearrange("(kt p) n -> p kt n", p=128))], None)
        wo = wview(so, 4, 1024)
        for half in range(2):
            sg_ = wload([(0, NKT, 512, win_view(l, 2816 + b * 1024 + half * 512, 512))], None)
            for fl in range(4):
                f = half * 4 + fl
                for tb in range(NTB):
                    py = nb()

                    def fy(e, py=py, f=f, tb=tb):
                        ins = None
                        for kt in range(4):
                            ins = e.matmul(PS[:, py, :], lhsT=wo[:, kt, f * 128:(f + 1) * 128], rhs=BR[:, kt, tbs(tb)],
                                           start=(kt == 0), stop=(kt == 3))
                        return ins

                    S.op("pe", fy, reads=[("w", so)] + [("br", kt, tb) for kt in range(4)], writes=[("ps", py)])
                    pg = nb()
                    proj_group(pg, sg_, fl * 128, tb)
                    q = (f * NTB + tb) % 2
                    S.op("act", lambda e, q=q, pg=pg, f=f: e.activation(out=SG[q][:], in_=PS[:, pg, :], func=AF.Sigmoid,
                                                                         bias=PRM[:, l, P_BG + b * 8 + f:P_BG + b * 8 + f + 1]),
                         reads=[("ps", pg), "prm"], writes=[("sg", q)])
                    if b == 0:
                        S.op("dve", lambda e, q=q, py=py, f=f, tb=tb: e.tensor_tensor(out=MRG[:, f, tbs(tb)], in0=PS[:, py, :],
                                                                                      in1=SG[q][:], op=ALU.mult),
                             reads=[("ps", py), ("sg", q)], writes=[("mrg", f, tb)])
                    else:
                        S.op("dve", lambda e, q=q, py=py: e.tensor_tensor(out=TMP[q][:], in0=PS[:, py, :], in1=SG[q][:],
                                                                          op=ALU.mult),
                             reads=[("ps", py), ("sg", q)], writes=[("tmp", q)])
                        S.op("dve", lambda e, q=q, f=f, tb=tb: e.tensor_tensor(out=MRG[:, f, tbs(tb)], in0=MRG[:, f, tbs(tb)],
                                                                               in1=TMP[q][:], op=ALU.add),
                             reads=[("tmp", q), ("mrg", f, tb)], writes=[("mrg", f, tb)])
        S.barrier()
        A.pop()

    def layer(l):
        S.enabled = DBG >= 1
        rmsnorm_to_h(l, P_GMIX, "n1")

        A.push()

        S.enabled = DBG >= 2
        A.push()
        Q = nc.alloc_sbuf_tensor_at(f"q_l{l}", [128, 4, SEQ], BF16, offset=mrg_off)
        Kt = A.alloc("k", [128, SEQ], BF16)
        V = A.alloc("v", [128, 16, 128], BF16)
        A.push()
        COS = [A.alloc("cos", [128, TBW], BF16) for _ in range(2)]
        SIN = [A.alloc("sin", [128, TBW], BF16) for _ in range(2)]
        QR = [A.alloc("qr", [128, TBW], BF16) for _ in range(2)]
        T1 = A.alloc("t1", [128, TBW], F32)
        T2 = A.alloc("t2", [128, TBW], F32)
        sq_ = wload([(0, NKT, 512, win_view(l, 0, 512))], None)
        skv = wload([(0, NKT, 256, win_view(l, 512, 256))], None)

        def rope_block(pi, tb, dst_ap, dst_key, cnt):
            q = cnt % 2
            cq = tb % 2
            if DBG < 2.1:
                return
            S.op("dve", lambda e: e.tensor_copy(out=QR[q][:], in_=PS[:, pi, :]), reads=[("ps", pi)],
                 writes=[("qr", q)])
            p2 = nb()
            S.op("pe", lambda e: e.matmul(PS[:, p2, :], lhsT=PERM[:], rhs=QR[q][:], start=True, stop=True),
                 reads=["perm", ("qr", q)], writes=[("ps", p2)])
            S.op("dve", lambda e: e.tensor_tensor(out=T1[:], in0=PS[:, pi, :], in1=COS[cq][:], op=ALU.mult),
                 reads=[("ps", pi), ("cos", cq)], writes=["t1"])
            S.op("dve", lambda e: e.tensor_tensor(out=T2[:], in0=PS[:, p2, :], in1=SIN[cq][:], op=ALU.mult),
                 reads=[("ps", p2), ("sin", cq)], writes=["t2"])
            S.op("dve", lambda e: e.tensor_tensor(out=dst_ap, in0=T1[:], in1=T2[:], op=ALU.add), reads=["t1", "t2"],
                 writes=[dst_key])

        cnt = 0
        for tb in range(NTB):
            cq = tb % 2
            S.op("pool", lambda e, tb=tb, cq=cq: e.dma_start(out=COS[cq][:], in_=cst_d[:, 1152 + tb * TBW:1152 + (tb + 1) * TBW]),
                 writes=[("cos", cq)], dma=True, chan=f"cos{cq}")
            S.op("pool", lambda e, tb=tb, cq=cq: e.dma_start(out=SIN[cq][:], in_=cst_d[:, 1152 + SEQ + tb * TBW:1152 + SEQ + (tb + 1) * TBW]),
                 writes=[("sin", cq)], dma=True, chan=f"sin{cq}")
            for g in range(4):
                pi = nb()
                proj_group(pi, sq_, g * 128, tb)
                rope_block(pi, tb, Q[:, g, tbs(tb)], ("q", g, tb), cnt)
                cnt += 1
            pi = nb()
            proj_group(pi, skv, 0, tb, ncols_slot=256)
            rope_block(pi, tb, Kt[:, tbs(tb)], ("k", tb), cnt)
            cnt += 1
        S.enabled = DBG >= 2.2
        kvv = wview(skv, NKT, 256)
        for t4 in range(4):
            pi = nb()

            def fv(e, pi=pi, t4=t4):
                ins = None
                for j in range(4):
                    tt = t4 * 4 + j
                    for kt in range(NKT):
                        ins = e.matmul(PS[:, pi, j * 128:(j + 1) * 128], lhsT=hT[:, kt, tt * 128:(tt + 1) * 128],
                                       rhs=kvv[:, kt, 128:256], start=(kt == 0), stop=(kt == NKT - 1))
                return ins

            S.op("pe", fv, reads=[("w", skv)] + [("h", kt, t4) for kt in range(NKT)], writes=[("ps", pi)])
            S.op("dve", lambda e, pi=pi, t4=t4: e.tensor_copy(out=V[:, t4 * 4:(t4 + 1) * 4, :],
                                                               in_=PS[:, pi, :].rearrange("p (a b) -> p a b", a=4)),
                 reads=[("ps", pi)], writes=[("v", t4)])
        S.barrier()
        A.pop()
        S.enabled = DBG >= 2.5
        MK = A.alloc("mk", [128, 2, 512], BF16)
        IDB = A.alloc("idb", [128, 128], BF16)
        PB = [A.alloc("pb", [128, TBW], BF16) for _ in range(8)]
        DEN = [A.alloc("den", [128, TBW], F32) for _ in range(1)]
        S.op("pool", lambda e: e.dma_start(out=MK[:], in_=cst_d[:, 0:1024].rearrange("p (a b) -> p a b", a=2)),
             writes=["mk"], dma=True, chan="mk")
        S.op("dve", lambda e: e.tensor_copy(out=IDB[:], in_=IDN[:]), reads=["idn"], writes=["idb"])
        S.op("act", lambda e: e.activation(out=ESK[:], in_=PRM[:, l, P_SINK:P_SINK + 4], func=AF.Exp), reads=["prm"],
             writes=["esk"])
        pcnt = [0]

        def att_s(qb):
            qs = slice(qb * 128, (qb + 1) * 128)
            pbs = {}
            kts = [qb] if qb == 0 else [qb - 1, qb]
            k_ = 0
            for kvh in range(2):
                hs = slice(kvh * 64, (kvh + 1) * 64)
                for kt_ in kts:
                    pa = (k_ if qb else 2 * k_) % 4
                    k_ += 1
                    ks = slice(kt_ * 128, (kt_ + 1) * 128)
                    mi = 0 if kt_ == qb else 1

                    def fs(e, pa=pa, hs=hs, ks=ks, qs=qs, kvh=kvh, mi=mi):
                        e.matmul(PS[:, pa, :], lhsT=IDB[:], rhs=MK[:, mi, :], start=True, stop=False)
                        return e.matmul(PS[:, pa, :].rearrange("p (a b) -> p a b", a=4), lhsT=Kt[hs, ks], rhs=Q[hs, :, qs],
                                        start=False, stop=True, tile_position=(kvh * 64, 0))

                    S.op("pe", fs, reads=[("k", kt_ // 4), "idb", "mk"] + [("q", g, qb // 4) for g in range(4)],
                         writes=[("ps", pa)])
                    pq = pcnt[0] % 8
                    pcnt[0] += 1
                    S.op("act", lambda e, pa=pa, pq=pq: e.activation(out=PB[pq][:], in_=PS[:, pa, :], func=AF.Exp, scale=0.125),
                         reads=[("ps", pa)], writes=[("pb", pq)])
                    pbs[(kvh, kt_)] = pq
            return pbs

        def att_pv(qb, pbs):
            qs = slice(qb * 128, (qb + 1) * 128)
            po = 4 + 2 * (qb % 2)
            pd = po + 1
            kts = [qb] if qb == 0 else [qb - 1, qb]
            n_ = len(kts)
            for kvh in range(2):
                hs = slice(kvh * 64, (kvh + 1) * 64)
                for i_, kt_ in enumerate(kts):
                    pq = pbs[(kvh, kt_)]
                    S.op("pe", lambda e, pq=pq, hs=hs, kt_=kt_, i_=i_, n_=n_, kvh=kvh, po=po: e.matmul(
                        PS[hs, po, :], lhsT=V[:, kt_, hs], rhs=PB[pq][:], start=(i_ == 0), stop=(i_ == n_ - 1),
                        tile_position=(0, kvh * 64)),
                         reads=[("pb", pq), ("v", kt_ // 4)], writes=[("ps", po)])
                    S.op("pe", lambda e, pq=pq, hs=hs, i_=i_, n_=n_, kvh=kvh, pd=pd: e.matmul(
                        PS[hs, pd, :], lhsT=ONES[:, 0:64], rhs=PB[pq][:], start=(i_ == 0), stop=(i_ == n_ - 1),
                        tile_position=(0, kvh * 64)),
                         reads=[("pb", pq), "ones"], writes=[("ps", pd)])
            dq = 0
            S.op("dve", lambda e, dq=dq, pd=pd: e.tensor_tensor(
                out=DEN[dq][:].rearrange("p (a b) -> p a b", a=4), in0=PS[:, pd, :].rearrange("p (a b) -> p a b", a=4),
                in1=ESK[:].unsqueeze(2).to_broadcast([128, 4, 128]), op=ALU.add),
                 reads=[("ps", pd), "esk"], writes=[("den", dq)])
            S.op("dve", lambda e, dq=dq: e.reciprocal(out=DEN[dq][:], in_=DEN[dq][:]), reads=[("den", dq)],
                 writes=[("den", dq)])
            S.op("dve", lambda e, dq=dq, po=po, qs=qs: e.tensor_tensor(
                out=BR[:, :, qs], in0=PS[:, po, :].rearrange("p (a b) -> p a b", a=4),
                in1=DEN[dq][:].rearrange("p (a b) -> p a b", a=4), op=ALU.mult),
                 reads=[("ps", po), ("den", dq)], writes=[("br", g, qb // 4) for g in range(4)])

        prev_pbs = att_s(0)
        for qb in range(16):
            nxt = att_s(qb + 1) if qb + 1 < 16 else None
            att_pv(qb, prev_pbs)
            prev_pbs = nxt
        S.barrier()
        A.pop()
        S.enabled = DBG >= 2.8
        merge_branch(l, 0, w_ao_d)

        S.enabled = DBG >= 3
        A.push()
        Z = A.alloc("z", [128, SEQ + 2], F32)
        Y1 = [A.alloc("y1", [128, TBW], F32) for _ in range(2)]
        S.op("dve", lambda e: e.memset(Z[:, 0:2], 0.0), writes=["z0"])
        scc = wload([(0, NKT, 512, win_view(l, 1280, 512))], None)
        for f in range(4):
            for tb in range(NTB):
                pi = nb()
                proj_group(pi, scc, f * 128, tb)
                S.op("dve", lambda e, pi=pi, f=f, tb=tb: e.tensor_copy(out=BR[:, f, tbs(tb)], in_=PS[:, pi, :]),
                     reads=[("ps", pi)], writes=[("br", f, tb)])
        scx = wload([(0, NKT, 512, win_view(l, 1792, 512))], None)
        for f in range(4):
            for tb in range(NTB):
                pi = nb()
                proj_group(pi, scx, f * 128, tb)
                a0 = tb * TBW
                S.op("dve", lambda e, pi=pi, f=f, tb=tb, a0=a0: e.tensor_tensor(out=Z[:, 2 + a0:2 + a0 + TBW], in0=PS[:, pi, :],
                                                                               in1=BR[:, f, tbs(tb)], op=ALU.mult),
                     reads=[("ps", pi), ("br", f, tb)], writes=[("z", tb)])
                yq = tb % 2
                cw = lambda j, f=f: PRM[:, l, P_CONVW + j * 4 + f:P_CONVW + j * 4 + f + 1]

                def fconv(e, a0=a0, yq=yq, f=f, tb=tb, cw=cw):
                    e.tensor_scalar(out=Y1[yq][:], in0=Z[:, 2 + a0:2 + a0 + TBW], scalar1=cw(2), scalar2=None, op0=ALU.mult)
                    e.scalar_tensor_tensor(out=Y1[yq][:], in0=Z[:, 1 + a0:1 + a0 + TBW], scalar=cw(1), in1=Y1[yq][:],
                                           op0=ALU.mult, op1=ALU.add)
                    return e.scalar_tensor_tensor(out=BR[:, f, tbs(tb)], in0=Z[:, a0:a0 + TBW], scalar=cw(0), in1=Y1[yq][:],
                                                  op0=ALU.mult, op1=ALU.add)

                S.op("dve", fconv, reads=[("z", tb), ("z", tb - 1), "z0", "prm"], writes=[("y1", yq), ("br", f, tb)])
        scb = wload([(0, NKT, 512, win_view(l, 768, 512))], None)
        for f in range(4):
            for tb in range(NTB):
                pi = nb()
                proj_group(pi, scb, f * 128, tb)
                S.op("dve", lambda e, pi=pi, f=f, tb=tb: e.tensor_tensor(out=BR[:, f, tbs(tb)], in0=PS[:, pi, :],
                                                                        in1=BR[:, f, tbs(tb)], op=ALU.mult),
                     reads=[("ps", pi), ("br", f, tb)], writes=[("br", f, tb)])
        S.barrier()
        A.pop()
        merge_branch(l, 1, w_co_d)

        S.enabled = DBG >= 4
        A.push()
        WBU = [A.alloc("wbu", [128, 2048], BF16) for _ in range(2)]
        CW = A.alloc("cw", [128, 16, 2, 64], BF16)
        DIAGD = A.alloc("diagd", [128, 4, 128], BF16)
        L1 = A.alloc("l1", [128, 2, 16], F32)
        L2 = A.alloc("l2", [128, 2, 16], F32)
        XC = A.alloc("xc", [128, 2, 16], F32)
        su = wload([(0, NKT, 512, win_view(l, 2304, 512))], None)
        for ct in range(4):
            for tb in range(NTB):
                pi = nb()
                proj_group(pi, su, ct * 128, tb)
                S.op("dve", lambda e, pi=pi, ct=ct, tb=tb: e.tensor_copy(out=BR[:, ct, tbs(tb)], in_=PS[:, pi, :]),
                     reads=[("ps", pi)], writes=[("br", ct, tb)])

        S.barrier()
        def coeffs(eng_name, are, aim, ldt, shape, tmps, key):
            dt_, lr, li, t3, t4, qr, qi, t7 = tmps[:8]
            rd = [key + "_in"]
            wr = [key]
            PI = math.pi
            S.op("act", lambda e: e.activation(out=dt_, in_=ldt, func=AF.Exp), reads=rd, writes=wr)

            ti = tmps[8]

            def red(e, x):
                e.tensor_scalar(out=dt_, in0=x, scalar1=1.0 / (2 * PI), scalar2=0.5, op0=ALU.mult, op1=ALU.add)
                e.tensor_copy(out=ti, in_=dt_)
                e.tensor_copy(out=dt_, in_=ti)
                e.scalar_tensor_tensor(out=x, in0=dt_, scalar=-2 * PI, in1=x, op0=ALU.mult, op1=ALU.add)
                e.tensor_scalar(out=dt_, in0=x, scalar1=-PI, scalar2=2 * PI, op0=ALU.is_lt, op1=ALU.mult)
                e.tensor_tensor(out=x, in0=x, in1=dt_, op=ALU.add)
                e.tensor_scalar(out=dt_, in0=x, scalar1=PI, scalar2=-2 * PI, op0=ALU.is_gt, op1=ALU.mult)
                return e.tensor_tensor(out=x, in0=x, in1=dt_, op=ALU.add)

            def f1(e):
                e.tensor_tensor(out=t3, in0=are, in1=dt_, op=ALU.mult)
                e.tensor_tensor(out=t4, in0=aim, in1=dt_, op=ALU.mult)
                e.tensor_scalar(out=t7, in0=t4, scalar1=0.5 * PI, scalar2=None, op0=ALU.add)
                red(e, t7)
                return red(e, t4)

            S.op(eng_name, f1, reads=wr, writes=wr)

            def f3(e):
                e.activation(out=t3, in_=t3, func=AF.Exp)
                e.activation(out=t7, in_=t7, func=AF.Sin)
                return e.activation(out=t4, in_=t4, func=AF.Sin)

            S.op("act", f3, reads=wr, writes=wr)

            def f4(e):
                e.tensor_tensor(out=lr, in0=t3, in1=t7, op=ALU.mult)
                e.tensor_tensor(out=li, in0=t3, in1=t4, op=ALU.mult)
                e.tensor_scalar(out=t3, in0=lr, scalar1=-1.0, scalar2=None, op0=ALU.add)
                e.tensor_tensor(out=t4, in0=are, in1=are, op=ALU.mult)
                e.tensor_tensor(out=t7, in0=aim, in1=aim, op=ALU.mult)
                e.tensor_tensor(out=t4, in0=t4, in1=t7, op=ALU.add)
                e.reciprocal(out=t4, in_=t4)
                e.tensor_tensor(out=qr, in0=t3, in1=are, op=ALU.mult)
                e.tensor_tensor(out=t7, in0=li, in1=aim, op=ALU.mult)
                e.tensor_tensor(out=qr, in0=qr, in1=t7, op=ALU.add)
                e.tensor_tensor(out=qr, in0=qr, in1=t4, op=ALU.mult)
                e.tensor_tensor(out=qi, in0=li, in1=are, op=ALU.mult)
                e.tensor_tensor(out=t7, in0=t3, in1=aim, op=ALU.mult)
                e.tensor_tensor(out=qi, in0=qi, in1=t7, op=ALU.subtract)
                return e.tensor_tensor(out=qi, in0=qi, in1=t4, op=ALU.mult)

            S.op(eng_name, f4, reads=wr, writes=wr)
            return lr, li, qr, qi

        A.push()
        TA = [A.alloc("ta", [128, 16], F32) for _ in range(8)] + [A.alloc("tai", [128, 16], mybir.dt.int32)]
        lrA, liA, qrA, qiA = coeffs("dve", PRM[:, l, P_AA:P_AA + 16], PRM[:, l, P_AI:P_AI + 16], PRM[:, l, P_LDT:P_LDT + 16],
                                [128, 16], [t[:] for t in TA], "cfA")

        def fL(e):
            e.tensor_copy(out=L1[:, 0, :], in_=lrA)
            e.tensor_copy(out=L1[:, 1, :], in_=lrA)
            e.tensor_scalar(out=L2[:, 0, :], in0=liA, scalar1=-1.0, scalar2=None, op0=ALU.mult)
            e.tensor_copy(out=L2[:, 1, :], in_=liA)
            return e.memset(XC[:], 0.0)

        S.op("dve", fL, reads=["cfA"], writes=["L", "xc"])
        wbase = A.offs["ws0"]
        U4 = 16 * SCH * 4
        mkb = lambda nm, off, dt=F32, shape=None: nc.alloc_sbuf_tensor_at(f"{nm}_l{l}", shape or [128, 16, SCH], dt,
                                                                          offset=wbase + off)
        CJ = mkb("cj", 0, BF16)
        SJ = mkb("sj", U4 // 2, BF16)
        DEC = mkb("dec", U4)
        T1s = mkb("t1s", 2 * U4)
        T2s = mkb("t2s", 3 * U4)
        T3s = mkb("t3s", 4 * U4)
        XB = mkb("xb", 5 * U4, BF16, [128, 2, 16, SCH])
        TIs = mkb("tis", 5 * U4, mybir.dt.int32)
        PHI = A.alloc("phi", [128, 16], F32)
        RR = A.alloc("rr", [128, 16], F32)
        S.op("act", lambda e: e.activation(out=PHI[:], in_=PRM[:, l, P_LDT:P_LDT + 16], func=AF.Exp), reads=["prm"],
             writes=["tabp"])

        def ft1(e):
            e.tensor_tensor(out=RR[:], in0=PRM[:, l, P_AA:P_AA + 16], in1=PHI[:], op=ALU.mult)
            return e.tensor_tensor(out=PHI[:], in0=PRM[:, l, P_AI:P_AI + 16], in1=PHI[:], op=ALU.mult)

        S.op("dve", ft1, reads=["tabp", "prm"], writes=["tabp"])
        S.op("act", lambda e: e.activation(out=RR[:], in_=RR[:], func=AF.Exp), reads=["tabp"], writes=["tabp"])

        def ft2(e):
            e.tensor_tensor(out=T2s[:], in0=PHI[:].unsqueeze(2).to_broadcast([128, 16, SCH]),
                            in1=JI[:].unsqueeze(1).to_broadcast([128, 16, SCH]), op=ALU.mult)
            e.tensor_scalar(out=T3s[:], in0=T2s[:], scalar1=0.5 * math.pi, scalar2=None, op0=ALU.add)
            red_angle(e, T2s[:], T1s[:], TIs[:])
            red_angle(e, T3s[:], T1s[:], TIs[:])
            e.tensor_copy(out=DEC[:], in_=RR[:].unsqueeze(2).to_broadcast([128, 16, SCH]))
            return e.memset(DEC[:, :, 0:1], 0.0)

        S.op("dve", ft2, reads=["tabp", "ji"], writes=["tab"])

        def ft3(e):
            e.activation(out=SJ[:], in_=T2s[:], func=AF.Sin)
            return e.activation(out=CJ[:], in_=T3s[:], func=AF.Sin)

        S.op("act", ft3, reads=["tab"], writes=["tab"])
        def fCd(e):
            return [e.dma_start(out=CW[:, :, 0, :], in_=ssmC_d[l][:, 0, :].rearrange("p (a b) -> p a b", a=16)),
                    e.dma_start(out=CW[:, :, 1, :], in_=ssmC_d[l][:, 1, :].rearrange("p (a b) -> p a b", a=16))]

        S.op("pool", fCd, writes=["cw"], dma=True, chan="cw", ndma=2)
        S.op("pool", lambda e: e.tensor_scalar(out=CW[:, :, 1, :], in0=CW[:, :, 1, :], scalar1=-1.0, scalar2=None,
                                               op0=ALU.mult), reads=["cw"], writes=["cw"])
        def fdd(e):
            ins = None
            for ct in range(4):
                ins = e.tensor_scalar(out=DIAGD[:, ct, :], in0=IDN[:], scalar1=PRM[:, l, P_SSMD + ct:P_SSMD + ct + 1],
                                      scalar2=None, op0=ALU.mult)
            return ins

        S.op("dve", fdd, reads=["idn", "prm"], writes=["diagd"])
        DG = A.alloc("dg", [128, 4, 128], F32)
        BB = A.alloc("bb", [128, 2, 512], F32)
        TQ = [A.alloc("tq", [128, 512], F32) for _ in range(2)]
        for qt in range(4):
            S.op("sp", lambda e, qt=qt: e.dma_start(out=BB[:], in_=ssmB_d[l][:, :, qt * 512:(qt + 1) * 512]),
                 writes=["bb"], dma=True, chan="bb")
            for qi_, qsrc in enumerate((qrA, qiA)):
                def fdg(e, qsrc=qsrc, qt=qt):
                    ins = None
                    for j in range(4):
                        ins = e.tensor_scalar(out=DG[:, j, :], in0=IDN[:], scalar1=qsrc[:, qt * 4 + j:qt * 4 + j + 1],
                                              scalar2=None, op0=ALU.mult)
                    return ins

                S.op("dve", fdg, reads=["cfA", "idn"], writes=["dg"])

                def fqb(e, qi_=qi_):
                    ins = None
                    for j in range(4):
                        ins = e.matmul(PS[:, qi_, j * 128:(j + 1) * 128], lhsT=ONESF[:], rhs=DG[:, j, :], start=True, stop=True)
                    return ins

                S.op("pe", fqb, reads=["dg", "onesf"], writes=[("ps", qi_)])

            def fB(e, qt=qt):
                hs_ = slice(qt * 512, (qt + 1) * 512)
                QBr = PS[:, 0, :]
                QBi = PS[:, 1, :]
                e.tensor_tensor(out=TQ[0][:], in0=QBr, in1=BB[:, 0, :], op=ALU.mult)
                e.tensor_tensor(out=TQ[1][:], in0=QBi, in1=BB[:, 1, :], op=ALU.mult)
                e.tensor_tensor(out=WBU[0][:, hs_], in0=TQ[0][:], in1=TQ[1][:], op=ALU.subtract)
                e.tensor_tensor(out=TQ[0][:], in0=QBr, in1=BB[:, 1, :], op=ALU.mult)
                e.tensor_tensor(out=TQ[1][:], in0=QBi, in1=BB[:, 0, :], op=ALU.mult)
                return e.tensor_tensor(out=WBU[1][:, hs_], in0=TQ[0][:], in1=TQ[1][:], op=ALU.add)

            S.op("dve", fB, reads=[("ps", 0), ("ps", 1), "bb"], writes=[("wbu", k) for k in range(8)] + ["tq"])
        S.barrier()
        A.pop()

        A.push()
        XS2 = [A.alloc("xs", [128, 2, 16, SCH], F32) for _ in range(2)]
        TM1 = A.alloc("tm1", [128, 2, 16], F32)
        TM2 = A.alloc("tm2", [128, 2, 16], F32)
        G1 = A.alloc("g1", [128, 4, SCH], F32)
        flat2 = lambda ap: ap.rearrange("p a b -> p (a b)")
        PSBU = [("ps", 4), ("ps", 5), ("ps", 6), ("ps", 7)]

        def stage_a1(c):
            cs_ = slice(c * SCH, (c + 1) * SCH)
            tbk = (c * SCH) // TBW
            xk = c % 2
            XS = XS2[xk]

            def fbu(e, cs_=cs_):
                ins = None
                for pr in range(16):
                    ct = pr // 4
                    for ri in range(2):
                        o0 = (ri * 16 + pr) * SCH
                        bank = 4 + o0 // 512
                        oo = o0 % 512
                        ins = e.matmul(PS[:, bank, oo:oo + SCH], lhsT=WBU[ri][:, pr * 128:(pr + 1) * 128], rhs=BR[:, ct, cs_],
                                       start=True, stop=True)
                return ins

            S.op("pe", fbu, reads=[("wbu", ct) for ct in range(8)] + [("br", ct, tbk) for ct in range(4)], writes=PSBU)
            S.op("act", lambda e: e.activation(out=XS[:].rearrange("p a b c -> p (a b c)"),
                                               in_=PS[:, 4:8, :].rearrange("p a b -> p (a b)"), func=AF.Identity),
                 reads=PSBU, writes=[("xs_re", xk), ("xs_im", xk)])

        def stage_a2(c):
            xk = c % 2
            XS = XS2[xk]
            BRe = XS[:, 0, :, :]
            BIm = XS[:, 1, :, :]
            kre, kim = ("xs_re", xk), ("xs_im", xk)

            def fscan(e):
                e.tensor_tensor(out=T1s[:], in0=BRe, in1=CJ[:], op=ALU.mult)
                e.nosync()
                e.tensor_tensor(out=T2s[:], in0=BIm, in1=SJ[:], op=ALU.mult)
                e.tensor_tensor(out=T1s[:], in0=T1s[:], in1=T2s[:], op=ALU.add)
                e.tensor_tensor(out=T2s[:], in0=BRe, in1=SJ[:], op=ALU.mult)
                e.nosync()
                e.tensor_tensor(out=BIm, in0=BIm, in1=CJ[:], op=ALU.mult)
                e.tensor_tensor(out=BIm, in0=BIm, in1=T2s[:], op=ALU.subtract)
                e.nosync()
                e.tensor_tensor(out=TM1[:], in0=XC[:], in1=L1[:], op=ALU.mult)
                e.nosync()
                e.tensor_tensor(out=TM2[:], in0=XC[:, ::-1, :], in1=L2[:], op=ALU.mult)
                e.tensor_tensor(out=TM1[:], in0=TM1[:], in1=TM2[:], op=ALU.add)
                e.tensor_tensor(out=T1s[:, :, 0], in0=T1s[:, :, 0], in1=TM1[:, 0, :], op=ALU.add)
                e.nosync()
                e.tensor_tensor(out=BIm[:, :, 0], in0=BIm[:, :, 0], in1=TM1[:, 1, :], op=ALU.add)
                e.tensor_tensor_scan(out=flat2(BRe), data0=flat2(DEC[:]), data1=flat2(T1s[:]), initial=0.0, op0=ALU.mult,
                                     op1=ALU.add)
                e.nosync()
                e.tensor_tensor_scan(out=flat2(T2s[:]), data0=flat2(DEC[:]), data1=flat2(BIm), initial=0.0, op0=ALU.mult,
                                     op1=ALU.add)
                e.tensor_tensor(out=T1s[:], in0=T2s[:], in1=SJ[:], op=ALU.mult)
                e.nosync()
                e.tensor_tensor(out=BIm, in0=BRe, in1=CJ[:], op=ALU.mult)
                e.tensor_tensor(out=XB[:, 0, :, :], in0=BIm, in1=T1s[:], op=ALU.subtract)
                e.nosync()
                e.tensor_tensor(out=XC[:, 0, :], in0=BIm[:, :, SCH - 1], in1=T1s[:, :, SCH - 1], op=ALU.subtract)
                e.tensor_tensor(out=T1s[:], in0=BRe, in1=SJ[:], op=ALU.mult)
                e.nosync()
                e.tensor_tensor(out=BIm, in0=T2s[:], in1=CJ[:], op=ALU.mult)
                e.tensor_tensor(out=XC[:, 1, :], in0=BIm[:, :, SCH - 1], in1=T1s[:, :, SCH - 1], op=ALU.add)
                e.nosync()
                return e.tensor_tensor(out=XB[:, 1, :, :], in0=BIm, in1=T1s[:], op=ALU.add)

            S.op("dve", fscan, reads=[kre, kim, "L", "xc", "tab"], writes=[kre, kim, "t1", "t2", "xc", "xb"])
            py = nb() % 4

            cs_ = slice(c * SCH, (c + 1) * SCH)
            tbk = (c * SCH) // TBW

            def fcm(e, py=py, cs_=cs_):
                ins = None
                for ct in range(4):
                    e.matmul(PS[:, py, ct * SCH:(ct + 1) * SCH], lhsT=DIAGD[:, ct, :], rhs=BR[:, ct, cs_], start=True, stop=False)
                    for half in range(2):
                        k = 0
                        for pl in range(2):
                            pr = ct * 4 + half * 2 + pl
                            for ri in range(2):
                                ins = e.matmul(PS[half * 64:(half + 1) * 64, py, ct * SCH:(ct + 1) * SCH],
                                               lhsT=CW[:, pr, ri, :], rhs=XB[:, ri, pr, :], start=False, stop=(k == 3),
                                               tile_position=(0, half * 64))
                                k += 1
                return ins

            S.op("pe", fcm, reads=["xb", "cw", "diagd"] + [("br", ct, tbk) for ct in range(4)], writes=[("ps", py)])
            return py

        def stage_b(c, py):
            cs_ = slice(c * SCH, (c + 1) * SCH)
            tbk = (c * SCH) // TBW

            PY = PS[:, py, 0:4 * SCH].rearrange("p (a b) -> p a b", a=4)
            S.op("act", lambda e, PY=PY: e.activation(out=G1[:], in_=PY, func=AF.Square), reads=[("ps", py)], writes=["g1"])

            def fys(e, PY=PY):
                e.tensor_scalar(out=G1[:], in0=G1[:], scalar1=0.044715, scalar2=1.0, op0=ALU.mult, op1=ALU.add)
                return e.tensor_tensor(out=G1[:], in0=G1[:], in1=PY, op=ALU.mult)

            S.op("dve", fys, reads=[("ps", py), "g1"], writes=["g1"])
            S.op("act", lambda e: e.activation(out=G1[:], in_=G1[:], func=AF.Sigmoid, scale=2.0 * math.sqrt(2.0 / math.pi)),
                 reads=["g1"], writes=["g1"])
            S.op("dve", lambda e, cs_=cs_, PY=PY: e.tensor_tensor(out=BR[:, :, cs_], in0=PY, in1=G1[:], op=ALU.mult),
                 reads=[("ps", py), "g1"], writes=[("yso", c)])

        pys = {}
        stage_a1(0)
        for c in range(NCH + 1):
            if c + 1 < NCH:
                stage_a1(c + 1)
            if c < NCH:
                pys[c] = stage_a2(c)
            if c >= 1:
                stage_b(c - 1, pys[c - 1])
        S.barrier()
        A.pop()
        SG = [A.alloc("sg", [128, TBW], F32) for _ in range(2)]
        sgl = wload([(0, 4, 512, w_glu_d[l].rearrange("(kt p) n -> p kt n", p=128))], None)
        wg = wview(sgl, 4, 512)
        for tb in range(NTB):
            pis = []
            for f in range(4):
                pi = nb()
                pis.append(pi)

                def fg(e, pi=pi, f=f, tb=tb):
                    ins = None
                    for kt in range(4):
                        ins = e.matmul(PS[:, pi, :], lhsT=wg[:, kt, f * 128:(f + 1) * 128], rhs=BR[:, kt, tbs(tb)],
                                       start=(kt == 0), stop=(kt == 3))
                    return ins

                S.op("pe", fg, reads=[("w", sgl)] + [("br", kt, tb) for kt in range(4)], writes=[("ps", pi)])
            for f in range(4):
                q = f % 2
                S.op("act", lambda e, q=q, pi=pis[f]: e.activation(out=SG[q][:], in_=PS[:, pi, :], func=AF.Sigmoid),
                     reads=[("ps", pis[f])], writes=[("sg", q)])
                S.op("dve", lambda e, q=q, f=f, tb=tb: e.tensor_tensor(out=BR[:, f, tbs(tb)], in0=BR[:, f, tbs(tb)], in1=SG[q][:],
                                                                      op=ALU.mult),
                     reads=[("sg", q), ("br", f, tb)], writes=[("br", f, tb)])
        S.barrier()
        A.pop()
        merge_branch(l, 2, w_so_d)

        S.enabled = DBG >= 5
        for half in range(2):
            sm = wload([(0, NKT, 512, w_mix_d[l].rearrange("(kt p) n -> p kt n", p=128)[:, :, half * 512:(half + 1) * 512])],
                       None)
            wm = wview(sm, NKT, 512)
            for fl in range(4):
                f2 = half * 4 + fl
                for tb in range(NTB):
                    pi = nb()

                    def fm(e, pi=pi, fl=fl, tb=tb, wm=wm):
                        ins = None
                        for kt in range(NKT):
                            ins = e.matmul(PS[:, pi, :], lhsT=wm[:, kt, fl * 128:(fl + 1) * 128], rhs=MRG[:, kt, tbs(tb)],
                                           start=(kt == 0), stop=(kt == NKT - 1))
                        return ins

                    S.op("pe", fm, reads=[("w", sm)] + [("mrg", kt, tb) for kt in range(NKT)], writes=[("ps", pi)])
                    S.op("dve", lambda e, pi=pi, f2=f2, tb=tb: e.tensor_tensor(out=xT[:, f2, tbs(tb)], in0=xT[:, f2, tbs(tb)],
                                                                              in1=PS[:, pi, :], op=ALU.add),
                         reads=[("ps", pi), ("x", f2, tb)], writes=[("x", f2, tb)])
        S.barrier()
        A.pop()

        S.enabled = DBG >= 6
        rmsnorm_to_h(l, P_GFFN, "n2")
        A.push()
        SGT = [A.alloc("sgt", [128, TBW], F32) for _ in range(3)]
        wfi = w_fi_d[l].rearrange("(kt p) n -> p kt n", p=128)
        scnt = 0
        for grp in range(2):
            j0 = grp * 11
            jl = 0
            while jl < 11:
                nj = min(2, 11 - jl)
                j = j0 + jl
                sf = wload([(0, NKT, nj * 128, wfi[:, :, j * 128:(j + nj) * 128]),
                            (NKT * nj * 128, NKT, nj * 128, wfi[:, :, FFH + j * 128:FFH + (j + nj) * 128])], None)
                wgt = wview(sf, NKT, nj * 128, 0)
                wup = wview(sf, NKT, nj * 128, NKT * nj * 128)
                for jj in range(nj):
                    for tb in range(NTB):
                        pg = nb()
                        pu = nb()

                        def fgu(e, pg=pg, pu=pu, jj=jj, tb=tb, wgt=wgt, wup=wup):
                            ins = None
                            for kt in range(NKT):
                                e.matmul(PS[:, pg, :], lhsT=wgt[:, kt, jj * 128:(jj + 1) * 128], rhs=hT[:, kt, tbs(tb)],
                                         start=(kt == 0), stop=(kt == NKT - 1))
                            for kt in range(NKT):
                                ins = e.matmul(PS[:, pu, :], lhsT=wup[:, kt, jj * 128:(jj + 1) * 128], rhs=hT[:, kt, tbs(tb)],
                                               start=(kt == 0), stop=(kt == NKT - 1))
                            return ins

                        S.op("pe", fgu, reads=[("w", sf)] + [("h", kt, tb) for kt in range(NKT)],
                             writes=[("ps", pg), ("ps", pu)])
                        q = scnt % 3
                        scnt += 1
                        S.op("act", lambda e, q=q, pg=pg: e.activation(out=SGT[q][:], in_=PS[:, pg, :], func=AF.Silu),
                             reads=[("ps", pg)], writes=[("sgt", q)])
                        S.op("dve", lambda e, q=q, pu=pu, a=jl + jj, tb=tb: e.tensor_tensor(out=FFA[:, a, tbs(tb)], in0=PS[:, pu, :],
                                                                                           in1=SGT[q][:], op=ALU.mult),
                             reads=[("ps", pu), ("sgt", q)], writes=[("ffa", jl + jj, tb)])
                jl += nj
            for fp in range(4):
                so_ = wload([(0, 11, 256, w_fo_d[l][j0 * 128:(j0 + 11) * 128, fp * 256:(fp + 1) * 256].rearrange(
                    "(j p) n -> p j n", p=128))], None)
                wo_ = wview(so_, 11, 256)
                for fl in range(2):
                    f = fp * 2 + fl
                    for tb in range(NTB):
                        pi = nb()

                        def ffo(e, pi=pi, fl=fl, tb=tb, wo_=wo_):
                            ins = None
                            for a in range(11):
                                ins = e.matmul(PS[:, pi, :], lhsT=wo_[:, a, fl * 128:(fl + 1) * 128], rhs=FFA[:, a, tbs(tb)],
                                               start=(a == 0), stop=(a == 10))
                            return ins

                        S.op("pe", ffo, reads=[("w", so_)] + [("ffa", a, tb) for a in range(11)], writes=[("ps", pi)])
                        S.op("dve", lambda e, pi=pi, f=f, tb=tb: e.tensor_tensor(out=xT[:, f, tbs(tb)], in0=xT[:, f, tbs(tb)],
                                                                                in1=PS[:, pi, :], op=ALU.add),
                             reads=[("ps", pi), ("x", f, tb)], writes=[("x", f, tb)])
        S.barrier()
        A.pop()

    for l_ in range(NL):
        layer(l_)

    S.enabled = True
    A.push()
    SQ = [A.alloc("sq", [128, TBW], BF16) for _ in range(3)]
    MS = [A.alloc("ms", [128, TBW], F32) for _ in range(2)]
    OST = [A.alloc("ost", [128, TBW], F32) for _ in range(4)]
    ocnt = 0
    outs = []
    for tb in range(NTB):
        pi = nb()
        for kt in range(NKT):
            q = (tb * NKT + kt) % 3
            S.op("act", lambda e, q=q, kt=kt, tb=tb: e.activation(out=SQ[q][:], in_=xT[:, kt, tbs(tb)], func=AF.Square),
                 reads=[("x", kt, tb)], writes=[("sq", q)])
            S.op("pe", lambda e, q=q, kt=kt, pi=pi: e.matmul(PS[:, pi, :], lhsT=ONES[:], rhs=SQ[q][:], start=(kt == 0),
                                                            stop=(kt == NKT - 1)),
                 reads=[("sq", q), "ones"], writes=[("ps", pi)])
        m = tb % 2
        S.op("dve", lambda e, m=m, pi=pi: e.tensor_scalar(out=MS[m][:], in0=PS[:, pi, :], scalar1=1.0 / D, scalar2=EPS,
                                                         op0=ALU.mult, op1=ALU.add),
             reads=[("ps", pi)], writes=[("ms", m)])
        S.op("act", lambda e, m=m: e.activation(out=MS[m][:], in_=MS[m][:], func=AF.Sqrt), reads=[("ms", m)], writes=[("ms", m)])
        S.op("dve", lambda e, m=m: e.reciprocal(out=MS[m][:], in_=MS[m][:]), reads=[("ms", m)], writes=[("ms", m)])
        for kt in range(NKT):
            oq = ocnt % 4
            ocnt += 1
            S.op("dve", lambda e, m=m, kt=kt, tb=tb, oq=oq: e.scalar_tensor_tensor(
                out=OST[oq][:], in0=xT[:, kt, tbs(tb)], scalar=PRM[:, 0, P_GFIN + kt:P_GFIN + kt + 1], in1=MS[m][:],
                op0=ALU.mult, op1=ALU.mult),
                 reads=[("x", kt, tb), ("ms", m), "prm"], writes=[("ost", oq)])
            o = S.op("sp", lambda e, kt=kt, tb=tb, oq=oq: e.dma_start(out=outT_d[kt * 128:(kt + 1) * 128, tbs(tb)], in_=OST[oq][:]),
                     reads=[("ost", oq)], writes=[("out", kt, tb)], dma=True, chan=f"o{oq}")
            outs.append(o)
    S.op("sp", lambda e: e.nop(), extra=outs)
    A.pop()
    S.emit(nc)
    return nc


def _host_consts():
    cst = np.zeros((128, CSTW), np.float32)
    cst[:, 1152 + 2 * SEQ:1152 + 2 * SEQ + SCH] = np.arange(SCH, dtype=np.float32)[None, :]
    cst[:, CSTW - 128:] = np.eye(128, dtype=np.float32)
    j = np.arange(128)[:, None]
    i = np.arange(128)[None, :]
    cur = np.where(j <= i, 0.0, -240000.0).astype(np.float32)
    prev = np.where(j > i, 0.0, -240000.0).astype(np.float32)
    cst[:, 0:512] = np.tile(cur, (1, 4))
    cst[:, 512:1024] = np.tile(prev, (1, 4))
    perm = np.zeros((128, 128), np.float32)
    for h in range(2):
        for ii in range(8):
            perm[h * 64 + ii + 8, h * 64 + ii] = -1.0
            perm[h * 64 + ii, h * 64 + ii + 8] = 1.0
    cst[:, 1024:1152] = perm
    pos = np.arange(SEQ, dtype=np.float32)
    inv_freq = (np.float32(500000.0) ** (-np.arange(0, 16, 2, dtype=np.float32) / np.float32(16))).astype(np.float32)
    ang = pos[:, None] * inv_freq[None, :]
    c = np.cos(ang).astype(np.float32).T
    s = np.sin(ang).astype(np.float32).T
    cos_t = np.ones((128, SEQ), np.float32)
    sin_t = np.zeros((128, SEQ), np.float32)
    for h in range(2):
        cos_t[h * 64:h * 64 + 8] = c
        cos_t[h * 64 + 8:h * 64 + 16] = c
        sin_t[h * 64:h * 64 + 8] = s
        sin_t[h * 64 + 8:h * 64 + 16] = s
    cst[:, 1152:1152 + SEQ] = cos_t
    cst[:, 1152 + SEQ:1152 + 2 * SEQ] = sin_t
    return cst


def _prep_inputs(inp, NL):
    f = lambda a: np.ascontiguousarray(np.asarray(a, dtype=np.float32))
    w_in = f(inp["w_in"])[:NL].copy()
    w_in[:, :, 0:512] = w_in[:, :, 0:512].reshape(NL, D, 2, 4, 64).transpose(0, 1, 3, 2, 4).reshape(NL, D, 512)
    w_ao = f(inp["w_attn_o"])[:NL].reshape(NL, 2, 4, 64, D).transpose(0, 2, 1, 3, 4).reshape(NL, 512, D)
    prm = np.zeros((NL, 128, NPRM), np.float32)
    t8 = lambda v: v.reshape(-1, 128).T
    a_re = f(inp["ssm_a_re"])
    a_im = f(inp["ssm_a_im"])
    ldt = f(inp["ssm_log_dt"])
    b_re = f(inp["ssm_b_re"])
    b_im = f(inp["ssm_b_im"])
    c_re = f(inp["ssm_c_re"])
    c_im = f(inp["ssm_c_im"])
    ssmB = np.zeros((NL, 128, 2, 2048), np.float32)
    ssmC = np.zeros((NL, 128, 2, 16, 64), np.float32)
    sinks = f(inp["attn_sinks"])
    for l in range(NL):
        prm[l, :, P_GMIX:P_GMIX + 8] = t8(f(inp["norm_mix"])[l])
        prm[l, :, P_GFFN:P_GFFN + 8] = t8(f(inp["norm_ffn"])[l])
        prm[l, :, P_BG:P_BG + 24] = t8(f(inp["b_gate"])[l])
        cw = f(inp["conv_w"])[l]
        for j in range(3):
            prm[l, :, P_CONVW + j * 4:P_CONVW + j * 4 + 4] = t8(cw[j])
        prm[l, :, P_SSMD:P_SSMD + 4] = t8(f(inp["ssm_d"])[l])
        for kvh in range(2):
            prm[l, kvh * 64:(kvh + 1) * 64, P_SINK:P_SINK + 4] = sinks[l, kvh * 4:(kvh + 1) * 4][None, :]
        arA = a_re[l].reshape(16, 2, 64).transpose(1, 2, 0).reshape(128, 16)
        aiA = a_im[l].reshape(16, 2, 64).transpose(1, 2, 0).reshape(128, 16)
        ldA = np.broadcast_to(ldt[l].reshape(16, 2, 1), (16, 2, 64)).transpose(1, 2, 0).reshape(128, 16)
        prm[l, :, P_AA:P_AA + 16] = arA
        prm[l, :, P_AI:P_AI + 16] = aiA
        prm[l, :, P_LDT:P_LDT + 16] = ldA
        prm[l, :, P_GFIN:P_GFIN + 8] = t8(f(inp["norm_final"]))
        for ct in range(4):
            for gl in range(8):
                g = ct * 8 + gl
                c0 = g * 64
                ssmB[l, gl * 16:(gl + 1) * 16, 0, c0:c0 + 64] = b_re[l, g].T
                ssmB[l, gl * 16:(gl + 1) * 16, 1, c0:c0 + 64] = b_im[l, g].T
        for pr in range(16):
            plh = (pr % 4) % 2
            for gsel in range(2):
                g = 2 * pr + gsel
                c0 = plh * 32 + gsel * 16
                ssmC[l, gsel * 64:(gsel + 1) * 64, 0, pr, c0:c0 + 16] = c_re[l, g].T
                ssmC[l, gsel * 64:(gsel + 1) * 64, 1, pr, c0:c0 + 16] = c_im[l, g].T
    shared = {
        "w_in": w_in, "w_attn_o": np.ascontiguousarray(w_ao), "w_conv_o": f(inp["w_conv_o"])[:NL],
        "w_ssm_glu": f(inp["w_ssm_glu"])[:NL], "w_ssm_o": f(inp["w_ssm_o"])[:NL], "w_mix_o": f(inp["w_mix_o"])[:NL],
        "w_ffn_in": f(inp["w_ffn_in"])[:NL], "w_ffn_out": f(inp["w_ffn_out"])[:NL],
        "prm": prm, "ssmB": ssmB, "ssmC": ssmC.reshape(NL, 128, 2, 1024), "cst": _host_consts(),
    }
    return shared


_NC_CACHE = {}


def kernel(NL=NLAYER, DBG=99, **inp):
    x = np.asarray(inp["x"], dtype=np.float32)
    B = x.shape[0]
    shared = _prep_inputs(inp, NL)
    if (NL, DBG) not in _NC_CACHE:
        _NC_CACHE[(NL, DBG)] = build_program(NL, DBG)
    nc = _NC_CACHE[(NL, DBG)]
    in_maps = []
    for b in range(B):
        m = dict(shared)
        m["xT"] = np.ascontiguousarray(x[b].T)
        in_maps.append(m)
    res = run_bass_kernel_spmd(nc, in_maps, core_ids=list(range(B)))
    out = np.stack([np.ascontiguousarray(r["outT"].T) for r in res.results], axis=0)
    return out.astype(np.float32)
```

```python
import contextlib
import math
import numpy as np
import concourse.bass as bass
import concourse.mybir as mybir
from concourse.bass_utils import run_bass_kernel_spmd

F32 = mybir.dt.float32
BF16 = mybir.dt.bfloat16
AF = mybir.ActivationFunctionType
ALU = mybir.AluOpType

D = 1024
SEQ = 2048
NLAYER = 4
NKT = 8
NTB = 4
TBW = 512
FFH = 2816
INC = 5888
EPS = 1e-6
SCH = 64
NCH = SEQ // SCH


class Op:
    __slots__ = ("eng", "fn", "deps", "dma", "chan", "ticket", "need_inc", "idx", "ndma")

    def __init__(self, eng, fn, dma, chan, ndma):
        self.eng = eng
        self.fn = fn
        self.dma = dma
        self.chan = chan
        self.ndma = ndma
        self.deps = set()
        self.ticket = None
        self.need_inc = False


class Sched:
    ENGS = ("pe", "act", "dve", "pool", "sp")

    def __init__(self):
        self.ops = []
        self.last_w = {}
        self.readers = {}
        self.last_by_eng = {}
        self.dmas_since = []

    enabled = True

    def op(self, eng, fn, reads=(), writes=(), dma=False, chan=None, ndma=1, extra=()):
        if not self.enabled:
            return None
        o = Op(eng, fn, dma, chan, ndma)
        o.idx = len(self.ops)
        deps = set(extra)
        for k in reads:
            w = self.last_w.get(k)
            if w is not None:
                deps.add(w)
        for k in writes:
            w = self.last_w.get(k)
            if w is not None:
                deps.add(w)
            deps.update(self.readers.get(k, ()))
        for k in reads:
            self.readers.setdefault(k, []).append(o)
        for k in writes:
            self.last_w[k] = o
            self.readers[k] = []
        deps.discard(o)
        o.deps = deps
        self.ops.append(o)
        if dma:
            self.dmas_since.append(o)
        else:
            self.last_by_eng[eng] = o
        return o

    def barrier(self):
        if not self.enabled:
            return
        allops = [o for o in self.last_by_eng.values()] + list(self.dmas_since)
        self.last_w = {}
        self.readers = {}
        self.dmas_since = []
        for e in self.ENGS:
            self.op(e, lambda eng: eng.nop(), extra=[o for o in allops])

    def emit(self, nc):
        for o in self.ops:
            for d in o.deps:
                if d.dma:
                    continue
                if d.eng != o.eng or d.eng != "pe":
                    d.need_inc = True
        counts = {e: 0 for e in self.ENGS}
        chan_counts = {}
        for o in self.ops:
            if o.dma:
                c = chan_counts.get(o.chan, 0) + o.ndma
                chan_counts[o.chan] = c
                o.ticket = ("c:" + o.chan, 16 * c)
            elif o.need_inc:
                counts[o.eng] += 1
                o.ticket = ("e:" + o.eng, counts[o.eng])
        sem_names = ["e:" + e for e in self.ENGS] + ["c:" + c for c in chan_counts] + ["k:" + e for e in self.ENGS]
        with contextlib.ExitStack() as st:
            sems = {}
            for n in sem_names:
                sems[n] = st.enter_context(nc.semaphore(n.replace(":", "_")))
            block = st.enter_context(nc.Block())
            per_eng = {e: [o for o in self.ops if o.eng == e] for e in self.ENGS}

            def run(engname, eng):
                waited = {}
                CH.sem = sems["k:" + engname]
                CH.cnt = 0
                for o in per_eng[engname]:
                    need = {}
                    for d in o.deps:
                        if (not d.dma) and d.eng == engname and engname == "pe":
                            continue
                        s, v = d.ticket
                        if waited.get(s, 0) >= v:
                            continue
                        if need.get(s, 0) < v:
                            need[s] = v
                    for s, v in need.items():
                        eng.wait_ge(sems[s], v)
                        waited[s] = v
                    if engname in ("act", "dve", "pool") and not o.dma:
                        ins = o.fn(EngProxy(eng))
                    else:
                        ins = o.fn(eng)
                    if o.dma:
                        if not isinstance(ins, (list, tuple)):
                            ins = [ins]
                        assert len(ins) == o.ndma
                        for i_ in ins:
                            i_.then_inc(sems[o.ticket[0]], 16)
                    elif o.need_inc:
                        ins.then_inc(sems[o.ticket[0]], 1)

            @block.tensor
            def _(e):
                run("pe", e)

            @block.scalar
            def _(e):
                run("act", e)

            @block.vector
            def _(e):
                run("dve", e)

            @block.gpsimd
            def _(e):
                run("pool", e)

            @block.sync
            def _(e):
                run("sp", e)


class _Chain:
    sem = None
    cnt = 0


CH = _Chain()


def C(e, ins):
    CH.cnt += 1
    ins.then_inc(CH.sem, 1)
    e.wait_ge(CH.sem, CH.cnt)
    return ins


class EngProxy:
    def __init__(self, e):
        self._e = e
        self._last = None
        self._skip = False

    def nosync(self):
        self._skip = True

    def __getattr__(self, name):
        real = getattr(self._e, name)

        def w(*a, **k):
            if self._last is not None and not self._skip:
                C(self._e, self._last)
            self._skip = False
            ins = real(*a, **k)
            self._last = ins
            return ins

        return w


def red_angle(e, x, tmpf, tmpi):
    PI = math.pi
    e.tensor_scalar(out=tmpf, in0=x, scalar1=1.0 / (2 * PI), scalar2=0.5, op0=ALU.mult, op1=ALU.add)
    e.tensor_copy(out=tmpi, in_=tmpf)
    e.tensor_copy(out=tmpf, in_=tmpi)
    e.scalar_tensor_tensor(out=x, in0=tmpf, scalar=-2 * PI, in1=x, op0=ALU.mult, op1=ALU.add)
    e.tensor_scalar(out=tmpf, in0=x, scalar1=-PI, scalar2=2 * PI, op0=ALU.is_lt, op1=ALU.mult)
    e.tensor_tensor(out=x, in0=x, in1=tmpf, op=ALU.add)
    e.tensor_scalar(out=tmpf, in0=x, scalar1=PI, scalar2=-2 * PI, op0=ALU.is_gt, op1=ALU.mult)
    return e.tensor_tensor(out=x, in0=x, in1=tmpf, op=ALU.add)


class Arena:
    def __init__(self, nc, lo=16512, hi=225792):
        self.nc = nc
        self.lo = lo
        self.hi = hi
        self.top = lo
        self.n = 0
        self.stack = []
        self.offs = {}
        self.peak = lo

    def alloc(self, name, shape, dtype):
        nbytes = int(np.prod(shape[1:])) * mybir.dt.size(dtype)
        off = (self.top + 63) // 64 * 64
        assert off + nbytes <= self.hi, (name, off, nbytes, self.hi)
        t = self.nc.alloc_sbuf_tensor_at(f"{name}_{self.n}", list(shape), dtype, offset=off)
        self.offs[name] = off
        self.top = off + nbytes
        self.peak = max(self.peak, self.top)
        self.n += 1
        return t

    def push(self):
        self.stack.append(self.top)

    def pop(self):
        self.top = self.stack.pop()


P_GMIX = 0
P_GFFN = 8
P_BG = 16
P_CONVW = 40
P_SSMD = 52
P_SINK = 56
P_AA = 60
P_AI = 76
P_LDT = 92
P_GFIN = 108
NPRM = 116

W_SLOTS = 3
CSTW = 2 * 512 + 128 + 2 * SEQ + SCH + 128


def build_program(NL=NLAYER, DBG=99):
    nc = bass.Bass("TRN2", target_bir_lowering=False)
    dt_in = lambda name, shape: nc.dram_tensor(name, list(shape), F32, kind="ExternalInput").ap()
    xT_d = dt_in("xT", [D, SEQ])
    w_in_d = dt_in("w_in", [NL, D, INC])
    w_ao_d = dt_in("w_attn_o", [NL, 512, D])
    w_co_d = dt_in("w_conv_o", [NL, 512, D])
    w_glu_d = dt_in("w_ssm_glu", [NL, 512, 512])
    w_so_d = dt_in("w_ssm_o", [NL, 512, D])
    w_mix_d = dt_in("w_mix_o", [NL, D, D])
    w_fi_d = dt_in("w_ffn_in", [NL, D, 2 * FFH])
    w_fo_d = dt_in("w_ffn_out", [NL, FFH, D])
    prm_d = dt_in("prm", [NL, 128, NPRM])
    ssmB_d = dt_in("ssmB", [NL, 128, 2, 2048])
    ssmC_d = dt_in("ssmC", [NL, 128, 2, 1024])
    cst_d = dt_in("cst", [128, CSTW])
    outT_d = nc.dram_tensor("outT", [D, SEQ], F32, kind="ExternalOutput").ap()

    S = Sched()
    A = Arena(nc)
    PS = nc.alloc_psum_tensor("ps", [128, 8, 512], F32)

    xT = A.alloc("xT", [128, NKT, SEQ], F32)
    hT = A.alloc("hT", [128, NKT, SEQ], BF16)
    WS = [A.alloc(f"ws{i}", [128, 4096], BF16) for i in range(W_SLOTS)]
    mrg_off = (A.top + 63) // 64 * 64
    MRG = A.alloc("mrg", [128, NKT, SEQ], BF16)
    BR = A.alloc("br", [128, 4, SEQ], BF16)
    FFA = nc.alloc_sbuf_tensor_at("ffa", [128, 11, SEQ], BF16, offset=mrg_off)
    PRM = A.alloc("prm", [128, NL, NPRM], F32)
    ONES = A.alloc("ones", [128, 128], BF16)
    PERM = A.alloc("perm", [128, 128], BF16)
    ESK = A.alloc("esk", [128, 4], F32)
    JI = A.alloc("ji", [128, SCH], F32)
    IDN = A.alloc("idn", [128, 128], F32)
    ONESF = A.alloc("onesf", [128, 128], F32)

    psc = [0]

    def nb(n=1):
        i = psc[0] % 8
        psc[0] += 1
        return i

    wsc = [0]

    def wload(views, rshape):
        s = wsc[0] % W_SLOTS
        wsc[0] += 1

        def fn(e, s=s, views=views):
            out = []
            for (c0, a, b, src) in views:
                dst = WS[s][:, c0:c0 + a * b].rearrange("p (a b) -> p a b", a=a)
                for ai in range(a):
                    out.append(e.dma_start(out=dst[:, ai, :], in_=src[:, ai, :]))
            return out

        S.op("pool", fn, writes=[("w", s)], dma=True, chan=f"w{s}", ndma=sum(v[1] for v in views))
        return s

    def wview(s, a, b, c0=0):
        return WS[s][:, c0:c0 + a * b].rearrange("p (a b) -> p a b", a=a)

    def tbs(tb):
        return slice(tb * TBW, (tb + 1) * TBW)

    for tb in range(NTB):
        for kt in range(NKT):
            S.op("sp", lambda e, kt=kt, tb=tb: e.dma_start(out=xT[:, kt, tbs(tb)], in_=xT_d[kt * 128:(kt + 1) * 128, tbs(tb)]),
                 writes=[("x", kt, tb)], dma=True, chan=f"x{kt}_{tb}")
    S.op("sp", lambda e: e.dma_start(out=PRM[:], in_=prm_d.rearrange("l p c -> p l c")), writes=["prm"], dma=True,
         chan="prm")
    S.op("sp", lambda e: e.dma_start(out=JI[:], in_=cst_d[:, 1152 + 2 * SEQ:1152 + 2 * SEQ + SCH]), writes=["ji"], dma=True,
         chan="ji")
    S.op("sp", lambda e: e.dma_start(out=IDN[:], in_=cst_d[:, CSTW - 128:CSTW]), writes=["idn"], dma=True, chan="idn")
    S.op("dve", lambda e: e.memset(ONESF[:], 1.0), writes=["onesf"])
    S.op("dve", lambda e: e.memset(ONES[:], 1.0), writes=["ones"])
    S.op("pool", lambda e: e.dma_start(out=PERM[:], in_=cst_d[:, 1024:1152]), writes=["perm"], dma=True, chan="perm")

    def rmsnorm_to_h(l, gcol, name):
        nb0 = A.offs["br"] + 3 * SEQ * 2
        SQ = [nc.alloc_sbuf_tensor_at(f"nsq{i}_{name}_l{l}", [128, TBW], BF16, offset=nb0 + i * TBW * 2) for i in range(2)]
        MS = [nc.alloc_sbuf_tensor_at(f"nms_{name}_l{l}", [128, TBW], F32, offset=nb0 + 2 * TBW * 2)] * 2
        for tb in range(NTB):
            pi = nb()
            for kt in range(NKT):
                q = (tb * NKT + kt) % 2
                S.op("act", lambda e, q=q, kt=kt, tb=tb: e.activation(out=SQ[q][:], in_=xT[:, kt, tbs(tb)], func=AF.Square),
                     reads=[("x", kt, tb)], writes=[("sq", q)])
                S.op("pe", lambda e, q=q, kt=kt, pi=pi: e.matmul(PS[:, pi, :], lhsT=ONES[:], rhs=SQ[q][:], start=(kt == 0),
                                                                stop=(kt == NKT - 1)),
                     reads=[("sq", q), "ones"], writes=[("ps", pi)])
            m = 0
            S.op("dve", lambda e, m=m, pi=pi: e.tensor_scalar(out=MS[m][:], in0=PS[:, pi, :], scalar1=1.0 / D, scalar2=EPS,
                                                             op0=ALU.mult, op1=ALU.add),
                 reads=[("ps", pi)], writes=[("ms", m)])
            S.op("act", lambda e, m=m: e.activation(out=MS[m][:], in_=MS[m][:], func=AF.Sqrt), reads=[("ms", m)],
                 writes=[("ms", m)])
            S.op("dve", lambda e, m=m: e.reciprocal(out=MS[m][:], in_=MS[m][:]), reads=[("ms", m)], writes=[("ms", m)])
            for kt in range(NKT):
                S.op("dve", lambda e, m=m, kt=kt, tb=tb: e.scalar_tensor_tensor(
                    out=hT[:, kt, tbs(tb)], in0=xT[:, kt, tbs(tb)], scalar=PRM[:, l, gcol + kt:gcol + kt + 1], in1=MS[m][:],
                    op0=ALU.mult, op1=ALU.mult),
                     reads=[("x", kt, tb), ("ms", m), "prm"], writes=[("h", kt, tb)])

    def proj_group(pi, s, c0, tb, ncols_slot=512):
        wv = wview(s, NKT, ncols_slot)

        def fn(e):
            ins = None
            for kt in range(NKT):
                ins = e.matmul(PS[:, pi, :], lhsT=wv[:, kt, c0:c0 + 128], rhs=hT[:, kt, tbs(tb)], start=(kt == 0),
                               stop=(kt == NKT - 1))
            return ins

        S.op("pe", fn, reads=[("w", s)] + [("h", kt, tb) for kt in range(NKT)], writes=[("ps", pi)])

    def win_view(l, c0, ncols):
        return w_in_d[l].rearrange("(kt p) n -> p kt n", p=128)[:, :, c0:c0 + ncols]

    def merge_branch(l, b, wo_d):
        A.push()
        SG = [A.alloc("sg", [128, TBW], F32) for _ in range(2)]
        TMP = [A.alloc("tmp", [128, TBW], F32) for _ in range(2)]
        so = wload([(0, 4, 1024, wo_d[l].rearrange("(kt p) n -> p kt n", p=128))], None)
        wo = wview(so, 4, 1024)
        for half in range(2):
            sg_ = wload([(0, NKT, 512, win_view(l, 2816 + b * 1024 + half * 512, 512))], None)
            for fl in range(4):
                f = half * 4 + fl
                for tb in range(NTB):
                    py = nb()

                    def fy(e, py=py, f=f, tb=tb):
                        ins = None
                        for kt in range(4):
                            ins = e.matmul(PS[:, py, :], lhsT=wo[:, kt, f * 128:(f + 1) * 128], rhs=BR[:, kt, tbs(tb)],
                                           start=(kt == 0), stop=(kt == 3))
                        return ins

                    S.op("pe", fy, reads=[("w", so)] + [("br", kt, tb) for kt in range(4)], writes=[("ps", py)])
                    pg = nb()
                    proj_group(pg, sg_, fl * 128, tb)
                    q = (f * NTB + tb) % 2
                    S.op("act", lambda e, q=q, pg=pg, f=f: e.activation(out=SG[q][:], in_=PS[:, pg, :], func=AF.Sigmoid,
                                                                         bias=PRM[:, l, P_BG + b * 8 + f:P_BG + b * 8 + f + 1]),
                         reads=[("ps", pg), "prm"], writes=[("sg", q)])
                    if b == 0:
                        S.op("dve", lambda e, q=q, py=py, f=f, tb=tb: e.tensor_tensor(out=MRG[:, f, tbs(tb)], in0=PS[:, py, :],
                                                                                      in1=SG[q][:], op=ALU.mult),
                             reads=[("ps", py), ("sg", q)], writes=[("mrg", f, tb)])
                    else:
                        S.op("dve", lambda e, q=q, py=py: e.tensor_tensor(out=TMP[q][:], in0=PS[:, py, :], in1=SG[q][:],
                                                                          op=ALU.mult),
                             reads=[("ps", py), ("sg", q)], writes=[("tmp", q)])
                        S.op("dve", lambda e, q=q, f=f, tb=tb: e.tensor_tensor(out=MRG[:, f, tbs(tb)], in0=MRG[:, f, tbs(tb)],
                                                                               in1=TMP[q][:], op=ALU.add),
                             reads=[("tmp", q), ("mrg", f, tb)], writes=[("mrg", f, tb)])
        S.barrier()
        A.pop()

    def layer(l):
        S.enabled = DBG >= 1
        rmsnorm_to_h(l, P_GMIX, "n1")

        A.push()

        S.enabled = DBG >= 2
        A.push()
        Q = nc.alloc_sbuf_tensor_at(f"q_l{l}", [128, 4, SEQ], BF16, offset=mrg_off)
        Kt = A.alloc("k", [128, SEQ], BF16)
        V = A.alloc("v", [128, 16, 128], BF16)
        A.push()
        COS = [A.alloc("cos", [128, TBW], BF16) for _ in range(2)]
        SIN = [A.alloc("sin", [128, TBW], BF16) for _ in range(2)]
        QR = [A.alloc("qr", [128, TBW], BF16) for _ in range(2)]
        T1 = A.alloc("t1", [128, TBW], F32)
        T2 = A.alloc("t2", [128, TBW], F32)
        sq_ = wload([(0, NKT, 512, win_view(l, 0, 512))], None)
        skv = wload([(0, NKT, 256, win_view(l, 512, 256))], None)

        def rope_block(pi, tb, dst_ap, dst_key, cnt):
            q = cnt % 2
            cq = tb % 2
            if DBG < 2.1:
                return
            S.op("dve", lambda e: e.tensor_copy(out=QR[q][:], in_=PS[:, pi, :]), reads=[("ps", pi)],
                 writes=[("qr", q)])
            p2 = nb()
            S.op("pe", lambda e: e.matmul(PS[:, p2, :], lhsT=PERM[:], rhs=QR[q][:], start=True, stop=True),
                 reads=["perm", ("qr", q)], writes=[("ps", p2)])
            S.op("dve", lambda e: e.tensor_tensor(out=T1[:], in0=PS[:, pi, :], in1=COS[cq][:], op=ALU.mult),
                 reads=[("ps", pi), ("cos", cq)], writes=["t1"])
            S.op("dve", lambda e: e.tensor_tensor(out=T2[:], in0=PS[:, p2, :], in1=SIN[cq][:], op=ALU.mult),
                 reads=[("ps", p2), ("sin", cq)], writes=["t2"])
            S.op("dve", lambda e: e.tensor_tensor(out=dst_ap, in0=T1[:], in1=T2[:], op=ALU.add), reads=["t1", "t2"],
                 writes=[dst_key])

        cnt = 0
        for tb in range(NTB):
            cq = tb % 2
            S.op("pool", lambda e, tb=tb, cq=cq: e.dma_start(out=COS[cq][:], in_=cst_d[:, 1152 + tb * TBW:1152 + (tb + 1) * TBW]),
                 writes=[("cos", cq)], dma=True, chan=f"cos{cq}")
            S.op("pool", lambda e, tb=tb, cq=cq: e.dma_start(out=SIN[cq][:], in_=cst_d[:, 1152 + SEQ + tb * TBW:1152 + SEQ + (tb + 1) * TBW]),
                 writes=[("sin", cq)], dma=True, chan=f"sin{cq}")
            for g in range(4):
                pi = nb()
                proj_group(pi, sq_, g * 128, tb)
                rope_block(pi, tb, Q[:, g, tbs(tb)], ("q", g, tb), cnt)
                cnt += 1
            pi = nb()
            proj_group(pi, skv, 0, tb, ncols_slot=256)
            rope_block(pi, tb, Kt[:, tbs(tb)], ("k", tb), cnt)
            cnt += 1
        S.enabled = DBG >= 2.2
        kvv = wview(skv, NKT, 256)
        for t4 in range(4):
            pi = nb()

            def fv(e, pi=pi, t4=t4):
                ins = None
                for j in range(4):
                    tt = t4 * 4 + j
                    for kt in range(NKT):
                        ins = e.matmul(PS[:, pi, j * 128:(j + 1) * 128], lhsT=hT[:, kt, tt * 128:(tt + 1) * 128],
                                       rhs=kvv[:, kt, 128:256], start=(kt == 0), stop=(kt == NKT - 1))
                return ins

            S.op("pe", fv, reads=[("w", skv)] + [("h", kt, t4) for kt in range(NKT)], writes=[("ps", pi)])
            S.op("dve", lambda e, pi=pi, t4=t4: e.tensor_copy(out=V[:, t4 * 4:(t4 + 1) * 4, :],
                                                               in_=PS[:, pi, :].rearrange("p (a b) -> p a b", a=4)),
                 reads=[("ps", pi)], writes=[("v", t4)])
        S.barrier()
        A.pop()
        S.enabled = DBG >= 2.5
        MK = A.alloc("mk", [128, 2, 512], BF16)
        IDB = A.alloc("idb", [128, 128], BF16)
        PB = [A.alloc("pb", [128, TBW], BF16) for _ in range(8)]
        DEN = [A.alloc("den", [128, TBW], F32) for _ in range(1)]
        S.op("pool", lambda e: e.dma_start(out=MK[:], in_=cst_d[:, 0:1024].rearrange("p (a b) -> p a b", a=2)),
             writes=["mk"], dma=True, chan="mk")
        S.op("dve", lambda e: e.tensor_copy(out=IDB[:], in_=IDN[:]), reads=["idn"], writes=["idb"])
        S.op("act", lambda e: e.activation(out=ESK[:], in_=PRM[:, l, P_SINK:P_SINK + 4], func=AF.Exp), reads=["prm"],
             writes=["esk"])
        pcnt = [0]

        def att_s(qb):
            qs = slice(qb * 128, (qb + 1) * 128)
            pbs = {}
            kts = [qb] if qb == 0 else [qb - 1, qb]
            k_ = 0
            for kvh in range(2):
                hs = slice(kvh * 64, (kvh + 1) * 64)
                for kt_ in kts:
                    pa = (k_ if qb else 2 * k_) % 4
                    k_ += 1
                    ks = slice(kt_ * 128, (kt_ + 1) * 128)
                    mi = 0 if kt_ == qb else 1

                    def fs(e, pa=pa, hs=hs, ks=ks, qs=qs, kvh=kvh, mi=mi):
                        e.matmul(PS[:, pa, :], lhsT=IDB[:], rhs=MK[:, mi, :], start=True, stop=False)
                        return e.matmul(PS[:, pa, :].rearrange("p (a b) -> p a b", a=4), lhsT=Kt[hs, ks], rhs=Q[hs, :, qs],
                                        start=False, stop=True, tile_position=(kvh * 64, 0))

                    S.op("pe", fs, reads=[("k", kt_ // 4), "idb", "mk"] + [("q", g, qb // 4) for g in range(4)],
                         writes=[("ps", pa)])
                    pq = pcnt[0] % 8
                    pcnt[0] += 1
                    S.op("act", lambda e, pa=pa, pq=pq: e.activation(out=PB[pq][:], in_=PS[:, pa, :], func=AF.Exp, scale=0.125),
                         reads=[("ps", pa)], writes=[("pb", pq)])
                    pbs[(kvh, kt_)] = pq
            return pbs

        def att_pv(qb, pbs):
            qs = slice(qb * 128, (qb + 1) * 128)
            po = 4 + 2 * (qb % 2)
            pd = po + 1
            kts = [qb] if qb == 0 else [qb - 1, qb]
            n_ = len(kts)
            for kvh in range(2):
                hs = slice(kvh * 64, (kvh + 1) * 64)
                for i_, kt_ in enumerate(kts):
                    pq = pbs[(kvh, kt_)]
                    S.op("pe", lambda e, pq=pq, hs=hs, kt_=kt_, i_=i_, n_=n_, kvh=kvh, po=po: e.matmul(
                        PS[hs, po, :], lhsT=V[:, kt_, hs], rhs=PB[pq][:], start=(i_ == 0), stop=(i_ == n_ - 1),
                        tile_position=(0, kvh * 64)),
                         reads=[("pb", pq), ("v", kt_ // 4)], writes=[("ps", po)])
                    S.op("pe", lambda e, pq=pq, hs=hs, i_=i_, n_=n_, kvh=kvh, pd=pd: e.matmul(
                        PS[hs, pd, :], lhsT=ONES[:, 0:64], rhs=PB[pq][:], start=(i_ == 0), stop=(i_ == n_ - 1),
                        tile_position=(0, kvh * 64)),
                         reads=[("pb", pq), "ones"], writes=[("ps", pd)])
            dq = 0
            S.op("dve", lambda e, dq=dq, pd=pd: e.tensor_tensor(
                out=DEN[dq][:].rearrange("p (a b) -> p a b", a=4), in0=PS[:, pd, :].rearrange("p (a b) -> p a b", a=4),
                in1=ESK[:].unsqueeze(2).to_broadcast([128, 4, 128]), op=ALU.add),
                 reads=[("ps", pd), "esk"], writes=[("den", dq)])
            S.op("dve", lambda e, dq=dq: e.reciprocal(out=DEN[dq][:], in_=DEN[dq][:]), reads=[("den", dq)],
                 writes=[("den", dq)])
            S.op("dve", lambda e, dq=dq, po=po, qs=qs: e.tensor_tensor(
                out=BR[:, :, qs], in0=PS[:, po, :].rearrange("p (a b) -> p a b", a=4),
                in1=DEN[dq][:].rearrange("p (a b) -> p a b", a=4), op=ALU.mult),
                 reads=[("ps", po), ("den", dq)], writes=[("br", g, qb // 4) for g in range(4)])

        prev_pbs = att_s(0)
        for qb in range(16):
            nxt = att_s(qb + 1) if qb + 1 < 16 else None
            att_pv(qb, prev_pbs)
            prev_pbs = nxt
        S.barrier()
        A.pop()
        S.enabled = DBG >= 2.8
        merge_branch(l, 0, w_ao_d)

        S.enabled = DBG >= 3
        A.push()
        Z = A.alloc("z", [128, SEQ + 2], F32)
        Y1 = [A.alloc("y1", [128, TBW], F32) for _ in range(2)]
        S.op("dve", lambda e: e.memset(Z[:, 0:2], 0.0), writes=["z0"])
        scc = wload([(0, NKT, 512, win_view(l, 1280, 512))], None)
        for f in range(4):
            for tb in range(NTB):
                pi = nb()
                proj_group(pi, scc, f * 128, tb)
                S.op("dve", lambda e, pi=pi, f=f, tb=tb: e.tensor_copy(out=BR[:, f, tbs(tb)], in_=PS[:, pi, :]),
                     reads=[("ps", pi)], writes=[("br", f, tb)])
        scx = wload([(0, NKT, 512, win_view(l, 1792, 512))], None)
        for f in range(4):
            for tb in range(NTB):
                pi = nb()
                proj_group(pi, scx, f * 128, tb)
                a0 = tb * TBW
                S.op("dve", lambda e, pi=pi, f=f, tb=tb, a0=a0: e.tensor_tensor(out=Z[:, 2 + a0:2 + a0 + TBW], in0=PS[:, pi, :],
                                                                               in1=BR[:, f, tbs(tb)], op=ALU.mult),
                     reads=[("ps", pi), ("br", f, tb)], writes=[("z", tb)])
                yq = tb % 2
                cw = lambda j, f=f: PRM[:, l, P_CONVW + j * 4 + f:P_CONVW + j * 4 + f + 1]

                def fconv(e, a0=a0, yq=yq, f=f, tb=tb, cw=cw):
                    e.tensor_scalar(out=Y1[yq][:], in0=Z[:, 2 + a0:2 + a0 + TBW], scalar1=cw(2), scalar2=None, op0=ALU.mult)
                    e.scalar_tensor_tensor(out=Y1[yq][:], in0=Z[:, 1 + a0:1 + a0 + TBW], scalar=cw(1), in1=Y1[yq][:],
                                           op0=ALU.mult, op1=ALU.add)
                    return e.scalar_tensor_tensor(out=BR[:, f, tbs(tb)], in0=Z[:, a0:a0 + TBW], scalar=cw(0), in1=Y1[yq][:],
                                                  op0=ALU.mult, op1=ALU.add)

                S.op("dve", fconv, reads=[("z", tb), ("z", tb - 1), "z0", "prm"], writes=[("y1", yq), ("br", f, tb)])
        scb = wload([(0, NKT, 512, win_view(l, 768, 512))], None)
        for f in range(4):
            for tb in range(NTB):
                pi = nb()
                proj_group(pi, scb, f * 128, tb)
                S.op("dve", lambda e, pi=pi, f=f, tb=tb: e.tensor_tensor(out=BR[:, f, tbs(tb)], in0=PS[:, pi, :],
                                                                        in1=BR[:, f, tbs(tb)], op=ALU.mult),
                     reads=[("ps", pi), ("br", f, tb)], writes=[("br", f, tb)])
        S.barrier()
        A.pop()
        merge_branch(l, 1, w_co_d)

        S.enabled = DBG >= 4
        A.push()
        WBU = [A.alloc("wbu", [128, 2048], BF16) for _ in range(2)]
        CW = A.alloc("cw", [128, 16, 2, 64], BF16)
        DIAGD = A.alloc("diagd", [128, 4, 128], BF16)
        L1 = A.alloc("l1", [128, 2, 16], F32)
        L2 = A.alloc("l2", [128, 2, 16], F32)
        XC = A.alloc("xc", [128, 2, 16], F32)
        su = wload([(0, NKT, 512, win_view(l, 2304, 512))], None)
        for ct in range(4):
            for tb in range(NTB):
                pi = nb()
                proj_group(pi, su, ct * 128, tb)
                S.op("dve", lambda e, pi=pi, ct=ct, tb=tb: e.tensor_copy(out=BR[:, ct, tbs(tb)], in_=PS[:, pi, :]),
                     reads=[("ps", pi)], writes=[("br", ct, tb)])

        S.barrier()
        def coeffs(eng_name, are, aim, ldt, shape, tmps, key):
            dt_, lr, li, t3, t4, qr, qi, t7 = tmps[:8]
            rd = [key + "_in"]
            wr = [key]
            PI = math.pi
            S.op("act", lambda e: e.activation(out=dt_, in_=ldt, func=AF.Exp), reads=rd, writes=wr)

            ti = tmps[8]

            def red(e, x):
                e.tensor_scalar(out=dt_, in0=x, scalar1=1.0 / (2 * PI), scalar2=0.5, op0=ALU.mult, op1=ALU.add)
                e.tensor_copy(out=ti, in_=dt_)
                e.tensor_copy(out=dt_, in_=ti)
                e.scalar_tensor_tensor(out=x, in0=dt_, scalar=-2 * PI, in1=x, op0=ALU.mult, op1=ALU.add)
                e.tensor_scalar(out=dt_, in0=x, scalar1=-PI, scalar2=2 * PI, op0=ALU.is_lt, op1=ALU.mult)
                e.tensor_tensor(out=x, in0=x, in1=dt_, op=ALU.add)
                e.tensor_scalar(out=dt_, in0=x, scalar1=PI, scalar2=-2 * PI, op0=ALU.is_gt, op1=ALU.mult)
                return e.tensor_tensor(out=x, in0=x, in1=dt_, op=ALU.add)

            def f1(e):
                e.tensor_tensor(out=t3, in0=are, in1=dt_, op=ALU.mult)
                e.tensor_tensor(out=t4, in0=aim, in1=dt_, op=ALU.mult)
                e.tensor_scalar(out=t7, in0=t4, scalar1=0.5 * PI, scalar2=None, op0=ALU.add)
                red(e, t7)
                return red(e, t4)

            S.op(eng_name, f1, reads=wr, writes=wr)

            def f3(e):
                e.activation(out=t3, in_=t3, func=AF.Exp)
                e.activation(out=t7, in_=t7, func=AF.Sin)
                return e.activation(out=t4, in_=t4, func=AF.Sin)

            S.op("act", f3, reads=wr, writes=wr)

            def f4(e):
                e.tensor_tensor(out=lr, in0=t3, in1=t7, op=ALU.mult)
                e.tensor_tensor(out=li, in0=t3, in1=t4, op=ALU.mult)
                e.tensor_scalar(out=t3, in0=lr, scalar1=-1.0, scalar2=None, op0=ALU.add)
                e.tensor_tensor(out=t4, in0=are, in1=are, op=ALU.mult)
                e.tensor_tensor(out=t7, in0=aim, in1=aim, op=ALU.mult)
                e.tensor_tensor(out=t4, in0=t4, in1=t7, op=ALU.add)
                e.reciprocal(out=t4, in_=t4)
                e.tensor_tensor(out=qr, in0=t3, in1=are, op=ALU.mult)
                e.tensor_tensor(out=t7, in0=li, in1=aim, op=ALU.mult)
                e.tensor_tensor(out=qr, in0=qr, in1=t7, op=ALU.add)
                e.tensor_tensor(out=qr, in0=qr, in1=t4, op=ALU.mult)
                e.tensor_tensor(out=qi, in0=li, in1=are, op=ALU.mult)
                e.tensor_tensor(out=t7, in0=t3, in1=aim, op=ALU.mult)
                e.tensor_tensor(out=qi, in0=qi, in1=t7, op=ALU.subtract)
                return e.tensor_tensor(out=qi, in0=qi, in1=t4, op=ALU.mult)

            S.op(eng_name, f4, reads=wr, writes=wr)
            return lr, li, qr, qi

        A.push()
        TA = [A.alloc("ta", [128, 16], F32) for _ in range(8)] + [A.alloc("tai", [128, 16], mybir.dt.int32)]
        lrA, liA, qrA, qiA = coeffs("dve", PRM[:, l, P_AA:P_AA + 16], PRM[:, l, P_AI:P_AI + 16], PRM[:, l, P_LDT:P_LDT + 16],
                                [128, 16], [t[:] for t in TA], "cfA")

        def fL(e):
            e.tensor_copy(out=L1[:, 0, :], in_=lrA)
            e.tensor_copy(out=L1[:, 1, :], in_=lrA)
            e.tensor_scalar(out=L2[:, 0, :], in0=liA, scalar1=-1.0, scalar2=None, op0=ALU.mult)
            e.tensor_copy(out=L2[:, 1, :], in_=liA)
            return e.memset(XC[:], 0.0)

        S.op("dve", fL, reads=["cfA"], writes=["L", "xc"])
        wbase = A.offs["ws0"]
        U4 = 16 * SCH * 4
        mkb = lambda nm, off, dt=F32, shape=None: nc.alloc_sbuf_tensor_at(f"{nm}_l{l}", shape or [128, 16, SCH], dt,
                                                                          offset=wbase + off)
        CJ = mkb("cj", 0, BF16)
        SJ = mkb("sj", U4 // 2, BF16)
        DEC = mkb("dec", U4)
        T1s = mkb("t1s", 2 * U4)
        T2s = mkb("t2s", 3 * U4)
        T3s = mkb("t3s", 4 * U4)
        XB = mkb("xb", 5 * U4, BF16, [128, 2, 16, SCH])
        TIs = mkb("tis", 5 * U4, mybir.dt.int32)
        PHI = A.alloc("phi", [128, 16], F32)
        RR = A.alloc("rr", [128, 16], F32)
        S.op("act", lambda e: e.activation(out=PHI[:], in_=PRM[:, l, P_LDT:P_LDT + 16], func=AF.Exp), reads=["prm"],
             writes=["tabp"])

        def ft1(e):
            e.tensor_tensor(out=RR[:], in0=PRM[:, l, P_AA:P_AA + 16], in1=PHI[:], op=ALU.mult)
            return e.tensor_tensor(out=PHI[:], in0=PRM[:, l, P_AI:P_AI + 16], in1=PHI[:], op=ALU.mult)

        S.op("dve", ft1, reads=["tabp", "prm"], writes=["tabp"])
        S.op("act", lambda e: e.activation(out=RR[:], in_=RR[:], func=AF.Exp), reads=["tabp"], writes=["tabp"])

        def ft2(e):
            e.tensor_tensor(out=T2s[:], in0=PHI[:].unsqueeze(2).to_broadcast([128, 16, SCH]),
                            in1=JI[:].unsqueeze(1).to_broadcast([128, 16, SCH]), op=ALU.mult)
            e.tensor_scalar(out=T3s[:], in0=T2s[:], scalar1=0.5 * math.pi, scalar2=None, op0=ALU.add)
            red_angle(e, T2s[:], T1s[:], TIs[:])
            red_angle(e, T3s[:], T1s[:], TIs[:])
            e.tensor_copy(out=DEC[:], in_=RR[:].unsqueeze(2).to_broadcast([128, 16, SCH]))
            return e.memset(DEC[:, :, 0:1], 0.0)

        S.op("dve", ft2, reads=["tabp", "ji"], writes=["tab"])

        def ft3(e):
            e.activation(out=SJ[:], in_=T2s[:], func=AF.Sin)
            return e.activation(out=CJ[:], in_=T3s[:], func=AF.Sin)

        S.op("act", ft3, reads=["tab"], writes=["tab"])
        def fCd(e):
            return [e.dma_start(out=CW[:, :, 0, :], in_=ssmC_d[l][:, 0, :].rearrange("p (a b) -> p a b", a=16)),
                    e.dma_start(out=CW[:, :, 1, :], in_=ssmC_d[l][:, 1, :].rearrange("p (a b) -> p a b", a=16))]

        S.op("pool", fCd, writes=["cw"], dma=True, chan="cw", ndma=2)
        S.op("pool", lambda e: e.tensor_scalar(out=CW[:, :, 1, :], in0=CW[:, :, 1, :], scalar1=-1.0, scalar2=None,
                                               op0=ALU.mult), reads=["cw"], writes=["cw"])
        def fdd(e):
            ins = None
            for ct in range(4):
                ins = e.tensor_scalar(out=DIAGD[:, ct, :], in0=IDN[:], scalar1=PRM[:, l, P_SSMD + ct:P_SSMD + ct + 1],
                                      scalar2=None, op0=ALU.mult)
            return ins

        S.op("dve", fdd, reads=["idn", "prm"], writes=["diagd"])
        DG = A.alloc("dg", [128, 4, 128], F32)
        BB = A.alloc("bb", [128, 2, 512], F32)
        TQ = [A.alloc("tq", [128, 512], F32) for _ in range(2)]
        for qt in range(4):
            S.op("sp", lambda e, qt=qt: e.dma_start(out=BB[:], in_=ssmB_d[l][:, :, qt * 512:(qt + 1) * 512]),
                 writes=["bb"], dma=True, chan="bb")
            for qi_, qsrc in enumerate((qrA, qiA)):
                def fdg(e, qsrc=qsrc, qt=qt):
                    ins = None
                    for j in range(4):
                        ins = e.tensor_scalar(out=DG[:, j, :], in0=IDN[:], scalar1=qsrc[:, qt * 4 + j:qt * 4 + j + 1],
                                              scalar2=None, op0=ALU.mult)
                    return ins

                S.op("dve", fdg, reads=["cfA", "idn"], writes=["dg"])

                def fqb(e, qi_=qi_):
                    ins = None
                    for j in range(4):
                        ins = e.matmul(PS[:, qi_, j * 128:(j + 1) * 128], lhsT=ONESF[:], rhs=DG[:, j, :], start=True, stop=True)
                    return ins

                S.op("pe", fqb, reads=["dg", "onesf"], writes=[("ps", qi_)])

            def fB(e, qt=qt):
                hs_ = slice(qt * 512, (qt + 1) * 512)
                QBr = PS[:, 0, :]
                QBi = PS[:, 1, :]
                e.tensor_tensor(out=TQ[0][:], in0=QBr, in1=BB[:, 0, :], op=ALU.mult)
                e.tensor_tensor(out=TQ[1][:], in0=QBi, in1=BB[:, 1, :], op=ALU.mult)
                e.tensor_tensor(out=WBU[0][:, hs_], in0=TQ[0][:], in1=TQ[1][:], op=ALU.subtract)
                e.tensor_tensor(out=TQ[0][:], in0=QBr, in1=BB[:, 1, :], op=ALU.mult)
                e.tensor_tensor(out=TQ[1][:], in0=QBi, in1=BB[:, 0, :], op=ALU.mult)
                return e.tensor_tensor(out=WBU[1][:, hs_], in0=TQ[0][:], in1=TQ[1][:], op=ALU.add)

            S.op("dve", fB, reads=[("ps", 0), ("ps", 1), "bb"], writes=[("wbu", k) for k in range(8)] + ["tq"])
        S.barrier()
        A.pop()

        A.push()
        XS2 = [A.alloc("xs", [128, 2, 16, SCH], F32) for _ in range(2)]
        TM1 = A.alloc("tm1", [128, 2, 16], F32)
        TM2 = A.alloc("tm2", [128, 2, 16], F32)
        G1 = A.alloc("g1", [128, 4, SCH], F32)
        flat2 = lambda ap: ap.rearrange("p a b -> p (a b)")
        PSBU = [("ps", 4), ("ps", 5), ("ps", 6), ("ps", 7)]

        def stage_a1(c):
            cs_ = slice(c * SCH, (c + 1) * SCH)
            tbk = (c * SCH) // TBW
            xk = c % 2
            XS = XS2[xk]

            def fbu(e, cs_=cs_):
                ins = None
                for pr in range(16):
                    ct = pr // 4
                    for ri in range(2):
                        o0 = (ri * 16 + pr) * SCH
                        bank = 4 + o0 // 512
                        oo = o0 % 512
                        ins = e.matmul(PS[:, bank, oo:oo + SCH], lhsT=WBU[ri][:, pr * 128:(pr + 1) * 128], rhs=BR[:, ct, cs_],
                                       start=True, stop=True)
                return ins

            S.op("pe", fbu, reads=[("wbu", ct) for ct in range(8)] + [("br", ct, tbk) for ct in range(4)], writes=PSBU)
            S.op("act", lambda e: e.activation(out=XS[:].rearrange("p a b c -> p (a b c)"),
                                               in_=PS[:, 4:8, :].rearrange("p a b -> p (a b)"), func=AF.Identity),
                 reads=PSBU, writes=[("xs_re", xk), ("xs_im", xk)])

        def stage_a2(c):
            xk = c % 2
            XS = XS2[xk]
            BRe = XS[:, 0, :, :]
            BIm = XS[:, 1, :, :]
            kre, kim = ("xs_re", xk), ("xs_im", xk)

            def fscan(e):
                e.tensor_tensor(out=T1s[:], in0=BRe, in1=CJ[:], op=ALU.mult)
                e.nosync()
                e.tensor_tensor(out=T2s[:], in0=BIm, in1=SJ[:], op=ALU.mult)
                e.tensor_tensor(out=T1s[:], in0=T1s[:], in1=T2s[:], op=ALU.add)
                e.tensor_tensor(out=T2s[:], in0=BRe, in1=SJ[:], op=ALU.mult)
                e.nosync()
                e.tensor_tensor(out=BIm, in0=BIm, in1=CJ[:], op=ALU.mult)
                e.tensor_tensor(out=BIm, in0=BIm, in1=T2s[:], op=ALU.subtract)
                e.nosync()
                e.tensor_tensor(out=TM1[:], in0=XC[:], in1=L1[:], op=ALU.mult)
                e.nosync()
                e.tensor_tensor(out=TM2[:], in0=XC[:, ::-1, :], in1=L2[:], op=ALU.mult)
                e.tensor_tensor(out=TM1[:], in0=TM1[:], in1=TM2[:], op=ALU.add)
                e.tensor_tensor(out=T1s[:, :, 0], in0=T1s[:, :, 0], in1=TM1[:, 0, :], op=ALU.add)
                e.nosync()
                e.tensor_tensor(out=BIm[:, :, 0], in0=BIm[:, :, 0], in1=TM1[:, 1, :], op=ALU.add)
                e.tensor_tensor_scan(out=flat2(BRe), data0=flat2(DEC[:]), data1=flat2(T1s[:]), initial=0.0, op0=ALU.mult,
                                     op1=ALU.add)
                e.nosync()
                e.tensor_tensor_scan(out=flat2(T2s[:]), data0=flat2(DEC[:]), data1=flat2(BIm), initial=0.0, op0=ALU.mult,
                                     op1=ALU.add)
                e.tensor_tensor(out=T1s[:], in0=T2s[:], in1=SJ[:], op=ALU.mult)
                e.nosync()
                e.tensor_tensor(out=BIm, in0=BRe, in1=CJ[:], op=ALU.mult)
                e.tensor_tensor(out=XB[:, 0, :, :], in0=BIm, in1=T1s[:], op=ALU.subtract)
                e.nosync()
                e.tensor_tensor(out=XC[:, 0, :], in0=BIm[:, :, SCH - 1], in1=T1s[:, :, SCH - 1], op=ALU.subtract)
                e.tensor_tensor(out=T1s[:], in0=BRe, in1=SJ[:], op=ALU.mult)
                e.nosync()
                e.tensor_tensor(out=BIm, in0=T2s[:], in1=CJ[:], op=ALU.mult)
                e.tensor_tensor(out=XC[:, 1, :], in0=BIm[:, :, SCH - 1], in1=T1s[:, :, SCH - 1], op=ALU.add)
                e.nosync()
                return e.tensor_tensor(out=XB[:, 1, :, :], in0=BIm, in1=T1s[:], op=ALU.add)

            S.op("dve", fscan, reads=[kre, kim, "L", "xc", "tab"], writes=[kre, kim, "t1", "t2", "xc", "xb"])
            py = nb() % 4

            cs_ = slice(c * SCH, (c + 1) * SCH)
            tbk = (c * SCH) // TBW

            def fcm(e, py=py, cs_=cs_):
                ins = None
                for ct in range(4):
                    e.matmul(PS[:, py, ct * SCH:(ct + 1) * SCH], lhsT=DIAGD[:, ct, :], rhs=BR[:, ct, cs_], start=True, stop=False)
                    for half in range(2):
                        k = 0
                        for pl in range(2):
                            pr = ct * 4 + half * 2 + pl
                            for ri in range(2):
                                ins = e.matmul(PS[half * 64:(half + 1) * 64, py, ct * SCH:(ct + 1) * SCH],
                                               lhsT=CW[:, pr, ri, :], rhs=XB[:, ri, pr, :], start=False, stop=(k == 3),
                                               tile_position=(0, half * 64))
                                k += 1
                return ins

            S.op("pe", fcm, reads=["xb", "cw", "diagd"] + [("br", ct, tbk) for ct in range(4)], writes=[("ps", py)])
            return py

        def stage_b(c, py):
            cs_ = slice(c * SCH, (c + 1) * SCH)
            tbk = (c * SCH) // TBW

            PY = PS[:, py, 0:4 * SCH].rearrange("p (a b) -> p a b", a=4)
            S.op("act", lambda e, PY=PY: e.activation(out=G1[:], in_=PY, func=AF.Square), reads=[("ps", py)], writes=["g1"])

            def fys(e, PY=PY):
                e.tensor_scalar(out=G1[:], in0=G1[:], scalar1=0.044715, scalar2=1.0, op0=ALU.mult, op1=ALU.add)
                return e.tensor_tensor(out=G1[:], in0=G1[:], in1=PY, op=ALU.mult)

            S.op("dve", fys, reads=[("ps", py), "g1"], writes=["g1"])
            S.op("act", lambda e: e.activation(out=G1[:], in_=G1[:], func=AF.Sigmoid, scale=2.0 * math.sqrt(2.0 / math.pi)),
                 reads=["g1"], writes=["g1"])
            S.op("dve", lambda e, cs_=cs_, PY=PY: e.tensor_tensor(out=BR[:, :, cs_], in0=PY, in1=G1[:], op=ALU.mult),
                 reads=[("ps", py), "g1"], writes=[("yso", c)])

        pys = {}
        stage_a1(0)
        for c in range(NCH + 1):
            if c + 1 < NCH:
                stage_a1(c + 1)
            if c < NCH:
                pys[c] = stage_a2(c)
            if c >= 1:
                stage_b(c - 1, pys[c - 1])
        S.barrier()
        A.pop()
        SG = [A.alloc("sg", [128, TBW], F32) for _ in range(2)]
        sgl = wload([(0, 4, 512, w_glu_d[l].rearrange("(kt p) n -> p kt n", p=128))], None)
        wg = wview(sgl, 4, 512)
        for tb in range(NTB):
            pis = []
            for f in range(4):
                pi = nb()
                pis.append(pi)

                def fg(e, pi=pi, f=f, tb=tb):
                    ins = None
                    for kt in range(4):
                        ins = e.matmul(PS[:, pi, :], lhsT=wg[:, kt, f * 128:(f + 1) * 128], rhs=BR[:, kt, tbs(tb)],
                                       start=(kt == 0), stop=(kt == 3))
                    return ins

                S.op("pe", fg, reads=[("w", sgl)] + [("br", kt, tb) for kt in range(4)], writes=[("ps", pi)])
            for f in range(4):
                q = f % 2
                S.op("act", lambda e, q=q, pi=pis[f]: e.activation(out=SG[q][:], in_=PS[:, pi, :], func=AF.Sigmoid),
                     reads=[("ps", pis[f])], writes=[("sg", q)])
                S.op("dve", lambda e, q=q, f=f, tb=tb: e.tensor_tensor(out=BR[:, f, tbs(tb)], in0=BR[:, f, tbs(tb)], in1=SG[q][:],
                                                                      op=ALU.mult),
                     reads=[("sg", q), ("br", f, tb)], writes=[("br", f, tb)])
        S.barrier()
        A.pop()
        merge_branch(l, 2, w_so_d)

        S.enabled = DBG >= 5
        for half in range(2):
            sm = wload([(0, NKT, 512, w_mix_d[l].rearrange("(kt p) n -> p kt n", p=128)[:, :, half * 512:(half + 1) * 512])],
                       None)
            wm = wview(sm, NKT, 512)
            for fl in range(4):
                f2 = half * 4 + fl
                for tb in range(NTB):
                    pi = nb()

                    def fm(e, pi=pi, fl=fl, tb=tb, wm=wm):
                        ins = None
                        for kt in range(NKT):
                            ins = e.matmul(PS[:, pi, :], lhsT=wm[:, kt, fl * 128:(fl + 1) * 128], rhs=MRG[:, kt, tbs(tb)],
                                           start=(kt == 0), stop=(kt == NKT - 1))
                        return ins

                    S.op("pe", fm, reads=[("w", sm)] + [("mrg", kt, tb) for kt in range(NKT)], writes=[("ps", pi)])
                    S.op("dve", lambda e, pi=pi, f2=f2, tb=tb: e.tensor_tensor(out=xT[:, f2, tbs(tb)], in0=xT[:, f2, tbs(tb)],
                                                                              in1=PS[:, pi, :], op=ALU.add),
                         reads=[("ps", pi), ("x", f2, tb)], writes=[("x", f2, tb)])
        S.barrier()
        A.pop()

        S.enabled = DBG >= 6
        rmsnorm_to_h(l, P_GFFN, "n2")
        A.push()
        SGT = [A.alloc("sgt", [128, TBW], F32) for _ in range(3)]
        wfi = w_fi_d[l].rearrange("(kt p) n -> p kt n", p=128)
        scnt = 0
        for grp in range(2):
            j0 = grp * 11
            jl = 0
            while jl < 11:
                nj = min(2, 11 - jl)
                j = j0 + jl
                sf = wload([(0, NKT, nj * 128, wfi[:, :, j * 128:(j + nj) * 128]),
                            (NKT * nj * 128, NKT, nj * 128, wfi[:, :, FFH + j * 128:FFH + (j + nj) * 128])], None)
                wgt = wview(sf, NKT, nj * 128, 0)
                wup = wview(sf, NKT, nj * 128, NKT * nj * 128)
                for jj in range(nj):
                    for tb in range(NTB):
                        pg = nb()
                        pu = nb()

                        def fgu(e, pg=pg, pu=pu, jj=jj, tb=tb, wgt=wgt, wup=wup):
                            ins = None
                            for kt in range(NKT):
                                e.matmul(PS[:, pg, :], lhsT=wgt[:, kt, jj * 128:(jj + 1) * 128], rhs=hT[:, kt, tbs(tb)],
                                         start=(kt == 0), stop=(kt == NKT - 1))
                            for kt in range(NKT):
                                ins = e.matmul(PS[:, pu, :], lhsT=wup[:, kt, jj * 128:(jj + 1) * 128], rhs=hT[:, kt, tbs(tb)],
                                               start=(kt == 0), stop=(kt == NKT - 1))
                            return ins

                        S.op("pe", fgu, reads=[("w", sf)] + [("h", kt, tb) for kt in range(NKT)],
                             writes=[("ps", pg), ("ps", pu)])
                        q = scnt % 3
                        scnt += 1
                        S.op("act", lambda e, q=q, pg=pg: e.activation(out=SGT[q][:], in_=PS[:, pg, :], func=AF.Silu),
                             reads=[("ps", pg)], writes=[("sgt", q)])
                        S.op("dve", lambda e, q=q, pu=pu, a=jl + jj, tb=tb: e.tensor_tensor(out=FFA[:, a, tbs(tb)], in0=PS[:, pu, :],
                                                                                           in1=SGT[q][:], op=ALU.mult),
                             reads=[("ps", pu), ("sgt", q)], writes=[("ffa", jl + jj, tb)])
                jl += nj
            for fp in range(4):
                so_ = wload([(0, 11, 256, w_fo_d[l][j0 * 128:(j0 + 11) * 128, fp * 256:(fp + 1) * 256].rearrange(
                    "(j p) n -> p j n", p=128))], None)
                wo_ = wview(so_, 11, 256)
                for fl in range(2):
                    f = fp * 2 + fl
                    for tb in range(NTB):
                        pi = nb()

                        def ffo(e, pi=pi, fl=fl, tb=tb, wo_=wo_):
                            ins = None
                            for a in range(11):
                                ins = e.matmul(PS[:, pi, :], lhsT=wo_[:, a, fl * 128:(fl + 1) * 128], rhs=FFA[:, a, tbs(tb)],
                                               start=(a == 0), stop=(a == 10))
                            return ins

                        S.op("pe", ffo, reads=[("w", so_)] + [("ffa", a, tb) for a in range(11)], writes=[("ps", pi)])
                        S.op("dve", lambda e, pi=pi, f=f, tb=tb: e.tensor_tensor(out=xT[:, f, tbs(tb)], in0=xT[:, f, tbs(tb)],
                                                                                in1=PS[:, pi, :], op=ALU.add),
                             reads=[("ps", pi), ("x", f, tb)], writes=[("x", f, tb)])
        S.barrier()
        A.pop()

    for l_ in range(NL):
        layer(l_)

    S.enabled = True
    A.push()
    SQ = [A.alloc("sq", [128, TBW], BF16) for _ in range(3)]
    MS = [A.alloc("ms", [128, TBW], F32) for _ in range(2)]
    OST = [A.alloc("ost", [128, TBW], F32) for _ in range(4)]
    ocnt = 0
    outs = []
    for tb in range(NTB):
        pi = nb()
        for kt in range(NKT):
            q = (tb * NKT + kt) % 3
            S.op("act", lambda e, q=q, kt=kt, tb=tb: e.activation(out=SQ[q][:], in_=xT[:, kt, tbs(tb)], func=AF.Square),
                 reads=[("x", kt, tb)], writes=[("sq", q)])
            S.op("pe", lambda e, q=q, kt=kt, pi=pi: e.matmul(PS[:, pi, :], lhsT=ONES[:], rhs=SQ[q][:], start=(kt == 0),
                                                            stop=(kt == NKT - 1)),
                 reads=[("sq", q), "ones"], writes=[("ps", pi)])
        m = tb % 2
        S.op("dve", lambda e, m=m, pi=pi: e.tensor_scalar(out=MS[m][:], in0=PS[:, pi, :], scalar1=1.0 / D, scalar2=EPS,
                                                         op0=ALU.mult, op1=ALU.add),
             reads=[("ps", pi)], writes=[("ms", m)])
        S.op("act", lambda e, m=m: e.activation(out=MS[m][:], in_=MS[m][:], func=AF.Sqrt), reads=[("ms", m)], writes=[("ms", m)])
        S.op("dve", lambda e, m=m: e.reciprocal(out=MS[m][:], in_=MS[m][:]), reads=[("ms", m)], writes=[("ms", m)])
        for kt in range(NKT):
            oq = ocnt % 4
            ocnt += 1
            S.op("dve", lambda e, m=m, kt=kt, tb=tb, oq=oq: e.scalar_tensor_tensor(
                out=OST[oq][:], in0=xT[:, kt, tbs(tb)], scalar=PRM[:, 0, P_GFIN + kt:P_GFIN + kt + 1], in1=MS[m][:],
                op0=ALU.mult, op1=ALU.mult),
                 reads=[("x", kt, tb), ("ms", m), "prm"], writes=[("ost", oq)])
            o = S.op("sp", lambda e, kt=kt, tb=tb, oq=oq: e.dma_start(out=outT_d[kt * 128:(kt + 1) * 128, tbs(tb)], in_=OST[oq][:]),
                     reads=[("ost", oq)], writes=[("out", kt, tb)], dma=True, chan=f"o{oq}")
            outs.append(o)
    S.op("sp", lambda e: e.nop(), extra=outs)
    A.pop()
    S.emit(nc)
    return nc


def _host_consts():
    cst = np.zeros((128, CSTW), np.float32)
    cst[:, 1152 + 2 * SEQ:1152 + 2 * SEQ + SCH] = np.arange(SCH, dtype=np.float32)[None, :]
    cst[:, CSTW - 128:] = np.eye(128, dtype=np.float32)
    j = np.arange(128)[:, None]
    i = np.arange(128)[None, :]
    cur = np.where(j <= i, 0.0, -240000.0).astype(np.float32)
    prev = np.where(j > i, 0.0, -240000.0).astype(np.float32)
    cst[:, 0:512] = np.tile(cur, (1, 4))
    cst[:, 512:1024] = np.tile(prev, (1, 4))
    perm = np.zeros((128, 128), np.float32)
    for h in range(2):
        for ii in range(8):
            perm[h * 64 + ii + 8, h * 64 + ii] = -1.0
            perm[h * 64 + ii, h * 64 + ii + 8] = 1.0
    cst[:, 1024:1152] = perm
    pos = np.arange(SEQ, dtype=np.float32)
    inv_freq = (np.float32(500000.0) ** (-np.arange(0, 16, 2, dtype=np.float32) / np.float32(16))).astype(np.float32)
    ang = pos[:, None] * inv_freq[None, :]
    c = np.cos(ang).astype(np.float32).T
    s = np.sin(ang).astype(np.float32).T
    cos_t = np.ones((128, SEQ), np.float32)
    sin_t = np.zeros((128, SEQ), np.float32)
    for h in range(2):
        cos_t[h * 64:h * 64 + 8] = c
        cos_t[h * 64 + 8:h * 64 + 16] = c
        sin_t[h * 64:h * 64 + 8] = s
        sin_t[h * 64 + 8:h * 64 + 16] = s
    cst[:, 1152:1152 + SEQ] = cos_t
    cst[:, 1152 + SEQ:1152 + 2 * SEQ] = sin_t
    return cst


def _prep_inputs(inp, NL):
    f = lambda a: np.ascontiguousarray(np.asarray(a, dtype=np.float32))
    w_in = f(inp["w_in"])[:NL].copy()
    w_in[:, :, 0:512] = w_in[:, :, 0:512].reshape(NL, D, 2, 4, 64).transpose(0, 1, 3, 2, 4).reshape(NL, D, 512)
    w_ao = f(inp["w_attn_o"])[:NL].reshape(NL, 2, 4, 64, D).transpose(0, 2, 1, 3, 4).reshape(NL, 512, D)
    prm = np.zeros((NL, 128, NPRM), np.float32)
    t8 = lambda v: v.reshape(-1, 128).T
    a_re = f(inp["ssm_a_re"])
    a_im = f(inp["ssm_a_im"])
    ldt = f(inp["ssm_log_dt"])
    b_re = f(inp["ssm_b_re"])
    b_im = f(inp["ssm_b_im"])
    c_re = f(inp["ssm_c_re"])
    c_im = f(inp["ssm_c_im"])
    ssmB = np.zeros((NL, 128, 2, 2048), np.float32)
    ssmC = np.zeros((NL, 128, 2, 16, 64), np.float32)
    sinks = f(inp["attn_sinks"])
    for l in range(NL):
        prm[l, :, P_GMIX:P_GMIX + 8] = t8(f(inp["norm_mix"])[l])
        prm[l, :, P_GFFN:P_GFFN + 8] = t8(f(inp["norm_ffn"])[l])
        prm[l, :, P_BG:P_BG + 24] = t8(f(inp["b_gate"])[l])
        cw = f(inp["conv_w"])[l]
        for j in range(3):
            prm[l, :, P_CONVW + j * 4:P_CONVW + j * 4 + 4] = t8(cw[j])
        prm[l, :, P_SSMD:P_SSMD + 4] = t8(f(inp["ssm_d"])[l])
        for kvh in range(2):
            prm[l, kvh * 64:(kvh + 1) * 64, P_SINK:P_SINK + 4] = sinks[l, kvh * 4:(kvh + 1) * 4][None, :]
        arA = a_re[l].reshape(16, 2, 64).transpose(1, 2, 0).reshape(128, 16)
        aiA = a_im[l].reshape(16, 2, 64).transpose(1, 2, 0).reshape(128, 16)
        ldA = np.broadcast_to(ldt[l].reshape(16, 2, 1), (16, 2, 64)).transpose(1, 2, 0).reshape(128, 16)
        prm[l, :, P_AA:P_AA + 16] = arA
        prm[l, :, P_AI:P_AI + 16] = aiA
        prm[l, :, P_LDT:P_LDT + 16] = ldA
        prm[l, :, P_GFIN:P_GFIN + 8] = t8(f(inp["norm_final"]))
        for ct in range(4):
            for gl in range(8):
                g = ct * 8 + gl
                c0 = g * 64
                ssmB[l, gl * 16:(gl + 1) * 16, 0, c0:c0 + 64] = b_re[l, g].T
                ssmB[l, gl * 16:(gl + 1) * 16, 1, c0:c0 + 64] = b_im[l, g].T
        for pr in range(16):
            plh = (pr % 4) % 2
            for gsel in range(2):
                g = 2 * pr + gsel
                c0 = plh * 32 + gsel * 16
                ssmC[l, gsel * 64:(gsel + 1) * 64, 0, pr, c0:c0 + 16] = c_re[l, g].T
                ssmC[l, gsel * 64:(gsel + 1) * 64, 1, pr, c0:c0 + 16] = c_im[l, g].T
    shared = {
        "w_in": w_in, "w_attn_o": np.ascontiguousarray(w_ao), "w_conv_o": f(inp["w_conv_o"])[:NL],
        "w_ssm_glu": f(inp["w_ssm_glu"])[:NL], "w_ssm_o": f(inp["w_ssm_o"])[:NL], "w_mix_o": f(inp["w_mix_o"])[:NL],
        "w_ffn_in": f(inp["w_ffn_in"])[:NL], "w_ffn_out": f(inp["w_ffn_out"])[:NL],
        "prm": prm, "ssmB": ssmB, "ssmC": ssmC.reshape(NL, 128, 2, 1024), "cst": _host_consts(),
    }
    return shared


_NC_CACHE = {}


def kernel(NL=NLAYER, DBG=99, **inp):
    x = np.asarray(inp["x"], dtype=np.float32)
    B = x.shape[0]
    shared = _prep_inputs(inp, NL)
    if (NL, DBG) not in _NC_CACHE:
        _NC_CACHE[(NL, DBG)] = build_program(NL, DBG)
    nc = _NC_CACHE[(NL, DBG)]
    in_maps = []
    for b in range(B):
        m = dict(shared)
        m["xT"] = np.ascontiguousarray(x[b].T)
        in_maps.append(m)
    res = run_bass_kernel_spmd(nc, in_maps, core_ids=list(range(B)))
    out = np.stack([np.ascontiguousarray(r["outT"].T) for r in res.results], axis=0)
    return out.astype(np.float32)
```
